# Optimizing a Trainium2 kernel written in Bass

```python
import jax, jax.numpy as jnp
from jax import lax
import numpy as np

D_MODEL = 2048
BATCH = 4
SEQ = 2048
DEPTH = 2
DEC_BATCH = 128
DEC_SEQ = 1
PAST_LEN = 16384
PAGE_SIZE = 128

MIX_DIM = D_MODEL
CONV_DIM = MIX_DIM // 2
CONV_GROUPS = 16
CONV_WIDTH = 3
GLA_HEADS = 4
GLA_V_DIM = MIX_DIM - CONV_DIM
GLA_QK_DIM = GLA_V_DIM // 2
GLA_DK = GLA_QK_DIM // GLA_HEADS
GLA_DV = GLA_V_DIM // GLA_HEADS
GATE_RANK = 16
GATE_TAU = 16.0
GLA_CHUNK = 64
D_FF = 5632
EPS = 1e-6

_S1 = CONV_DIM
_S2 = _S1 + CONV_DIM
_S3 = _S2 + CONV_DIM
_S4 = _S3 + GLA_QK_DIM
_S5 = _S4 + GLA_QK_DIM
_S6 = _S5 + GLA_V_DIM
_S7 = _S6 + GLA_V_DIM
IN_COLS = _S7 + GATE_RANK

kernel_name = "hybrid_shortconv_gla_convffn_step"


def rmsnorm(x, g):
    xf = x.astype(jnp.float32)
    var = jnp.mean(xf * xf, axis=-1, keepdims=True)
    return (xf * lax.rsqrt(var + EPS) * g.astype(jnp.float32)).astype(x.dtype)


def causal_dwconv(u, prev, w):
    full = jnp.concatenate([prev.astype(u.dtype), u], axis=1)
    L = u.shape[1]
    y = full[:, 0:L] * w[0]
    for tap in range(1, CONV_WIDTH):
        y = y + full[:, tap:tap + L] * w[tap]
    return y, full[:, -(CONV_WIDTH - 1):]


def pick_chunk(L):
    return GLA_CHUNK if L % GLA_CHUNK == 0 else L


def gla_chunked(q, k, v, logf, s0, chunk):
    B, L, H, DK = q.shape
    DV = v.shape[-1]
    n = L // chunk

    def blocks(t):
        return jnp.moveaxis(t.reshape((B, n, chunk) + t.shape[2:]), 1, 0)

    mask = jnp.tril(jnp.ones((chunk, chunk), dtype=bool))[None, :, :, None, None]

    def step(S, inp):
        qc, kc, vc, gc = inp
        b = jnp.cumsum(gc, axis=1)
        inter = jnp.einsum('bihk,bhkv->bihv', qc * jnp.exp(b), S)
        diff = b[:, :, None] - b[:, None, :]
        decay = jnp.exp(jnp.where(mask, diff, -jnp.inf))
        A = jnp.einsum('bihk,bjhk,bijhk->bhij', qc, kc, decay)
        intra = jnp.einsum('bhij,bjhv->bihv', A, vc)
        bC = b[:, -1]
        kdec = kc * jnp.exp(bC[:, None] - b)
        S_new = jnp.exp(bC)[..., None] * S + jnp.einsum('bjhk,bjhv->bhkv', kdec, vc)
        return S_new, inter + intra

    s_fin, o = lax.scan(step, s0, (blocks(q), blocks(k), blocks(v), blocks(logf)))
    o = jnp.moveaxis(o, 0, 1).reshape(B, L, H, DV)
    return o, s_fin


def mixer(xn, conv_prev, gla_prev, w_in, conv_w, gate_w2, gate_b, gla_norm_g, w_out, chunk):
    Bn, L, _ = xn.shape
    proj = xn @ w_in
    bg, cg, hin, q, k, v, og, glr = jnp.split(proj, [_S1, _S2, _S3, _S4, _S5, _S6, _S7], axis=-1)
    cu, conv_new = causal_dwconv(cg * hin, conv_prev, conv_w)
    ya = bg * cu
    f32 = jnp.float32
    logf = jax.nn.log_sigmoid((glr @ gate_w2 + gate_b).astype(f32)) / GATE_TAU
    qh = q.astype(f32).reshape(Bn, L, GLA_HEADS, GLA_DK) * (GLA_DK ** -0.5)
    kh = k.astype(f32).reshape(Bn, L, GLA_HEADS, GLA_DK)
    vh = v.astype(f32).reshape(Bn, L, GLA_HEADS, GLA_DV)
    lf = logf.reshape(Bn, L, GLA_HEADS, GLA_DK)
    o, s_new = gla_chunked(qh, kh, vh, lf, gla_prev.astype(f32), chunk)
    var = jnp.mean(o * o, axis=-1, keepdims=True)
    o = o * lax.rsqrt(var + EPS) * gla_norm_g.astype(f32).reshape(GLA_HEADS, GLA_DV)
    yb = (o.reshape(Bn, L, GLA_V_DIM) * jax.nn.silu(og.astype(f32))).astype(xn.dtype)
    out = jnp.concatenate([ya, yb], axis=-1) @ w_out
    return out, conv_new, s_new


def conv_ffn(xn, prev, w_up, conv_w, conv_b, w_down):
    u, v = jnp.split(xn @ w_up, 2, axis=-1)
    cu, new = causal_dwconv(u, prev, conv_w)
    h = jax.nn.silu(cu + conv_b) * v
    return h @ w_down, new


def trunk(x, conv_prev, gla_prev, ffn_prev, norm_mix_g, w_in, conv_w, gate_w2, gate_b,
          gla_norm_g, w_out, norm_ffn_g, w_up, ffn_conv_w, ffn_conv_b, w_down, final_norm_g):
    chunk = pick_chunk(x.shape[1])
    convs, glas, ffns = [], [], []
    for l in range(DEPTH):
        m, c_new, s_new = mixer(rmsnorm(x, norm_mix_g[l]), conv_prev[l], gla_prev[l], w_in[l],
                                conv_w[l], gate_w2[l], gate_b[l], gla_norm_g[l], w_out[l], chunk)
        x = x + m
        f, f_new = conv_ffn(rmsnorm(x, norm_ffn_g[l]), ffn_prev[l], w_up[l], ffn_conv_w[l],
                            ffn_conv_b[l], w_down[l])
        x = x + f
        convs.append(c_new)
        glas.append(s_new)
        ffns.append(f_new)
    return rmsnorm(x, final_norm_g), jnp.stack(convs), jnp.stack(glas), jnp.stack(ffns)


def setup_inputs(seed: int = 0) -> dict:
    key = jax.random.key(seed)
    ks = jax.random.split(key, 20)
    nrm = jax.random.normal
    f32 = jnp.float32
    return {
        "x_prompt": nrm(ks[0], (BATCH, SEQ, D_MODEL), f32),
        "x_sample": nrm(ks[1], (DEC_BATCH, DEC_SEQ, D_MODEL), f32),
        "state_conv": nrm(ks[2], (DEPTH, DEC_BATCH, CONV_WIDTH - 1, CONV_DIM), f32),
        "state_gla": 0.5 * nrm(ks[3], (DEPTH, DEC_BATCH, GLA_HEADS, GLA_DK, GLA_DV), f32),
        "state_ffn_conv": nrm(ks[4], (DEPTH, DEC_BATCH, CONV_WIDTH - 1, D_FF), f32),
        "norm_mix_g": 1.0 + 0.02 * nrm(ks[5], (DEPTH, D_MODEL), f32),
        "w_in": nrm(ks[6], (DEPTH, D_MODEL, IN_COLS), f32) * D_MODEL ** -0.5,
        "conv_w": nrm(ks[7], (DEPTH, CONV_WIDTH, CONV_DIM), f32) * CONV_WIDTH ** -0.5,
        "gate_w2": nrm(ks[8], (DEPTH, GATE_RANK, GLA_QK_DIM), f32) * GATE_RANK ** -0.5,
        "gate_b": 0.1 * nrm(ks[9], (DEPTH, GLA_QK_DIM), f32),
        "gla_norm_g": 1.0 + 0.02 * nrm(ks[10], (DEPTH, GLA_V_DIM), f32),
        "w_out": nrm(ks[11], (DEPTH, MIX_DIM, D_MODEL), f32) * MIX_DIM ** -0.5,
        "norm_ffn_g": 1.0 + 0.02 * nrm(ks[12], (DEPTH, D_MODEL), f32),
        "w_up": nrm(ks[13], (DEPTH, D_MODEL, 2 * D_FF), f32) * D_MODEL ** -0.5,
        "ffn_conv_w": nrm(ks[14], (DEPTH, CONV_WIDTH, D_FF), f32) * CONV_WIDTH ** -0.5,
        "ffn_conv_b": 0.02 * nrm(ks[15], (DEPTH, D_FF), f32),
        "w_down": nrm(ks[16], (DEPTH, D_FF, D_MODEL), f32) * D_FF ** -0.5,
        "final_norm_g": 1.0 + 0.02 * nrm(ks[17], (D_MODEL,), f32),
    }


def reference(x_prompt, x_sample, state_conv, state_gla, state_ffn_conv, norm_mix_g, w_in,
              conv_w, gate_w2, gate_b, gla_norm_g, w_out, norm_ffn_g, w_up, ffn_conv_w,
              ffn_conv_b, w_down, final_norm_g):
    dt = x_prompt.dtype
    zero_conv = jnp.zeros((DEPTH, BATCH, CONV_WIDTH - 1, CONV_DIM), dt)
    zero_gla = jnp.zeros((DEPTH, BATCH, GLA_HEADS, GLA_DK, GLA_DV), jnp.float32)
    zero_ffn = jnp.zeros((DEPTH, BATCH, CONV_WIDTH - 1, D_FF), dt)
    y_prompt, conv_p, gla_p, ffn_p = trunk(
        x_prompt, zero_conv, zero_gla, zero_ffn, norm_mix_g, w_in, conv_w, gate_w2, gate_b,
        gla_norm_g, w_out, norm_ffn_g, w_up, ffn_conv_w, ffn_conv_b, w_down, final_norm_g)
    y_sample, conv_s, gla_s, ffn_s = trunk(
        x_sample, state_conv, state_gla, state_ffn_conv, norm_mix_g, w_in, conv_w, gate_w2,
        gate_b, gla_norm_g, w_out, norm_ffn_g, w_up, ffn_conv_w, ffn_conv_b, w_down, final_norm_g)
    return (y_prompt, y_sample, conv_p, gla_p, ffn_p, conv_s, gla_s, ffn_s)
```

```python
import numpy as np
from contextlib import ExitStack
import concourse.bass as bass
import concourse.mybir as mybir
from concourse.bass_utils import run_bass_kernel_spmd

F32 = mybir.dt.float32
BF16 = mybir.dt.bfloat16
ALU = mybir.AluOpType
AF = mybir.ActivationFunctionType

ENGS = ("pe", "act", "dve", "pool", "sp")

D = 2048
KC = 16
L = 2
CD = 1024
CC = 8
H = 4
DK = 128
DV = 256
GR = 16
S1, S2, S3 = 1024, 2048, 3072
S4, S5, S6, S7 = 3584, 4096, 5120, 6144
INC = 6160
EPS = 1e-6
NS = 16
NH = 2
NX = NS + NH
NSLOT = 6


class Cfg:
    def __init__(self, T=1024, NPASS=1, DFF=5632, NG=4):
        self.T, self.NPASS, self.DFF, self.NG = T, NPASS, DFF, NG
        self.FC = DFF // 128
        self.FG = self.FC // NG
        self.SEQ = T * NPASS
        self.TT = T + NX
        self.NTB = T // 128
        o = 0
        self.off = {}
        for l in range(L):
            for name, n in (("nmg", 16), ("cw", 24), ("gng", 8), ("nfg", 16),
                            ("fcw", 3 * self.FC), ("fcb", self.FC)):
                self.off[(name, l)] = o
                o += n
        self.off[("fng", 0)] = o
        o += 16
        self.VR = o
        self.VB = (o + 127) // 128


class Sched:
    def __init__(self, sems, dma_sems):
        self.sem = dict(zip(ENGS, sems))
        self.cnt = {e: 0 for e in ENGS}
        self.ops = {e: [] for e in ENGS}
        self.waited = {e: {} for e in ENGS}
        self.res = {}
        self.dma_sems = {k: list(v) for k, v in dma_sems.items()}
        self.dma_val = {k: [0] * len(v) for k, v in self.dma_sems.items()}
        self.dma_rr = {k: 0 for k in self.dma_sems}
        self.dry = False

    def _need(self, eng, h, waits):
        if h is None:
            return
        sem, val, heng = h
        if heng == "pe" and eng == "pe":
            return
        k = id(sem)
        if self.waited[eng].get(k, 0) >= val:
            return
        prev = waits.get(k)
        if prev is None or prev[1] < val:
            waits[k] = (sem, val)

    def op(self, eng, fn, reads=(), writes=(), signal=True, dma=False, cc=None):
        if self.dry:
            return None
        waits = {}
        for r in reads:
            ent = self.res.get(r)
            if ent is not None:
                self._need(eng, ent[0], waits)
        for w in writes:
            ent = self.res.get(w)
            if ent is not None:
                self._need(eng, ent[0], waits)
                for rh in ent[1]:
                    self._need(eng, rh, waits)
        if dma:
            i = self.dma_rr[eng]
            self.dma_rr[eng] = (i + 1) % len(self.dma_sems[eng])
            dsem = self.dma_sems[eng][i]
            if self.dma_val[eng][i] > 0:
                self._need(eng, (dsem, self.dma_val[eng][i], "dma"), waits)
            self.dma_val[eng][i] += 16
            handle = (dsem, self.dma_val[eng][i], "dma")
            inc = (dsem, 16)
        elif cc is not None:
            handle = (cc, 1, "cc")
            inc = (cc, None)
        elif signal:
            self.cnt[eng] += 1
            handle = (self.sem[eng], self.cnt[eng], eng)
            inc = (self.sem[eng], 1)
        else:
            handle = (self.sem[eng], self.cnt[eng] + 1, eng)
            inc = None
        wl = list(waits.values())
        for sem, val in wl:
            self.waited[eng][id(sem)] = val
        self.ops[eng].append((wl, fn, inc))
        for r in reads:
            self.res.setdefault(r, [None, []])[1].append(handle)
        for w in writes:
            self.res[w] = [handle, []]
        return handle

    def barrier(self, engs=("pe", "act", "dve", "sp")):
        if self.dry:
            return
        for e in engs:
            wl = []
            for o in ENGS:
                if o == e or self.cnt[o] == 0:
                    continue
                if o == "pool":
                    continue
                if self.waited[e].get(id(self.sem[o]), 0) < self.cnt[o]:
                    wl.append((self.sem[o], self.cnt[o]))
                    self.waited[e][id(self.sem[o])] = self.cnt[o]
            for q in ("sp", "act"):
                for i, dsem in enumerate(self.dma_sems[q]):
                    v = self.dma_val[q][i]
                    if v > 0 and self.waited[e].get(id(dsem), 0) < v:
                        wl.append((dsem, v))
                        self.waited[e][id(dsem)] = v
            if wl:
                self.ops[e].append((wl, None, None))

    def replay(self, eng, e):
        for wl, fn, inc in self.ops[eng]:
            for sem, val in wl:
                e.wait_ge(sem, val)
            if fn is None:
                continue
            ins = fn(e)
            if inc is not None:
                if inc[1] is None:
                    ins.then_inc(inc[0])
                else:
                    ins.then_inc(inc[0], inc[1])

    def final_wait(self, e):
        for q in self.dma_sems:
            for i, dsem in enumerate(self.dma_sems[q]):
                if self.dma_val[q][i] > 0:
                    e.wait_ge(dsem, self.dma_val[q][i])
        for en in ENGS:
            if self.cnt[en] > 0:
                e.wait_ge(self.sem[en], self.cnt[en])


def build_nc(cfg):
    T, TT, NPASS, DFF, FC, FG, NG, NTB = cfg.T, cfg.TT, cfg.NPASS, cfg.DFF, cfg.FC, cfg.FG, cfg.NG, cfg.NTB
    SEQ = cfg.SEQ
    nc = bass.Bass("TRN2", target_bir_lowering=False)

    def din(name, shape):
        return nc.dram_tensor(name, list(shape), F32, kind="ExternalInput").ap()

    def dout(name, shape):
        return nc.dram_tensor(name, list(shape), F32, kind="ExternalOutput").ap()

    x_p = din("x_p", [SEQ, D]); x_s = din("x_s", [NS, D]); x_h = din("x_h", [NH, D])
    flag_d = din("flag", [128, 2])
    st_conv = din("st_conv", [L, NS, 2, CD]); st_gla = din("st_gla", [L, NS, H, DK, DV])
    st_ffn = din("st_ffn", [L, NS, 2, DFF])
    w_in = din("w_in", [L, D, INC]); w_out = din("w_out", [L, D, D])
    w_up = din("w_up", [L, D, 2 * DFF]); w_down = din("w_down", [L, DFF, D])
    gw_d = din("gw", [L, GR + 1, H * DK])
    vecs_d = din("vecs", [cfg.VB * 128, 128])
    consts_d = din("consts", [128, 770])
    y_p = dout("y_p", [SEQ, D]); y_s = dout("y_s", [NS, D])
    conv_p = dout("conv_p", [NPASS, L, 2, CD]); gla_p = dout("gla_p", [NPASS, L, H, DK, DV])
    ffn_p = dout("ffn_p", [NPASS, L, 2, DFF])
    conv_s = dout("conv_s", [L, NS, 2, CD]); gla_s = dout("gla_s", [L, NS, H, DK, DV])
    ffn_s = dout("ffn_s", [L, NS, 2, DFF])

    es = ExitStack()
    with es:
        def sb(name, shape, dt=F32):
            return es.enter_context(nc.sbuf_tensor(name, list(shape), dt))

        xT = sb("xT", [128, KC, TT]); xnT = sb("xnT", [128, KC, TT], BF16)
        yT = sb("yT", [128, KC, TT], BF16)
        wsl = [sb("wsl%d" % i, [128, KC, 128], BF16) for i in range(NSLOT)]
        EW = TT + 2
        tabc_n = max(3 * EW, 2048 + EW)
        tabc = sb("tabc", [128, tabc_n])
        tA = tabc[:, 0:EW]; tB = tabc[:, EW:2 * EW]; tC = tabc[:, tabc_n - EW:tabc_n]
        io = tabc[:, 0:2048]
        nxb = (KC * TT // 2) // 2048
        if nxb >= 1:
            yflat = yT[:].rearrange("p k t -> p (k t)").bitcast(F32)
            xin = [(yflat[:, i * 2048:(i + 1) * 2048], [("xin", i)]) for i in range(nxb)]
        else:
            xin = [(io, ["tA", "tB"])]
        sqv = tB.bitcast(BF16)
        SCR = 7016
        scr = sb("scr", [128, SCR])
        cst_t = sb("consts_sb", [128, 770])
        ident = cst_t[:, 0:128]; Uc = cst_t[:, 128:256]; Lm = cst_t[:, 256:384]
        maskT = cst_t[:, 384:512]; cmat = cst_t[:, 512:514]
        idb16 = cst_t[:, 514:770].rearrange("p (r t) -> p r t", r=16)
        identb = sb("identb", [128, 128], BF16); onesb = sb("onesb", [128, 128], BF16)
        vec = sb("vec_sb", [128, cfg.VB * 128])
        gw = sb("gw_sb", [32, L, H * DK], BF16)
        glrT = sb("glrT", [32, TT], BF16)
        Sst = sb("Sst", [128, H, DV])
        ps = [es.enter_context(nc.psum_tensor("ps%d" % i, [128, 512], F32)) for i in range(8)]
        psb = [p[:].bitcast(BF16) for p in ps]
        sems = [es.enter_context(nc.semaphore("s_%s" % e)) for e in ENGS]
        dsems = {"sp": [es.enter_context(nc.semaphore("dsp_%d" % i)) for i in range(16)],
                 "pool": [es.enter_context(nc.semaphore("dpl_%d" % i)) for i in range(8)],
                 "act": [es.enter_context(nc.semaphore("dac_%d" % i)) for i in range(8)]}
        S = Sched(sems, dsems)
        ibS = [[nc.dram_tensor("ibS_%d_%d" % (l, h), [128, DV], F32, kind="Internal") for h in range(H)] for l in range(L)]
        obS = [[nc.dram_tensor("obS_%d_%d" % (l, h), [256, DV], F32, kind="Internal") for h in range(H)] for l in range(L)]
        ibX = [nc.dram_tensor("ibX_%d" % i, [128, 2 * KC], F32, kind="Internal") for i in range(2 * L)]
        obX = [nc.dram_tensor("obX_%d" % i, [256, 2 * KC], F32, kind="Internal") for i in range(2 * L)]
        ccsems = [es.enter_context(nc.semaphore("cc_%d" % i)) for i in range(L * H + 2 * L)]
        PAIRS = [[0, 1], [2, 3], [4, 5], [6, 7]]
        flag = sb("flag_sb", [128, 2]); xhs = sb("xhs", [128, KC, 2]); xhr = sb("xhr", [128, KC, 2])

        def sv(off, n, dt=F32, parts=128):
            v = scr[0:parts, off:off + n]
            return v.bitcast(dt) if dt != F32 else v
        g_e = sv(0, 128); g_lf = sv(128, 128); g_eb = sv(256, 128); g_enb = sv(384, 128)
        g_erev = sv(512, 128); g_dec = sv(640, 2); g_ss = sv(642, 2); g_rs = sv(644, 2)
        g_qd = sv(648, 64, BF16); g_kd = sv(712, 64, BF16); g_kdec = sv(776, 64, BF16)
        g_vbf = sv(840, 128, BF16); g_Sbf = sv(968, 128, BF16); g_qkT = sv(1096, 128, BF16)
        g_ATm = sv(1224, 64, BF16); g_sg = sv(1288, 256); g_yb = sv(1544, 128, BF16)
        g_Qm = sv(1672, 256).rearrange("p (r t) -> p r t", r=16)
        g_akq = sv(1928, 48); g_qs = sv(1976, 128); g_ks = sv(2104, 128); g_as = sv(2232, 128)
        g_vs = sv(2360, 256)
        g_Sb = [sv(2616 + i * 512, 512).rearrange("p (r v) -> p r v", r=2) for i in range(2)]
        g_Km = tabc[0:16, 0:2048].rearrange("p (r k) -> p r k", r=16)
        g_qeT = sv(3640, 64 * 8, BF16).rearrange("p (b t) -> p b t", t=128)
        g_sgs = sv(4152, 128 * 8, BF16).rearrange("p (b v) -> p b v", v=DV)
        g_SA = sv(5176, 256); g_SAb = sv(5432, 128, BF16); g_junk = sv(5560, 128, BF16)
        g_Bsn = sv(5688, 1); g_Etot = sv(5692, 1)
        g_ecol = [sv(5690, 1), sv(5694, 1)]
        g_dec = [g_dec, sv(5696, 2)]
        g_qd = [g_qd, sv(5700, 64, BF16)]; g_kd = [g_kd, sv(5764, 64, BF16)]; g_kdec = [g_kdec, sv(5828, 64, BF16)]
        g_vbf = [g_vbf, sv(5892, 128, BF16)]
        g_yb = [g_yb, sv(6020, 128, BF16)]
        g_ss = [g_ss, sv(6148, 2)]; g_rs = [g_rs, sv(6152, 2)]
        g_sig = sv(6156, 256)
        g_ss8 = sv(6720, 8); g_rs8 = sv(6728, 8)
        g_c = sv(6736, 2); g_cv = sv(6752, 256)
        g_Sbb = [sv(6412 + i * 128, 128, BF16) for i in range(2)]
        g_vsb = sv(2360, 128, BF16)
        g_Qmb = sv(1672, 128, BF16).rearrange("p (r t) -> p r t", r=16)
        g_Kmb = tabc[0:16, 0:1024].bitcast(BF16).rearrange("p (r k) -> p r k", r=16)
        g_Sb1 = [sv(2616 + i * 256, 256) for i in range(4)] + [tabc[:, 1024 + i * 256:1280 + i * 256] for i in range(min(8, (tabc_n - 1024) // 256))]
        NSB = len(g_Sb1)
        stg_in = sv(0, 1408); stg_out = sv(1408, 1408)
        fst = sv(2816, 11 * 32).rearrange("p (j c) -> p j c", c=32)
        nst = sv(3168, 11 * 34).rearrange("p (j c) -> p j c", c=34)

        class WS:
            def __init__(self):
                self.req = []
                self.i = 0
                self.loaded = 0
            def get(self, parts):
                if S.dry:
                    self.req.append(parts)
                    self.i += 1
                    return self.i - 1, wsl[(self.i - 1) % NSLOT]
                u = self.i
                self.i += 1
                return u, wsl[u % NSLOT]
            def load(self, u):
                if u >= len(self.req):
                    return
                slot = wsl[u % NSLOT]
                for (kn, c0, ncol, src) in self.req[u]:
                    S.op("pool", lambda e, o=slot[:, 0:kn, c0:c0 + ncol], s=src: e.dma_start(out=o, in_=s),
                         writes=[("w", u % NSLOT)], dma=True)
                self.loaded = u + 1
            def release(self, u):
                if S.dry:
                    return
                if u + NSLOT == self.loaded:
                    self.load(u + NSLOT)
            def start(self):
                for u in range(min(NSLOT, len(self.req))):
                    self.load(u)
        ws = WS()

        def wunit(src2d, c0, ncols=128, kn=KC):
            return (kn, 0, ncols, src2d[:, c0:c0 + ncols].rearrange("(k p) n -> p k n", p=128))

        def colgroups(ns):
            g = []
            c = 0
            while c < T:
                n = min(512, T - c)
                g.append((c, n))
                c += n
            if ns:
                g.append((T, NX))
            return g

        BX, BY = (0, 1, 2), (3, 4, 5)

        def proj_fm(slot, u, nk, rhs_t, rhs_res, groups, banks, M=128):
            for k in range(nk):
                for gi, (c0, n) in enumerate(groups):
                    S.op("pe", lambda e, o=ps[banks[gi]][0:M, 0:n], l=slot[:, k, 0:M], r=rhs_t[:, k, c0:c0 + n],
                         a=(k == 0), b=(k == nk - 1): e.matmul(o, lhsT=l, rhs=r, start=a, stop=b),
                         reads=[("w", u % NSLOT)] + rhs_res(k), writes=[("ps", banks[gi])], signal=(k == nk - 1))

        def vcol(name, l, i):
            o = cfg.off[(name, l)] + i
            return vec[:, o:o + 1]

        def load_x(p, ns):
            for tb in range(NTB):
                xb, xres = xin[tb % len(xin)]
                S.op("sp", lambda e, r0=p * T + tb * 128, xb=xb: e.dma_start(out=xb, in_=x_p[r0:r0 + 128, :]),
                     writes=xres, dma=True)
                for q in range(4):
                    bk = 4 + ((4 * tb + q) % 4)
                    for k in range(4):
                        dc = 4 * q + k
                        S.op("pe", lambda e, o=ps[bk][:, k * 128:(k + 1) * 128], i=xb[:, dc * 128:(dc + 1) * 128]:
                             e.transpose(out=o, in_=i, identity=ident), reads=xres + ["consts"],
                             writes=[("ps", bk)], signal=(k == 3))
                    eng = "act" if q % 2 == 0 else "dve"
                    o = xT[:, 4 * q:4 * q + 4, tb * 128:(tb + 1) * 128]
                    i = ps[bk][:, :].rearrange("p (k t) -> p k t", k=4)
                    if eng == "act":
                        S.op("act", lambda e, o=o, i=i: e.copy(out=o, in_=i), reads=[("ps", bk)],
                             writes=[("xT", 4 * q + k) for k in range(4)])
                    else:
                        S.op("dve", lambda e, o=o, i=i: e.tensor_copy(out=o, in_=i), reads=[("ps", bk)],
                             writes=[("xT", 4 * q + k) for k in range(4)])
            if ns:
                S.op("sp", lambda e: e.dma_start(out=io[0:NS, :], in_=x_s), writes=["tA", "tB"], dma=True)
                S.op("sp", lambda e: e.dma_start(out=io[NS:NX, :], in_=x_h), writes=["tA", "tB"], dma=True)
                for dc in range(KC):
                    S.op("pe", lambda e, o=ps[6][:, dc * NX:(dc + 1) * NX], i=io[0:NX, dc * 128:(dc + 1) * 128]:
                         e.transpose(out=o, in_=i, identity=ident[0:NX, 0:NX]), reads=["tA", "tB", "consts"],
                         writes=[("ps", 6)], signal=(dc == KC - 1))
                S.op("dve", lambda e: e.tensor_copy(out=xT[:, :, T:T + NX],
                                                     in_=ps[6][:, 0:KC * NX].rearrange("p (k t) -> p k t", k=KC)),
                     reads=[("ps", 6)], writes=[("xT", k) for k in range(KC)])

        def rms_stats(groups, TTp):
            for dc in range(KC):
                sq = sqv[:, (dc % 2) * EW:(dc % 2) * EW + TTp]
                S.op("act", lambda e, o=sq, i=xT[:, dc, 0:TTp]: e.activation(out=o, in_=i, func=AF.Square),
                     reads=[("xT", dc)], writes=[("sq", dc % 2)])
                for gi, (c0, n) in enumerate(groups):
                    S.op("pe", lambda e, o=ps[BX[gi]][:, 0:n], r=sq[:, c0:c0 + n], a=(dc == 0), b=(dc == KC - 1):
                         e.matmul(o, lhsT=onesb[:], rhs=r, start=a, stop=b),
                         reads=[("sq", dc % 2), "onesb"], writes=[("ps", BX[gi])],
                         signal=(gi == len(groups) - 1))
            for gi, (c0, n) in enumerate(groups):
                S.op("act", lambda e, o=tC[:, c0:c0 + n], i=ps[BX[gi]][:, 0:n]:
                     e.activation(out=o, in_=i, func=AF.Ln, scale=1.0 / D, bias=EPS),
                     reads=[("ps", BX[gi])], writes=["tC"])
            S.op("act", lambda e: e.activation(out=tC[:, 0:TTp], in_=tC[:, 0:TTp], func=AF.Exp, scale=-0.5),
                 reads=["tC"], writes=["tC"])

        def rms_norm(gname, l, groups, TTp):
            rms_stats(groups, TTp)
            for dc in range(KC):
                S.op("dve", lambda e, o=xnT[:, dc, 0:TTp], i=xT[:, dc, 0:TTp], g=vcol(gname, l, dc):
                     e.scalar_tensor_tensor(out=o, in0=i, scalar=g, in1=tC[:, 0:TTp], op0=ALU.mult, op1=ALU.mult),
                     reads=[("xT", dc), "tC", "vec"], writes=["xnT"])

        def xn_res(k):
            return ["xnT"]

        def conv_state_in(l):
            for r in range(2):
                S.op("sp", lambda e, r=r: e.dma_start(out=stg_in[r * NS:(r + 1) * NS, 0:CD], in_=st_conv[l, :, r, :]),
                     writes=["stg_in"], dma=True)
            for c in range(CC):
                S.op("pe", lambda e, c=c: e.transpose(out=ps[7][:, c * 32:(c + 1) * 32],
                                                      in_=stg_in[0:32, c * 128:(c + 1) * 128], identity=ident[0:32, 0:32]),
                     reads=["stg_in", "consts"], writes=[("ps", 7)], signal=(c == CC - 1))
            S.op("dve", lambda e: e.tensor_copy(out=fst[:, 0:CC, :], in_=ps[7][:, 0:CC * 32].rearrange("p (j c) -> p j c", c=32)),
                 reads=[("ps", 7)], writes=["fst"])

        def state_out(nchunks, ns, dst_p, dst_s0, dst_s1):
            W = 2 + 2 * ns
            nb = (nchunks + 3) // 4
            for j in range(nchunks):
                bk = [5, 6, 7][j // 4]
                S.op("pe", lambda e, j=j, bk=bk: e.transpose(out=ps[bk][0:W, (j % 4) * 128:(j % 4 + 1) * 128],
                                                               in_=nst[:, j, 0:W], identity=ident),
                     reads=["nst", "consts"], writes=[("ps", bk)], signal=(j % 4 == 3 or j == nchunks - 1))
            for b in range(nb):
                bk = [5, 6, 7][b]
                n = min(4, nchunks - 4 * b) * 128
                S.op("act", lambda e, b=b, bk=bk, n=n: e.copy(out=stg_out[0:W, b * 512:b * 512 + n], in_=ps[bk][0:W, 0:n]),
                     reads=[("ps", bk)], writes=["stg_out"])
            S.op("sp", lambda e: e.dma_start(out=dst_p, in_=stg_out[0:2, 0:nchunks * 128]), reads=["stg_out"], dma=True)
            if ns:
                S.op("sp", lambda e: e.dma_start(out=dst_s0, in_=stg_out[2:2 + ns, 0:nchunks * 128]), reads=["stg_out"], dma=True)
                S.op("sp", lambda e: e.dma_start(out=dst_s1, in_=stg_out[2 + ns:2 + 2 * ns, 0:nchunks * 128]),
                     reads=["stg_out"], dma=True)

        def mixer_conv(p, l, ns, groups, TTp):
            win = w_in[l]
            if ns:
                conv_state_in(l)
            u, slot = ws.get([(KC, 0, GR, win[:, S7:S7 + GR].rearrange("(k p) n -> p k n", p=128))])
            proj_fm(slot, u, KC, xnT, xn_res, groups, BY, M=GR)
            ws.release(u)
            for gi, (c0, n) in enumerate(groups):
                S.op("act", lambda e, o=glrT[0:GR, c0:c0 + n], i=ps[BY[gi]][0:GR, 0:n]: e.copy(out=o, in_=i),
                     reads=[("ps", BY[gi])], writes=["glrT"])
            for c in range(CC):
                u, slot = ws.get([wunit(win, S1 + c * 128)])
                proj_fm(slot, u, KC, xnT, xn_res, groups, BX)
                ws.release(u)
                for gi, (c0, n) in enumerate(groups):
                    S.op("act", lambda e, o=tA[:, c0:c0 + n], i=ps[BX[gi]][:, 0:n]: e.copy(out=o, in_=i),
                         reads=[("ps", BX[gi])], writes=["tA"])
                u, slot = ws.get([wunit(win, S2 + c * 128)])
                proj_fm(slot, u, KC, xnT, xn_res, groups, BY)
                ws.release(u)
                for gi, (c0, n) in enumerate(groups):
                    S.op("dve", lambda e, o=tB[:, 2 + c0:2 + c0 + n], a=tA[:, c0:c0 + n], i=ps[BY[gi]][:, 0:n]:
                         e.tensor_tensor(out=o, in0=a, in1=i, op=ALU.mult),
                         reads=[("ps", BY[gi]), "tA"], writes=["tB"])
                S.op("dve", lambda e: e.tensor_copy(out=tB[:, 0:2], in_=tB[:, 2 + T + NS:2 + T + NX]),
                     reads=["tB"], writes=["tB"])
                w0, w1, w2 = vcol("cw", l, c), vcol("cw", l, 8 + c), vcol("cw", l, 16 + c)
                S.op("act", lambda e, w0=w0: e.activation(out=tC[:, 0:T], in_=tB[:, 0:T], func=AF.Copy, scale=w0),
                     reads=["tB", "vec"], writes=["tC"])
                S.op("dve", lambda e, w1=w1: e.scalar_tensor_tensor(out=tC[:, 0:T], in0=tB[:, 1:1 + T], scalar=w1,
                                                                     in1=tC[:, 0:T], op0=ALU.mult, op1=ALU.add),
                     reads=["tB", "tC", "vec"], writes=["tC"])
                S.op("dve", lambda e, w2=w2: e.scalar_tensor_tensor(out=tC[:, 0:T], in0=tB[:, 2:2 + T], scalar=w2,
                                                                     in1=tC[:, 0:T], op0=ALU.mult, op1=ALU.add),
                     reads=["tB", "tC", "vec"], writes=["tC"])
                if ns:
                    S.op("act", lambda e, w0=w0, c=c: e.activation(out=tC[:, T:T + ns], in_=fst[:, c, 0:ns], func=AF.Copy, scale=w0),
                         reads=["fst", "vec", "tC"], writes=["tC"])
                    S.op("dve", lambda e, w1=w1, c=c: e.scalar_tensor_tensor(out=tC[:, T:T + ns], in0=fst[:, c, ns:2 * ns], scalar=w1,
                                                                              in1=tC[:, T:T + ns], op0=ALU.mult, op1=ALU.add),
                         reads=["fst", "tC", "vec"], writes=["tC"])
                    S.op("dve", lambda e, w2=w2: e.scalar_tensor_tensor(out=tC[:, T:T + ns], in0=tB[:, 2 + T:2 + T + ns], scalar=w2,
                                                                         in1=tC[:, T:T + ns], op0=ALU.mult, op1=ALU.add),
                         reads=["tB", "tC", "vec"], writes=["tC"])
                S.op("dve", lambda e, c=c: e.tensor_copy(out=nst[:, c, 0:2], in_=tB[:, T:T + 2]), reads=["tB"], writes=["nst"])
                if ns:
                    S.op("dve", lambda e, c=c: e.tensor_copy(out=nst[:, c, 2:2 + ns], in_=fst[:, c, ns:2 * ns]),
                         reads=["fst"], writes=["nst"])
                    S.op("dve", lambda e, c=c: e.tensor_copy(out=nst[:, c, 2 + ns:2 + 2 * ns], in_=tB[:, 2 + T:2 + T + ns]),
                         reads=["tB"], writes=["nst"])
                u, slot = ws.get([wunit(win, c * 128)])
                proj_fm(slot, u, KC, xnT, xn_res, groups, BX)
                ws.release(u)
                for gi, (c0, n) in enumerate(groups):
                    S.op("dve", lambda e, o=yT[:, c, c0:c0 + n], a=tC[:, c0:c0 + n], i=ps[BX[gi]][:, 0:n]:
                         e.tensor_tensor(out=o, in0=a, in1=i, op=ALU.mult),
                         reads=[("ps", BX[gi]), "tC"], writes=[("yT", c)])
            state_out(CC, ns, conv_p[p, l], conv_s[l, :, 0, :], conv_s[l, :, 1, :])

        def gla_post(h, l, M, c0, o_ap, obank, sg_ap, par, tbank, ores=None):
            ores = ores or ("ps", obank)
            ss, rs, yb = g_ss[par], g_rs[par], g_yb[par]
            tps = psb[tbank]
            S.op("dve", lambda e: e.memset(ss[0:M, 0:2], 0.0), writes=[("g_ss", par)])
            S.op("act", lambda e: e.activation(out=g_junk[0:M, :], in_=o_ap, func=AF.Square, accum_out=ss[0:M, 0:1]),
                 reads=[ores], writes=["g_junk", ("g_ss", par)])
            S.op("act", lambda e: e.activation(out=rs[0:M, 0:1], in_=ss[0:M, 0:1], func=AF.Ln, scale=1.0 / DV, bias=EPS),
                 reads=[("g_ss", par)], writes=[("g_rs", par)])
            S.op("act", lambda e: e.activation(out=rs[0:M, 0:1], in_=rs[0:M, 0:1], func=AF.Exp, scale=-0.5),
                 reads=[("g_rs", par)], writes=[("g_rs", par)])
            S.op("dve", lambda e: e.scalar_tensor_tensor(out=yb[0:M, :], in0=o_ap, scalar=rs[0:M, 0:1],
                                                          in1=sg_ap, op0=ALU.mult, op1=ALU.mult),
                 reads=[ores, ("g_rs", par), "g_sg", "g_sgs"], writes=[("g_yb", par)])
            for vc in range(2):
                S.op("pe", lambda e, vc=vc: e.transpose(out=tps[:, 256 + vc * 128:256 + vc * 128 + M],
                                                        in_=yb[0:M, vc * 128:(vc + 1) * 128], identity=identb[0:M, 0:M]),
                     reads=[("g_yb", par), "identb"], writes=[("ps", tbank)], signal=(vc == 1))
            for vc in range(2):
                ch = CC + 2 * h + vc
                S.op("dve", lambda e, vc=vc, ch=ch: e.tensor_scalar_mul(yT[:, ch, c0:c0 + M], tps[:, 256 + vc * 128:256 + vc * 128 + M],
                                                                         vcol("gng", l, 2 * h + vc)),
                     reads=[("ps", tbank), "vec"], writes=[("yT", ch)])

        def o_store(tb):
            return 4 + tb // 2, ps[4 + tb // 2][:, (tb % 2) * DV:(tb % 2 + 1) * DV]

        def mixer_gla(p, l, ns):
            for h in range(H):
                gla_head(p, l, ns, h)
            S.op("sp", lambda e: e.dma_start(out=gla_p[p, l].rearrange("h k v -> k h v"), in_=Sst[:]),
                 reads=[("S", hh) for hh in range(H)], dma=True)

        def gla_head(p, l, ns, h):
            win = w_in[l]
            sc = 1.0 / 16.0
            dst = [(0, 0), (0, 128), (1, 0), (1, 128), (1, 256), (1, 384)]
            if True:
                cols = (S3 + h * DK, S4 + h * DK, S5 + h * DV, S5 + h * DV + 128, S6 + h * DV, S6 + h * DV + 128)
                units = [ws.get([wunit(win, c0)]) for c0 in cols]

                def P(i, M, c0, last=False):
                    (u, slot), (bk, co) = units[i], dst[i]
                    for k in range(KC):
                        S.op("pe", lambda e, o=ps[bk][0:M, co:co + 128], lt=xnT[:, k, c0:c0 + M], r=slot[:, k, :],
                             a=(k == 0), b=(k == KC - 1): e.matmul(o, lhsT=lt, rhs=r, start=a, stop=b),
                             reads=[("w", u % NSLOT), "xnT"], writes=[("ps", bk)], signal=(k == KC - 1))
                    if last:
                        ws.release(u)

                def Z(M, c0):
                    S.op("pe", lambda e: e.matmul(ps[2][0:M, 0:128], lhsT=glrT[0:GR + 1, c0:c0 + M],
                                                  rhs=gw[0:GR + 1, l, h * DK:(h + 1) * DK], start=True, stop=True),
                         reads=["glrT", "gw"], writes=[("ps", 2)])
                    S.op("act", lambda e: e.activation(out=g_e[0:M, :], in_=ps[2][0:M, 0:128], func=AF.Exp, scale=-1.0),
                         reads=[("ps", 2)], writes=["g_e"])
                    S.op("act", lambda e: e.activation(out=g_lf[0:M, :], in_=g_e[0:M, :], func=AF.Ln, bias=1.0),
                         reads=["g_e"], writes=["g_lf"])

                def cumsum(n):
                    S.op("pe", lambda e: e.matmul(ps[2][:, 128:256], lhsT=Uc, rhs=g_lf, start=True, stop=True),
                         reads=["g_lf", "consts"], writes=[("ps", 2)], signal=False)
                    S.op("pe", lambda e: e.matmul(ps[2][:, 256:384], lhsT=Lm, rhs=g_lf, start=True, stop=True),
                         reads=["g_lf", "consts"], writes=[("ps", 2)], signal=False)
                    S.op("pe", lambda e: e.matmul(ps[2][:, 384:386], lhsT=g_lf, rhs=cmat, start=True, stop=True),
                         reads=["g_lf", "consts"], writes=[("ps", 2)])

                def gates(n):
                    par = n % 2
                    S.op("act", lambda e: e.activation(out=g_eb, in_=ps[2][:, 128:256], func=AF.Exp, scale=-sc), reads=[("ps", 2)], writes=["g_eb"])
                    S.op("act", lambda e: e.activation(out=g_enb, in_=ps[2][:, 128:256], func=AF.Exp, scale=sc), reads=[("ps", 2)], writes=["g_enb"])
                    S.op("act", lambda e: e.activation(out=g_erev, in_=ps[2][:, 256:384], func=AF.Exp, scale=-sc), reads=[("ps", 2)], writes=["g_erev"])
                    S.op("act", lambda e: e.activation(out=g_dec[par], in_=ps[2][:, 384:386], func=AF.Exp, scale=-sc),
                         reads=[("ps", 2)], writes=[("g_dec", par)])
                    S.op("act", lambda e: e.activation(out=g_ecol[par], in_=ps[2][:, 385:386], func=AF.Exp, scale=-sc, bias=g_Bsn),
                         reads=[("ps", 2), "g_Bsn"], writes=[("g_ecol", par)])
                    S.op("dve", lambda e: e.scalar_tensor_tensor(out=g_Bsn, in0=ps[2][:, 384:385], scalar=-sc, in1=g_Bsn, op0=ALU.mult, op1=ALU.add),
                         reads=[("ps", 2), "g_Bsn"], writes=["g_Bsn"])

                def qkd(n):
                    par = n % 2
                    S.op("dve", lambda e: e.scalar_tensor_tensor(out=g_qd[par], in0=ps[0][:, 0:128], scalar=float(DK) ** -0.5, in1=g_eb,
                                                                  op0=ALU.mult, op1=ALU.mult), reads=[("ps", 0), "g_eb"], writes=[("g_qd", par)])
                    S.op("dve", lambda e: e.tensor_tensor(out=g_kd[par], in0=ps[0][:, 128:256], in1=g_enb, op=ALU.mult),
                         reads=[("ps", 0), "g_enb"], writes=[("g_kd", par)])
                    S.op("dve", lambda e: e.tensor_tensor(out=g_kdec[par], in0=ps[0][:, 128:256], in1=g_erev, op=ALU.mult),
                         reads=[("ps", 0), "g_erev"], writes=[("g_kdec", par)])

                def vg(n):
                    par = n % 2
                    S.op("act", lambda e: e.copy(out=g_vbf[par], in_=ps[1][:, 0:DV]), reads=[("ps", 1)], writes=[("g_vbf", par)])
                    S.op("act", lambda e: e.activation(out=g_sig, in_=ps[1][:, 256:512], func=AF.Exp, scale=-1.0), reads=[("ps", 1)], writes=["g_sig"])
                    S.op("act", lambda e: e.activation(out=g_sig, in_=g_sig, func=AF.Ln, bias=1.0), reads=["g_sig"], writes=["g_sig"])
                    S.op("act", lambda e: e.activation(out=g_sig, in_=g_sig, func=AF.Exp, scale=-1.0), reads=["g_sig"], writes=["g_sig"])
                    S.op("dve", lambda e: e.tensor_tensor(out=g_sgs[:, n, :], in0=ps[1][:, 256:512], in1=g_sig, op=ALU.mult),
                         reads=[("ps", 1), "g_sig"], writes=["g_sgs"])

                def Sbf(n):
                    par = n % 2
                    S.op("dve", lambda e: e.tensor_scalar_mul(g_Sbf, Sst[:, h, :], g_dec[par][:, 1:2]),
                         reads=[("S", h), ("g_dec", par)], writes=["g_Sbf"])

                def T1(tb):
                    par = tb % 2
                    S.op("pe", lambda e: e.transpose(out=psb[3][:, 0:128], in_=g_qd[par], identity=identb[:]), reads=[("g_qd", par), "identb"],
                         writes=[("ps", 3)], signal=False)
                    S.op("pe", lambda e: e.transpose(out=psb[3][:, 128:256], in_=g_kd[par], identity=identb[:]), reads=[("g_kd", par), "identb"],
                         writes=[("ps", 3)])
                    S.op("dve", lambda e: e.tensor_copy(out=g_qkT, in_=psb[3][:, 0:256]), reads=[("ps", 3)], writes=["g_qkT"])
                    S.op("dve", lambda e: e.tensor_scalar_mul(g_qeT[:, tb, :], psb[3][:, 0:128], g_ecol[par][:, 0:1]),
                         reads=[("ps", 3), ("g_ecol", par)], writes=["g_qeT"])

                def T2(tb):
                    S.op("pe", lambda e: e.matmul(ps[0][:, 256:384], lhsT=g_qkT[:, 128:256], rhs=g_qkT[:, 0:128], start=True, stop=True),
                         reads=["g_qkT"], writes=[("ps", 0)])
                    S.op("dve", lambda e: e.tensor_tensor(out=g_ATm, in0=ps[0][:, 256:384], in1=maskT, op=ALU.mult),
                         reads=[("ps", 0), "consts"], writes=["g_ATm"])

                def T3(tb):
                    par = tb % 2
                    ob, o_ap = o_store(tb)
                    S.op("pe", lambda e: e.matmul(o_ap, lhsT=g_ATm, rhs=g_vbf[par], start=False, stop=False, skip_group_check=True),
                         reads=["g_ATm", ("g_vbf", par)], writes=[("ps", ob)], signal=False)
                    S.op("pe", lambda e: e.matmul(o_ap, lhsT=g_qkT[:, 0:128], rhs=g_Sbf, start=False, stop=True, skip_group_check=True),
                         reads=["g_qkT", "g_Sbf"], writes=[("ps", ob)])
                    S.op("pe", lambda e: e.matmul(ps[3][:, 256:512], lhsT=g_kdec[par], rhs=g_vbf[par], start=True, stop=True),
                         reads=[("g_kdec", par), ("g_vbf", par)], writes=[("ps", 3)])
                    S.op("dve", lambda e: e.scalar_tensor_tensor(out=Sst[:, h, :], in0=Sst[:, h, :], scalar=g_dec[par][:, 0:1], in1=ps[3][:, 256:512],
                                                                  op0=ALU.mult, op1=ALU.add),
                         reads=[("ps", 3), ("g_dec", par), ("S", h), "g_Sbf"], writes=[("S", h)])

                S.op("dve", lambda e: e.memset(Sst[:, h, :], 0.0), writes=[("S", h)])
                S.op("dve", lambda e: e.memset(g_Bsn, 0.0), writes=["g_Bsn"])
                one_tile = (NTB == 1)
                def ld(r):
                    S.op("sp", lambda e: e.dma_start(out=g_Sb1[r % NSB], in_=st_gla[l, r, h]), writes=[("g_Sb", r % NSB)], dma=True)

                if ns:
                    M = NS
                    for r in range(min(NSB, M)):
                        ld(r)
                    Z(M, T)
                    for i in range(6):
                        P(i, M, T)
                    S.op("act", lambda e: e.activation(out=g_as[0:M, :], in_=g_lf[0:M, :], func=AF.Exp, scale=-sc), reads=["g_lf"], writes=["g_as"])
                    S.op("dve", lambda e: e.tensor_scalar_mul(g_qs[0:M, :], ps[0][0:M, 0:128], float(DK) ** -0.5),
                         reads=[("ps", 0)], writes=["g_qs"])
                    S.op("dve", lambda e: e.tensor_copy(out=g_ks[0:M, :], in_=ps[0][0:M, 128:256]), reads=[("ps", 0)], writes=["g_ks"])
                    S.op("act", lambda e: e.copy(out=g_vsb[0:M, :], in_=ps[1][0:M, 0:DV]), reads=[("ps", 1)], writes=["g_vs"])
                    S.op("act", lambda e: e.activation(out=g_sig[0:M, :], in_=ps[1][0:M, 256:512], func=AF.Exp, scale=-1.0), reads=[("ps", 1)], writes=["g_sig"])
                    S.op("act", lambda e: e.activation(out=g_sig[0:M, :], in_=g_sig[0:M, :], func=AF.Ln, bias=1.0), reads=["g_sig"], writes=["g_sig"])
                    S.op("act", lambda e: e.activation(out=g_sig[0:M, :], in_=g_sig[0:M, :], func=AF.Exp, scale=-1.0), reads=["g_sig"], writes=["g_sig"])
                    S.op("dve", lambda e: e.tensor_tensor(out=g_sg[0:M, :], in0=ps[1][0:M, 256:512], in1=g_sig[0:M, :], op=ALU.mult),
                         reads=[("ps", 1), "g_sig"], writes=["g_sg"])
                    S.op("dve", lambda e: e.tensor_tensor(out=g_e[0:M, :], in0=g_qs[0:M, :], in1=g_ks[0:M, :], op=ALU.mult),
                         reads=["g_qs", "g_ks", "g_lf"], writes=["g_e"])
                    S.op("dve", lambda e: e.reduce_sum(out=g_c[0:M, 0:1], in_=g_e[0:M, :], axis=mybir.AxisListType.X),
                         reads=["g_e"], writes=["g_c"])
                    S.op("dve", lambda e: e.tensor_scalar_mul(g_cv[0:M, :], ps[1][0:M, 0:DV], g_c[0:M, 0:1]),
                         reads=[("ps", 1), "g_c"], writes=["g_cv"])
                    S.op("dve", lambda e: e.tensor_tensor(out=g_qs[0:M, :], in0=g_qs[0:M, :], in1=g_as[0:M, :], op=ALU.mult),
                         reads=["g_qs", "g_as", "g_e"], writes=["g_qs"])
                    for i, src in enumerate((g_as, g_ks, g_qs)):
                        S.op("pe", lambda e, i=i, src=src: e.transpose(out=ps[2][:, 128 + i * M:128 + (i + 1) * M], in_=src[0:M, :],
                                                                       identity=ident[0:M, 0:M]),
                             reads=["g_as", "g_ks", "g_qs", "consts"], writes=[("ps", 2)], signal=(i == 2))
                    S.op("dve", lambda e: e.tensor_copy(out=g_akq[:, 0:3 * M], in_=ps[2][:, 128:128 + 3 * M]), reads=[("ps", 2)], writes=["g_akq"])
                    S.op("dve", lambda e: e.tensor_tensor(out=g_Qmb, in0=g_akq[:, None, 2 * M:3 * M].broadcast_to([128, M, M]), in1=idb16, op=ALU.mult),
                         reads=["g_akq", "consts"], writes=["g_Qm"])
                    S.op("dve", lambda e: e.tensor_tensor(out=g_Kmb, in0=g_ks[0:M, None, :].broadcast_to([M, M, 128]),
                                                           in1=ident[0:M, 0:M, None].broadcast_to([M, M, 128]), op=ALU.mult),
                         reads=["g_ks", "consts"], writes=["tA", "tB"])
                for b in range(4, 8):
                    S.op("dve", lambda e, b=b: e.memset(ps[b][:, :], 0.0), writes=[("ps", b)])
                Z(128, 0)
                P(0, 128, 0, one_tile)
                cumsum(0); gates(0)
                P(1, 128, 0, one_tile); P(2, 128, 0, one_tile); P(3, 128, 0, one_tile)
                qkd(0)
                P(4, 128, 0, one_tile); P(5, 128, 0, one_tile)
                vg(0); Sbf(0)
                for tb in range(NTB):
                    n = tb + 1
                    has = n < NTB
                    last = (n == NTB - 1)
                    c0 = n * 128
                    if has:
                        Z(128, c0)
                        P(0, 128, c0, last)
                    T1(tb)
                    if has:
                        cumsum(n); gates(n)
                        P(1, 128, c0, last)
                    T2(tb)
                    if has:
                        P(2, 128, c0, last); P(3, 128, c0, last)
                    T3(tb)
                    if has:
                        qkd(n)
                        P(4, 128, c0, last); P(5, 128, c0, last)
                        vg(n); Sbf(n)
                S.op("act", lambda e: e.activation(out=g_Etot, in_=g_Bsn, func=AF.Exp), reads=["g_Bsn"], writes=["g_Etot"])
                S.op("sp", lambda e, h=h: e.dma_start(out=ibS[l][h].ap(), in_=Sst[:, h, :]), reads=[("S", h)], writes=[("ibS", l, h)], dma=True)
                S.op("pool", lambda e, h=h: e.collective_compute("AllGather", ALU.bypass, replica_groups=PAIRS,
                                                                  ins=[ibS[l][h].ap().opt()], outs=[obS[l][h].ap().opt()]),
                     reads=[("ibS", l, h)], writes=[("obS", l, h)], cc=ccsems[l * H + h])
                if ns:
                    M = NS

                    def dS(r):
                        bk = 3 if r % 2 == 0 else 0
                        S.op("pe", lambda e: e.matmul(ps[bk][:, 256:512], lhsT=g_Kmb[:, r, :], rhs=g_vsb[0:M, :], start=True, stop=True),
                             reads=["tA", "tB", "g_vs"], writes=[("ps", bk)])

                    dS(0)
                    for r in range(M):
                        bk = 3 if r % 2 == 0 else 0
                        buf = g_Sb1[r % NSB]
                        bres = ("g_Sb", r % NSB)
                        S.op("act", lambda e, buf=buf, r=r: e.copy(out=g_Sbb[r % 2], in_=buf), reads=[bres], writes=[("g_Sbb", r % 2)])
                        S.op("dve", lambda e, buf=buf, bk=bk, r=r: e.scalar_tensor_tensor(
                            out=buf, in0=buf, scalar=g_akq[:, r:r + 1], in1=ps[bk][:, 256:512], op0=ALU.mult, op1=ALU.add),
                            reads=[("ps", bk), "g_akq", bres], writes=[bres])
                        if r + 1 < M:
                            dS(r + 1)
                        S.op("pe", lambda e, r=r: e.matmul(ps[2][0:M, 256:512], lhsT=g_Qmb[:, r, :], rhs=g_Sbb[r % 2],
                                                           start=(r == 0), stop=(r == M - 1)),
                             reads=["g_Qm", ("g_Sbb", r % 2)], writes=[("ps", 2)])
                        S.op("sp", lambda e, buf=buf, r=r: e.dma_start(out=gla_s[l, r, h], in_=buf), reads=[bres], dma=True)
                        if r + NSB < M:
                            ld(r + NSB)
                    S.op("sp", lambda e: e.dma_start(out=g_SA, in_=obS[l][h].ap()[0:128, :]), reads=[("obS", l, h)],
                         writes=["g_SA"], dma=True)
                    S.op("dve", lambda e: e.tensor_tensor(out=g_cv[0:M, :], in0=g_cv[0:M, :], in1=ps[2][0:M, 256:512], op=ALU.add),
                         reads=[("ps", 2), "g_cv"], writes=["g_cv"])
                    gla_post(h, l, M, T, g_cv[0:M, :], 2, g_sg[0:M, :], 0, 3, ores="g_cv")
                if not ns:
                    S.op("sp", lambda e, h=h: e.dma_start(out=g_SA, in_=obS[l][h].ap()[0:128, :]), reads=[("obS", l, h)], writes=["g_SA"], dma=True)
                S.op("dve", lambda e: e.tensor_scalar_mul(g_SA, g_SA, flag[:, 0:1]), reads=["g_SA", "flag"], writes=["g_SA"])
                S.op("dve", lambda e: e.tensor_copy(out=g_SAb, in_=g_SA), reads=["g_SA"], writes=["g_SAb"])
                for tb in range(NTB):
                    ob, o_ap = o_store(tb)
                    S.op("pe", lambda e, tb=tb, o_ap=o_ap: e.matmul(o_ap, lhsT=g_qeT[:, tb, :], rhs=g_SAb, start=False, stop=True, skip_group_check=True),
                         reads=["g_qeT", "g_SAb"], writes=[("ps", ob)])
                S.op("dve", lambda e: e.memset(g_ss8[:, 0:NTB], 0.0), writes=["g_ss8"])
                for tb in range(NTB):
                    ob, o_ap = o_store(tb)
                    S.op("act", lambda e, tb=tb, o_ap=o_ap: e.activation(out=g_junk, in_=o_ap, func=AF.Square, accum_out=g_ss8[:, tb:tb + 1]),
                         reads=[("ps", ob), "g_ss8"], writes=["g_junk", ("g_ss8", tb)])
                    S.op("act", lambda e, tb=tb: e.activation(out=g_rs8[:, tb:tb + 1], in_=g_ss8[:, tb:tb + 1], func=AF.Ln, scale=1.0 / DV, bias=EPS),
                         reads=[("g_ss8", tb)], writes=[("g_rs8", tb)])
                    S.op("act", lambda e, tb=tb: e.activation(out=g_rs8[:, tb:tb + 1], in_=g_rs8[:, tb:tb + 1], func=AF.Exp, scale=-0.5),
                         reads=[("g_rs8", tb)], writes=[("g_rs8", tb)])

                def evac(tb):
                    tbank = 3 if tb % 2 == 0 else 0
                    for vc in range(2):
                        ch = CC + 2 * h + vc
                        S.op("dve", lambda e, vc=vc, ch=ch: e.tensor_scalar_mul(yT[:, ch, tb * 128:(tb + 1) * 128],
                                                                                 psb[tbank][:, 256 + vc * 128:384 + vc * 128],
                                                                                 vcol("gng", l, 2 * h + vc)),
                             reads=[("ps", tbank), "vec"], writes=[("yT", ch)])

                for tb in range(NTB):
                    ob, o_ap = o_store(tb)
                    par = tb % 2
                    tbank = 3 if par == 0 else 0
                    S.op("dve", lambda e, tb=tb, o_ap=o_ap, par=par: e.scalar_tensor_tensor(
                        out=g_yb[par], in0=o_ap, scalar=g_rs8[:, tb:tb + 1], in1=g_sgs[:, tb, :], op0=ALU.mult, op1=ALU.mult),
                        reads=[("ps", ob), ("g_rs8", tb), "g_sgs"], writes=[("g_yb", par)])
                    for vc in range(2):
                        S.op("pe", lambda e, vc=vc, par=par, tbank=tbank: e.transpose(out=psb[tbank][:, 256 + vc * 128:384 + vc * 128],
                                                                                     in_=g_yb[par][:, vc * 128:(vc + 1) * 128], identity=identb[:]),
                             reads=[("g_yb", par), "identb"], writes=[("ps", tbank)], signal=(vc == 1))
                    if tb >= 1:
                        evac(tb - 1)
                evac(NTB - 1)
                S.op("dve", lambda e, h=h: e.scalar_tensor_tensor(out=Sst[:, h, :], in0=g_SA, scalar=g_Etot[:, 0:1], in1=Sst[:, h, :],
                                                                   op0=ALU.mult, op1=ALU.add),
                     reads=["g_SA", "g_Etot", ("S", h)], writes=[("S", h)])
        def xchg_x(idx):
            S.op("dve", lambda e: e.tensor_copy(out=xhs[:], in_=xT[:, :, T - 2:T]), reads=[("xT", k) for k in range(KC)], writes=["xhs"])
            S.op("sp", lambda e: e.dma_start(out=ibX[idx].ap(), in_=xhs[:].rearrange("p k t -> p (k t)")), reads=["xhs"],
                 writes=[("ibX", idx)], dma=True)
            S.op("pool", lambda e: e.collective_compute("AllGather", ALU.bypass, replica_groups=PAIRS,
                                                         ins=[ibX[idx].ap().opt()], outs=[obX[idx].ap().opt()]),
                 reads=[("ibX", idx)], writes=[("obX", idx)], cc=ccsems[L * H + idx])
            S.op("sp", lambda e: e.dma_start(out=xhr[:].rearrange("p k t -> p (k t)"), in_=obX[idx].ap()[0:128, :]), reads=[("obX", idx)],
                 writes=["xhr"], dma=True)
            S.op("dve", lambda e: e.tensor_scalar_mul(xT[:, :, T + NS:T + NX], xhr[:], flag[:, 0:1]), reads=["xhr", "flag"],
                 writes=[("xT", k) for k in range(KC)])

        def out_proj(l, groups):
            for oc in range(KC):
                u, slot = ws.get([wunit(w_out[l], oc * 128)])
                bk = BX if oc % 2 == 0 else BY
                proj_fm(slot, u, KC, yT, lambda k: [("yT", k)], groups, bk)
                ws.release(u)
                for gi, (c0, n) in enumerate(groups):
                    S.op("dve", lambda e, o=xT[:, oc, c0:c0 + n], i=ps[bk[gi]][:, 0:n]: e.tensor_tensor(out=o, in0=o, in1=i, op=ALU.add),
                         reads=[("ps", bk[gi])], writes=[("xT", oc)])

        def ffn(p, l, ns, groups, TTp):
            wup, wdn = w_up[l], w_down[l]
            for g in range(NG):
                if ns:
                    for r in range(2):
                        S.op("sp", lambda e, r=r, g=g: e.dma_start(out=stg_in[r * NS:(r + 1) * NS, 0:FG * 128],
                                                                    in_=st_ffn[l, :, r, g * FG * 128:(g + 1) * FG * 128]),
                             writes=["stg_in"], dma=True)
                    for j in range(FG):
                        S.op("pe", lambda e, j=j: e.transpose(out=ps[7][:, j * 32:(j + 1) * 32], in_=stg_in[0:32, j * 128:(j + 1) * 128],
                                                              identity=ident[0:32, 0:32]),
                             reads=["stg_in", "consts"], writes=[("ps", 7)], signal=(j == FG - 1))
                    S.op("dve", lambda e: e.tensor_copy(out=fst[:, 0:FG, :], in_=ps[7][:, 0:FG * 32].rearrange("p (j c) -> p j c", c=32)),
                         reads=[("ps", 7)], writes=["fst"])
                for j in range(FG):
                    fc = g * FG + j
                    u, slot = ws.get([wunit(wup, fc * 128)])
                    proj_fm(slot, u, KC, xnT, xn_res, groups, BX)
                    ws.release(u)
                    for gi, (c0, n) in enumerate(groups):
                        S.op("act", lambda e, o=tA[:, 2 + c0:2 + c0 + n], i=ps[BX[gi]][:, 0:n]: e.copy(out=o, in_=i),
                             reads=[("ps", BX[gi])], writes=["tA"])
                    S.op("act", lambda e: e.copy(out=tA[:, 0:2], in_=tA[:, 2 + T + NS:2 + T + NX]), reads=["tA"], writes=["tA"])
                    w0, w1, w2 = vcol("fcw", l, fc), vcol("fcw", l, FC + fc), vcol("fcw", l, 2 * FC + fc)
                    S.op("act", lambda e, w2=w2, fc=fc: e.activation(out=tB[:, 0:TTp], in_=tA[:, 2:2 + TTp], func=AF.Identity,
                                                                     scale=w2, bias=vcol("fcb", l, fc)),
                         reads=["tA", "vec"], writes=["tB"])
                    S.op("dve", lambda e, w1=w1: e.scalar_tensor_tensor(out=tB[:, 0:T], in0=tA[:, 1:1 + T], scalar=w1, in1=tB[:, 0:T],
                                                                         op0=ALU.mult, op1=ALU.add), reads=["tA", "tB", "vec"], writes=["tB"])
                    S.op("dve", lambda e, w0=w0: e.scalar_tensor_tensor(out=tB[:, 0:T], in0=tA[:, 0:T], scalar=w0, in1=tB[:, 0:T],
                                                                         op0=ALU.mult, op1=ALU.add), reads=["tA", "tB", "vec"], writes=["tB"])
                    if ns:
                        S.op("dve", lambda e, w1=w1, j=j: e.scalar_tensor_tensor(out=tB[:, T:T + ns], in0=fst[:, j, ns:2 * ns], scalar=w1,
                                                                                  in1=tB[:, T:T + ns], op0=ALU.mult, op1=ALU.add),
                             reads=["fst", "tB", "vec"], writes=["tB"])
                        S.op("dve", lambda e, w0=w0, j=j: e.scalar_tensor_tensor(out=tB[:, T:T + ns], in0=fst[:, j, 0:ns], scalar=w0,
                                                                                  in1=tB[:, T:T + ns], op0=ALU.mult, op1=ALU.add),
                             reads=["fst", "tB", "vec"], writes=["tB"])
                    S.op("act", lambda e: e.activation(out=tC[:, 0:TTp], in_=tB[:, 0:TTp], func=AF.Silu), reads=["tB"], writes=["tC"])
                    S.op("dve", lambda e, j=j: e.tensor_copy(out=nst[:, j, 0:2], in_=tA[:, T:T + 2]), reads=["tA"], writes=["nst"])
                    if ns:
                        S.op("dve", lambda e, j=j: e.tensor_copy(out=nst[:, j, 2:2 + ns], in_=fst[:, j, ns:2 * ns]), reads=["fst"], writes=["nst"])
                        S.op("dve", lambda e, j=j: e.tensor_copy(out=nst[:, j, 2 + ns:2 + 2 * ns], in_=tA[:, 2 + T:2 + T + ns]),
                             reads=["tA"], writes=["nst"])
                    u, slot = ws.get([wunit(wup, DFF + fc * 128)])
                    proj_fm(slot, u, KC, xnT, xn_res, groups, BY)
                    ws.release(u)
                    for gi, (c0, n) in enumerate(groups):
                        S.op("dve", lambda e, o=yT[:, j, c0:c0 + n], a=tC[:, c0:c0 + n], i=ps[BY[gi]][:, 0:n]:
                             e.tensor_tensor(out=o, in0=a, in1=i, op=ALU.mult),
                             reads=[("ps", BY[gi]), "tC"], writes=[("yT", j)])
                cs = slice(g * FG * 128, (g + 1) * FG * 128)
                state_out(FG, ns, ffn_p[p, l, :, cs], ffn_s[l, :, 0, cs], ffn_s[l, :, 1, cs])
                for oc in range(KC):
                    u, slot = ws.get([(FG, 0, 128, wdn[g * FG * 128:(g + 1) * FG * 128, oc * 128:(oc + 1) * 128]
                                       .rearrange("(k p) n -> p k n", p=128))])
                    bk = BX if oc % 2 == 0 else BY
                    proj_fm(slot, u, FG, yT, lambda k: [("yT", k)], groups, bk)
                    ws.release(u)
                    for gi, (c0, n) in enumerate(groups):
                        S.op("dve", lambda e, o=xT[:, oc, c0:c0 + n], i=ps[bk[gi]][:, 0:n]: e.tensor_tensor(out=o, in0=o, in1=i, op=ALU.add),
                             reads=[("ps", bk[gi])], writes=[("xT", oc)])

        def final_out(p, ns, groups, TTp):
            rms_stats(groups, TTp)
            for dc in range(KC):
                S.op("dve", lambda e, o=xT[:, dc, 0:TTp], g=vcol("fng", 0, dc):
                     e.scalar_tensor_tensor(out=o, in0=o, scalar=g, in1=tC[:, 0:TTp], op0=ALU.mult, op1=ALU.mult),
                     reads=["tC", "vec"], writes=[("xT", dc)])
            tiles = [(tb * 128, 128, y_p[p * T + tb * 128:p * T + (tb + 1) * 128, :]) for tb in range(NTB)]
            if ns:
                tiles.append((T, ns, y_s))
            for (c0, M, dst) in tiles:
                for q in range(4):
                    for k in range(4):
                        dc = 4 * q + k
                        S.op("pe", lambda e, q=q, k=k, dc=dc, c0=c0, M=M: e.transpose(out=ps[q][0:M, k * 128:(k + 1) * 128], in_=xT[:, dc, c0:c0 + M],
                                                                           identity=ident),
                             reads=[("xT", dc), "consts"], writes=[("ps", q)], signal=(k == 3))
                    eng = "act" if q % 2 == 0 else "dve"
                    if eng == "act":
                        S.op("act", lambda e, q=q, M=M: e.copy(out=io[0:M, q * 512:(q + 1) * 512], in_=ps[q][0:M, :]),
                             reads=[("ps", q)], writes=["tA", "tB"])
                    else:
                        S.op("dve", lambda e, q=q, M=M: e.tensor_copy(out=io[0:M, q * 512:(q + 1) * 512], in_=ps[q][0:M, :]),
                             reads=[("ps", q)], writes=["tA", "tB"])
                S.op("sp", lambda e, dst=dst, M=M: e.dma_start(out=dst, in_=io[0:M, :]), reads=["tA", "tB"], dma=True)

        def init():
            S.op("sp", lambda e: e.dma_start(out=cst_t[:], in_=consts_d), writes=["consts"], dma=True)
            S.op("sp", lambda e: e.dma_start(out=tabc[:, 0:cfg.VB * 128].rearrange("p (b c) -> p b c", c=128),
                                             in_=vecs_d.rearrange("(b r) c -> r b c", r=128)), writes=["tA", "tB"], dma=True)
            for l in range(L):
                S.op("pool", lambda e, l=l: e.dma_start(out=gw[0:GR + 1, l, :], in_=gw_d[l]), writes=["gw"], dma=True)
            S.op("sp", lambda e: e.dma_start(out=flag[:], in_=flag_d), writes=["flag"], dma=True)
            S.op("dve", lambda e: e.tensor_copy(out=identb[:], in_=ident), reads=["consts"], writes=["identb"])
            S.op("dve", lambda e: e.memset(onesb[:], 1.0), writes=["onesb"])
            for b in range(cfg.VB):
                S.op("pe", lambda e, b=b: e.transpose(out=ps[b % 4][:, 0:128], in_=tabc[:, b * 128:(b + 1) * 128], identity=ident),
                     reads=["tA", "tB", "consts"], writes=[("ps", b % 4)])
                S.op("act", lambda e, b=b: e.copy(out=vec[:, b * 128:(b + 1) * 128], in_=ps[b % 4][:, 0:128]),
                     reads=[("ps", b % 4)], writes=["vec"])

        def program():
            init()
            for p in range(NPASS):
                ns = NS if p == 0 else 0
                TTp = T + (NX if ns else 0)
                groups = colgroups(ns)
                S.barrier()
                S.op("dve", lambda e: e.memset(glrT[:], 1.0), writes=["glrT"])
                load_x(p, ns)
                for l in range(L):
                    S.barrier()
                    if l == 0:
                        S.op("dve", lambda e: e.memset(yT[:, :, T:TT], 0.0), writes=[("yT", k) for k in range(KC)])
                    rms_norm("nmg", l, groups, TTp)
                    S.barrier()
                    mixer_conv(p, l, ns, groups, TTp)
                    S.barrier()
                    mixer_gla(p, l, ns)
                    out_proj(l, groups)
                    xchg_x(2 * l)
                    S.barrier()
                    rms_norm("nfg", l, groups, TTp)
                    S.barrier()
                    ffn(p, l, ns, groups, TTp)
                    if l + 1 < L:
                        xchg_x(2 * l + 1)
                S.barrier()
                final_out(p, ns, groups, TTp)

        S.dry = True
        program()
        S.dry = False
        ws.i = 0
        ws.start()
        program()

        with nc.Block() as block:
            @block.tensor
            def _(e):
                S.replay("pe", e)

            @block.scalar
            def _(e):
                S.replay("act", e)

            @block.vector
            def _(e):
                S.replay("dve", e)

            @block.gpsimd
            def _(e):
                S.replay("pool", e)

            @block.sync
            def _(e):
                S.replay("sp", e)
                S.final_wait(e)
    return nc


def make_consts():
    c = np.zeros((128, 770), np.float32)
    j = np.arange(128)[:, None]
    i = np.arange(128)[None, :]
    c[:, 0:128] = np.eye(128, dtype=np.float32)
    c[:, 128:256] = (j <= i).astype(np.float32) - (j <= 63).astype(np.float32)
    c[:, 256:384] = (j > i).astype(np.float32)
    c[:, 384:512] = (j <= i).astype(np.float32)
    c[:, 512] = 1.0
    c[:, 513] = (np.arange(128) <= 63).astype(np.float32)
    c[:, 514:770] = np.eye(16, dtype=np.float32).reshape(1, 256)
    return c


def pack_vecs(cfg, inp):
    rows = []
    for l in range(L):
        rows.append(inp["norm_mix_g"][l].reshape(16, 128))
        rows.append(inp["conv_w"][l].reshape(24, 128))
        rows.append(inp["gla_norm_g"][l].reshape(8, 128))
        rows.append(inp["norm_ffn_g"][l].reshape(16, 128))
        rows.append(inp["ffn_conv_w"][l].reshape(3 * cfg.FC, 128))
        rows.append(inp["ffn_conv_b"][l].reshape(cfg.FC, 128))
    rows.append(inp["final_norm_g"].reshape(16, 128))
    v = np.concatenate(rows, axis=0).astype(np.float32)
    out = np.zeros((cfg.VB * 128, 128), np.float32)
    out[:v.shape[0]] = v
    return out


_NC_CACHE = {}


def run(cfg, inp, ncores):
    key = (cfg.T, cfg.NPASS, cfg.DFF, cfg.NG)
    if key not in _NC_CACHE:
        _NC_CACHE[key] = build_nc(cfg)
    nc = _NC_CACHE[key]
    T = cfg.T
    consts = make_consts()
    vecs = pack_vecs(cfg, inp)
    gwp = np.ascontiguousarray(np.concatenate([inp["gate_w2"], inp["gate_b"][:, None, :]], axis=1), dtype=np.float32)
    shared = {"w_in": inp["w_in"], "w_out": inp["w_out"], "w_up": inp["w_up"], "w_down": inp["w_down"],
              "gw": gwp, "vecs": vecs, "consts": consts}
    in_maps = []
    for c in range(ncores):
        m = dict(shared)
        sl = slice(c * NS, (c + 1) * NS)
        b, half = c // 2, c % 2
        m["x_p"] = np.ascontiguousarray(inp["x_prompt"][b, half * T:(half + 1) * T])
        m["x_h"] = (np.ascontiguousarray(inp["x_prompt"][b, T - NH:T]) if half else np.zeros((NH, D), np.float32))
        fl = np.zeros((128, 2), np.float32)
        fl[:, 0] = float(half)
        fl[:, 1] = 1.0 - float(half)
        m["flag"] = fl
        m["x_s"] = np.ascontiguousarray(inp["x_sample"][sl, 0, :])
        m["st_conv"] = np.ascontiguousarray(inp["state_conv"][:, sl])
        m["st_gla"] = np.ascontiguousarray(inp["state_gla"][:, sl])
        m["st_ffn"] = np.ascontiguousarray(inp["state_ffn_conv"][:, sl])
        in_maps.append(m)
    res = run_bass_kernel_spmd(nc, in_maps, core_ids=list(range(ncores)))
    return res.results


def assemble(cfg, R, ncores):
    nb = ncores // 2
    y_prompt = np.stack([np.concatenate([R[2 * b]["y_p"], R[2 * b + 1]["y_p"]], 0) for b in range(nb)], 0)
    y_sample = np.concatenate([R[c]["y_s"] for c in range(ncores)], 0)[:, None, :]
    conv_p = np.stack([R[2 * b + 1]["conv_p"][0] for b in range(nb)], 1)
    gla_p = np.stack([R[2 * b + 1]["gla_p"][0] for b in range(nb)], 1)
    ffn_p = np.stack([R[2 * b + 1]["ffn_p"][0] for b in range(nb)], 1)
    conv_s = np.concatenate([R[c]["conv_s"] for c in range(ncores)], 1)
    gla_s = np.concatenate([R[c]["gla_s"] for c in range(ncores)], 1)
    ffn_s = np.concatenate([R[c]["ffn_s"] for c in range(ncores)], 1)
    return tuple(np.ascontiguousarray(a, dtype=np.float32) for a in
                 (y_prompt, y_sample, conv_p, gla_p, ffn_p, conv_s, gla_s, ffn_s))


def kernel(**inputs):
    inp = {k: np.asarray(v) for k, v in inputs.items()}
    cfg = Cfg()
    R = run(cfg, inp, 8)
    return assemble(cfg, R, 8)
```

```python
import numpy as np
from contextlib import ExitStack
import concourse.bass as bass
import concourse.mybir as mybir
from concourse.bass_utils import run_bass_kernel_spmd

F32 = mybir.dt.float32
BF16 = mybir.dt.bfloat16
ALU = mybir.AluOpType
AF = mybir.ActivationFunctionType

ENGS = ("pe", "act", "dve", "pool", "sp")

D = 2048
KC = 16
L = 2
CD = 1024
CC = 8
H = 4
DK = 128
DV = 256
GR = 16
S1, S2, S3 = 1024, 2048, 3072
S4, S5, S6, S7 = 3584, 4096, 5120, 6144
INC = 6160
EPS = 1e-6
NS = 16
NH = 2
NX = NS + NH
NSLOT = 6


class Cfg:
    def __init__(self, T=1024, NPASS=1, DFF=5632, NG=4):
        self.T, self.NPASS, self.DFF, self.NG = T, NPASS, DFF, NG
        self.FC = DFF // 128
        self.FG = self.FC // NG
        self.SEQ = T * NPASS
        self.TT = T + NX
        self.NTB = T // 128
        o = 0
        self.off = {}
        for l in range(L):
            for name, n in (("nmg", 16), ("cw", 24), ("gng", 8), ("nfg", 16),
                            ("fcw", 3 * self.FC), ("fcb", self.FC)):
                self.off[(name, l)] = o
                o += n
        self.off[("fng", 0)] = o
        o += 16
        self.VR = o
        self.VB = (o + 127) // 128


class Sched:
    def __init__(self, sems, dma_sems):
        self.sem = dict(zip(ENGS, sems))
        self.cnt = {e: 0 for e in ENGS}
        self.ops = {e: [] for e in ENGS}
        self.waited = {e: {} for e in ENGS}
        self.res = {}
        self.dma_sems = {k: list(v) for k, v in dma_sems.items()}
        self.dma_val = {k: [0] * len(v) for k, v in self.dma_sems.items()}
        self.dma_rr = {k: 0 for k in self.dma_sems}
        self.dry = False

    def _need(self, eng, h, waits):
        if h is None:
            return
        sem, val, heng = h
        if heng == "pe" and eng == "pe":
            return
        k = id(sem)
        if self.waited[eng].get(k, 0) >= val:
            return
        prev = waits.get(k)
        if prev is None or prev[1] < val:
            waits[k] = (sem, val)

    def op(self, eng, fn, reads=(), writes=(), signal=True, dma=False, cc=None):
        if self.dry:
            return None
        waits = {}
        for r in reads:
            ent = self.res.get(r)
            if ent is not None:
                self._need(eng, ent[0], waits)
        for w in writes:
            ent = self.res.get(w)
            if ent is not None:
                self._need(eng, ent[0], waits)
                for rh in ent[1]:
                    self._need(eng, rh, waits)
        if dma:
            i = self.dma_rr[eng]
            self.dma_rr[eng] = (i + 1) % len(self.dma_sems[eng])
            dsem = self.dma_sems[eng][i]
            if self.dma_val[eng][i] > 0:
                self._need(eng, (dsem, self.dma_val[eng][i], "dma"), waits)
            self.dma_val[eng][i] += 16
            handle = (dsem, self.dma_val[eng][i], "dma")
            inc = (dsem, 16)
        elif cc is not None:
            handle = (cc, 1, "cc")
            inc = (cc, None)
        elif signal:
            self.cnt[eng] += 1
            handle = (self.sem[eng], self.cnt[eng], eng)
            inc = (self.sem[eng], 1)
        else:
            handle = (self.sem[eng], self.cnt[eng] + 1, eng)
            inc = None
        wl = list(waits.values())
        for sem, val in wl:
            self.waited[eng][id(sem)] = val
        self.ops[eng].append((wl, fn, inc))
        for r in reads:
            self.res.setdefault(r, [None, []])[1].append(handle)
        for w in writes:
            self.res[w] = [handle, []]
        return handle

    def barrier(self, engs=("pe", "act", "dve", "sp")):
        if self.dry:
            return
        for e in engs:
            wl = []
            for o in ENGS:
                if o == e or self.cnt[o] == 0:
                    continue
                if o == "pool":
                    continue
                if self.waited[e].get(id(self.sem[o]), 0) < self.cnt[o]:
                    wl.append((self.sem[o], self.cnt[o]))
                    self.waited[e][id(self.sem[o])] = self.cnt[o]
            for q in ("sp", "act"):
                for i, dsem in enumerate(self.dma_sems[q]):
                    v = self.dma_val[q][i]
                    if v > 0 and self.waited[e].get(id(dsem), 0) < v:
                        wl.append((dsem, v))
                        self.waited[e][id(dsem)] = v
            if wl:
                self.ops[e].append((wl, None, None))

    def replay(self, eng, e):
        for wl, fn, inc in self.ops[eng]:
            for sem, val in wl:
                e.wait_ge(sem, val)
            if fn is None:
                continue
            ins = fn(e)
            if inc is not None:
                if inc[1] is None:
                    ins.then_inc(inc[0])
                else:
                    ins.then_inc(inc[0], inc[1])

    def final_wait(self, e):
        for q in self.dma_sems:
            for i, dsem in enumerate(self.dma_sems[q]):
                if self.dma_val[q][i] > 0:
                    e.wait_ge(dsem, self.dma_val[q][i])
        for en in ENGS:
            if self.cnt[en] > 0:
                e.wait_ge(self.sem[en], self.cnt[en])


def build_nc(cfg):
    T, TT, NPASS, DFF, FC, FG, NG, NTB = cfg.T, cfg.TT, cfg.NPASS, cfg.DFF, cfg.FC, cfg.FG, cfg.NG, cfg.NTB
    SEQ = cfg.SEQ
    nc = bass.Bass("TRN2", target_bir_lowering=False)

    def din(name, shape):
        return nc.dram_tensor(name, list(shape), F32, kind="ExternalInput").ap()

    def dout(name, shape):
        return nc.dram_tensor(name, list(shape), F32, kind="ExternalOutput").ap()

    x_p = din("x_p", [SEQ, D]); x_s = din("x_s", [NS, D]); x_h = din("x_h", [NH, D])
    flag_d = din("flag", [128, 2])
    st_conv = din("st_conv", [L, NS, 2, CD]); st_gla = din("st_gla", [L, NS, H, DK, DV])
    st_ffn = din("st_ffn", [L, NS, 2, DFF])
    w_in = din("w_in", [L, D, INC]); w_out = din("w_out", [L, D, D])
    w_up = din("w_up", [L, D, 2 * DFF]); w_down = din("w_down", [L, DFF, D])
    gw_d = din("gw", [L, GR + 1, H * DK])
    vecs_d = din("vecs", [cfg.VB * 128, 128])
    consts_d = din("consts", [128, 770])
    y_p = dout("y_p", [SEQ, D]); y_s = dout("y_s", [NS, D])
    conv_p = dout("conv_p", [NPASS, L, 2, CD]); gla_p = dout("gla_p", [NPASS, L, H, DK, DV])
    ffn_p = dout("ffn_p", [NPASS, L, 2, DFF])
    conv_s = dout("conv_s", [L, NS, 2, CD]); gla_s = dout("gla_s", [L, NS, H, DK, DV])
    ffn_s = dout("ffn_s", [L, NS, 2, DFF])

    es = ExitStack()
    with es:
        def sb(name, shape, dt=F32):
            return es.enter_context(nc.sbuf_tensor(name, list(shape), dt))

        xT = sb("xT", [128, KC, TT]); xnT = sb("xnT", [128, KC, TT], BF16)
        yT = sb("yT", [128, KC, TT], BF16)
        wsl = [sb("wsl%d" % i, [128, KC, 128], BF16) for i in range(NSLOT)]
        EW = TT + 2
        tabc_n = max(3 * EW, 2048 + EW)
        tabc = sb("tabc", [128, tabc_n])
        tA = tabc[:, 0:EW]; tB = tabc[:, EW:2 * EW]; tC = tabc[:, tabc_n - EW:tabc_n]
        io = tabc[:, 0:2048]
        nxb = (KC * TT // 2) // 2048
        if nxb >= 1:
            yflat = yT[:].rearrange("p k t -> p (k t)").bitcast(F32)
            xin = [(yflat[:, i * 2048:(i + 1) * 2048], [("xin", i)]) for i in range(nxb)]
        else:
            xin = [(io, ["tA", "tB"])]
        sqv = tB.bitcast(BF16)
        SCR = 7016
        scr = sb("scr", [128, SCR])
        cst_t = sb("consts_sb", [128, 770])
        ident = cst_t[:, 0:128]; Uc = cst_t[:, 128:256]; Lm = cst_t[:, 256:384]
        maskT = cst_t[:, 384:512]; cmat = cst_t[:, 512:514]
        idb16 = cst_t[:, 514:770].rearrange("p (r t) -> p r t", r=16)
        identb = sb("identb", [128, 128], BF16); onesb = sb("onesb", [128, 128], BF16)
        vec = sb("vec_sb", [128, cfg.VB * 128])
        gw = sb("gw_sb", [32, L, H * DK], BF16)
        glrT = sb("glrT", [32, TT], BF16)
        Sst = sb("Sst", [128, H, DV])
        ps = [es.enter_context(nc.psum_tensor("ps%d" % i, [128, 512], F32)) for i in range(8)]
        psb = [p[:].bitcast(BF16) for p in ps]
        sems = [es.enter_context(nc.semaphore("s_%s" % e)) for e in ENGS]
        dsems = {"sp": [es.enter_context(nc.semaphore("dsp_%d" % i)) for i in range(16)],
                 "pool": [es.enter_context(nc.semaphore("dpl_%d" % i)) for i in range(8)],
                 "act": [es.enter_context(nc.semaphore("dac_%d" % i)) for i in range(8)]}
        S = Sched(sems, dsems)
        ibS = [[nc.dram_tensor("ibS_%d_%d" % (l, h), [128, DV], F32, kind="Internal") for h in range(H)] for l in range(L)]
        obS = [[nc.dram_tensor("obS_%d_%d" % (l, h), [256, DV], F32, kind="Internal") for h in range(H)] for l in range(L)]
        ibX = [nc.dram_tensor("ibX_%d" % i, [128, 2 * KC], F32, kind="Internal") for i in range(2 * L)]
        obX = [nc.dram_tensor("obX_%d" % i, [256, 2 * KC], F32, kind="Internal") for i in range(2 * L)]
        ccsems = [es.enter_context(nc.semaphore("cc_%d" % i)) for i in range(L * H + 2 * L)]
        PAIRS = [[0, 1], [2, 3], [4, 5], [6, 7]]
        flag = sb("flag_sb", [128, 2]); xhs = sb("xhs", [128, KC, 2]); xhr = sb("xhr", [128, KC, 2])

        def sv(off, n, dt=F32, parts=128):
            v = scr[0:parts, off:off + n]
            return v.bitcast(dt) if dt != F32 else v
        g_e = sv(0, 128); g_lf = sv(128, 128); g_eb = sv(256, 128); g_enb = sv(384, 128)
        g_erev = sv(512, 128); g_dec = sv(640, 2); g_ss = sv(642, 2); g_rs = sv(644, 2)
        g_qd = sv(648, 64, BF16); g_kd = sv(712, 64, BF16); g_kdec = sv(776, 64, BF16)
        g_vbf = sv(840, 128, BF16); g_Sbf = sv(968, 128, BF16); g_qkT = sv(1096, 128, BF16)
        g_ATm = sv(1224, 64, BF16); g_sg = sv(1288, 256); g_yb = sv(1544, 128, BF16)
        g_Qm = sv(1672, 256).rearrange("p (r t) -> p r t", r=16)
        g_akq = sv(1928, 48); g_qs = sv(1976, 128); g_ks = sv(2104, 128); g_as = sv(2232, 128)
        g_vs = sv(2360, 256)
        g_Sb = [sv(2616 + i * 512, 512).rearrange("p (r v) -> p r v", r=2) for i in range(2)]
        g_Km = tabc[0:16, 0:2048].rearrange("p (r k) -> p r k", r=16)
        g_qeT = sv(3640, 64 * 8, BF16).rearrange("p (b t) -> p b t", t=128)
        g_sgs = sv(4152, 128 * 8, BF16).rearrange("p (b v) -> p b v", v=DV)
        g_SA = sv(5176, 256); g_SAb = sv(5432, 128, BF16); g_junk = sv(5560, 128, BF16)
        g_Bsn = sv(5688, 1); g_Etot = sv(5692, 1)
        g_ecol = [sv(5690, 1), sv(5694, 1)]
        g_dec = [g_dec, sv(5696, 2)]
        g_qd = [g_qd, sv(5700, 64, BF16)]; g_kd = [g_kd, sv(5764, 64, BF16)]; g_kdec = [g_kdec, sv(5828, 64, BF16)]
        g_vbf = [g_vbf, sv(5892, 128, BF16)]
        g_yb = [g_yb, sv(6020, 128, BF16)]
        g_ss = [g_ss, sv(6148, 2)]; g_rs = [g_rs, sv(6152, 2)]
        g_sig = sv(6156, 256)
        g_ss8 = sv(6720, 8); g_rs8 = sv(6728, 8)
        g_c = sv(6736, 2); g_cv = sv(6752, 256)
        g_Sbb = [sv(6412 + i * 128, 128, BF16) for i in range(2)]
        g_vsb = sv(2360, 128, BF16)
        g_Qmb = sv(1672, 128, BF16).rearrange("p (r t) -> p r t", r=16)
        g_Kmb = tabc[0:16, 0:1024].bitcast(BF16).rearrange("p (r k) -> p r k", r=16)
        g_Sb1 = [sv(2616 + i * 256, 256) for i in range(4)] + [tabc[:, 1024 + i * 256:1280 + i * 256] for i in range(min(8, (tabc_n - 1024) // 256))]
        NSB = len(g_Sb1)
        stg_in = sv(0, 1408); stg_out = sv(1408, 1408)
        fst = sv(2816, 11 * 32).rearrange("p (j c) -> p j c", c=32)
        nst = sv(3168, 11 * 34).rearrange("p (j c) -> p j c", c=34)

        class WS:
            def __init__(self):
                self.req = []
                self.i = 0
                self.loaded = 0
            def get(self, parts):
                if S.dry:
                    self.req.append(parts)
                    self.i += 1
                    return self.i - 1, wsl[(self.i - 1) % NSLOT]
                u = self.i
                self.i += 1
                return u, wsl[u % NSLOT]
            def load(self, u):
                if u >= len(self.req):
                    return
                slot = wsl[u % NSLOT]
                for (kn, c0, ncol, src) in self.req[u]:
                    S.op("pool", lambda e, o=slot[:, 0:kn, c0:c0 + ncol], s=src: e.dma_start(out=o, in_=s),
                         writes=[("w", u % NSLOT)], dma=True)
                self.loaded = u + 1
            def release(self, u):
                if S.dry:
                    return
                if u + NSLOT == self.loaded:
                    self.load(u + NSLOT)
            def start(self):
                for u in range(min(NSLOT, len(self.req))):
                    self.load(u)
        ws = WS()

        def wunit(src2d, c0, ncols=128, kn=KC):
            return (kn, 0, ncols, src2d[:, c0:c0 + ncols].rearrange("(k p) n -> p k n", p=128))

        def colgroups(ns):
            g = []
            c = 0
            while c < T:
                n = min(512, T - c)
                g.append((c, n))
                c += n
            if ns:
                g.append((T, NX))
            return g

        BX, BY = (0, 1, 2), (3, 4, 5)

        def proj_fm(slot, u, nk, rhs_t, rhs_res, groups, banks, M=128):
            for k in range(nk):
                for gi, (c0, n) in enumerate(groups):
                    S.op("pe", lambda e, o=ps[banks[gi]][0:M, 0:n], l=slot[:, k, 0:M], r=rhs_t[:, k, c0:c0 + n],
                         a=(k == 0), b=(k == nk - 1): e.matmul(o, lhsT=l, rhs=r, start=a, stop=b),
                         reads=[("w", u % NSLOT)] + rhs_res(k), writes=[("ps", banks[gi])], signal=(k == nk - 1))

        def vcol(name, l, i):
            o = cfg.off[(name, l)] + i
            return vec[:, o:o + 1]

        def load_x(p, ns):
            for tb in range(NTB):
                xb, xres = xin[tb % len(xin)]
                S.op("sp", lambda e, r0=p * T + tb * 128, xb=xb: e.dma_start(out=xb, in_=x_p[r0:r0 + 128, :]),
                     writes=xres, dma=True)
                for q in range(4):
                    bk = 4 + ((4 * tb + q) % 4)
                    for k in range(4):
                        dc = 4 * q + k
                        S.op("pe", lambda e, o=ps[bk][:, k * 128:(k + 1) * 128], i=xb[:, dc * 128:(dc + 1) * 128]:
                             e.transpose(out=o, in_=i, identity=ident), reads=xres + ["consts"],
                             writes=[("ps", bk)], signal=(k == 3))
                    eng = "act" if q % 2 == 0 else "dve"
                    o = xT[:, 4 * q:4 * q + 4, tb * 128:(tb + 1) * 128]
                    i = ps[bk][:, :].rearrange("p (k t) -> p k t", k=4)
                    if eng == "act":
                        S.op("act", lambda e, o=o, i=i: e.copy(out=o, in_=i), reads=[("ps", bk)],
                             writes=[("xT", 4 * q + k) for k in range(4)])
                    else:
                        S.op("dve", lambda e, o=o, i=i: e.tensor_copy(out=o, in_=i), reads=[("ps", bk)],
                             writes=[("xT", 4 * q + k) for k in range(4)])
            if ns:
                S.op("sp", lambda e: e.dma_start(out=io[0:NS, :], in_=x_s), writes=["tA", "tB"], dma=True)
                S.op("sp", lambda e: e.dma_start(out=io[NS:NX, :], in_=x_h), writes=["tA", "tB"], dma=True)
                for dc in range(KC):
                    S.op("pe", lambda e, o=ps[6][:, dc * NX:(dc + 1) * NX], i=io[0:NX, dc * 128:(dc + 1) * 128]:
                         e.transpose(out=o, in_=i, identity=ident[0:NX, 0:NX]), reads=["tA", "tB", "consts"],
                         writes=[("ps", 6)], signal=(dc == KC - 1))
                S.op("dve", lambda e: e.tensor_copy(out=xT[:, :, T:T + NX],
                                                     in_=ps[6][:, 0:KC * NX].rearrange("p (k t) -> p k t", k=KC)),
                     reads=[("ps", 6)], writes=[("xT", k) for k in range(KC)])

        def rms_stats(groups, TTp):
            for dc in range(KC):
                sq = sqv[:, (dc % 2) * EW:(dc % 2) * EW + TTp]
                S.op("act", lambda e, o=sq, i=xT[:, dc, 0:TTp]: e.activation(out=o, in_=i, func=AF.Square),
                     reads=[("xT", dc)], writes=[("sq", dc % 2)])
                for gi, (c0, n) in enumerate(groups):
                    S.op("pe", lambda e, o=ps[BX[gi]][:, 0:n], r=sq[:, c0:c0 + n], a=(dc == 0), b=(dc == KC - 1):
                         e.matmul(o, lhsT=onesb[:], rhs=r, start=a, stop=b),
                         reads=[("sq", dc % 2), "onesb"], writes=[("ps", BX[gi])],
                         signal=(gi == len(groups) - 1))
            for gi, (c0, n) in enumerate(groups):
                S.op("act", lambda e, o=tC[:, c0:c0 + n], i=ps[BX[gi]][:, 0:n]:
                     e.activation(out=o, in_=i, func=AF.Ln, scale=1.0 / D, bias=EPS),
                     reads=[("ps", BX[gi])], writes=["tC"])
            S.op("act", lambda e: e.activation(out=tC[:, 0:TTp], in_=tC[:, 0:TTp], func=AF.Exp, scale=-0.5),
                 reads=["tC"], writes=["tC"])

        def rms_norm(gname, l, groups, TTp):
            rms_stats(groups, TTp)
            for dc in range(KC):
                S.op("dve", lambda e, o=xnT[:, dc, 0:TTp], i=xT[:, dc, 0:TTp], g=vcol(gname, l, dc):
                     e.scalar_tensor_tensor(out=o, in0=i, scalar=g, in1=tC[:, 0:TTp], op0=ALU.mult, op1=ALU.mult),
                     reads=[("xT", dc), "tC", "vec"], writes=["xnT"])

        def xn_res(k):
            return ["xnT"]

        def conv_state_in(l):
            for r in range(2):
                S.op("sp", lambda e, r=r: e.dma_start(out=stg_in[r * NS:(r + 1) * NS, 0:CD], in_=st_conv[l, :, r, :]),
                     writes=["stg_in"], dma=True)
            for c in range(CC):
                S.op("pe", lambda e, c=c: e.transpose(out=ps[7][:, c * 32:(c + 1) * 32],
                                                      in_=stg_in[0:32, c * 128:(c + 1) * 128], identity=ident[0:32, 0:32]),
                     reads=["stg_in", "consts"], writes=[("ps", 7)], signal=(c == CC - 1))
            S.op("dve", lambda e: e.tensor_copy(out=fst[:, 0:CC, :], in_=ps[7][:, 0:CC * 32].rearrange("p (j c) -> p j c", c=32)),
                 reads=[("ps", 7)], writes=["fst"])

        def state_out(nchunks, ns, dst_p, dst_s0, dst_s1):
            W = 2 + 2 * ns
            nb = (nchunks + 3) // 4
            for j in range(nchunks):
                bk = [5, 6, 7][j // 4]
                S.op("pe", lambda e, j=j, bk=bk: e.transpose(out=ps[bk][0:W, (j % 4) * 128:(j % 4 + 1) * 128],
                                                               in_=nst[:, j, 0:W], identity=ident),
                     reads=["nst", "consts"], writes=[("ps", bk)], signal=(j % 4 == 3 or j == nchunks - 1))
            for b in range(nb):
                bk = [5, 6, 7][b]
                n = min(4, nchunks - 4 * b) * 128
                S.op("act", lambda e, b=b, bk=bk, n=n: e.copy(out=stg_out[0:W, b * 512:b * 512 + n], in_=ps[bk][0:W, 0:n]),
                     reads=[("ps", bk)], writes=["stg_out"])
            S.op("sp", lambda e: e.dma_start(out=dst_p, in_=stg_out[0:2, 0:nchunks * 128]), reads=["stg_out"], dma=True)
            if ns:
                S.op("sp", lambda e: e.dma_start(out=dst_s0, in_=stg_out[2:2 + ns, 0:nchunks * 128]), reads=["stg_out"], dma=True)
                S.op("sp", lambda e: e.dma_start(out=dst_s1, in_=stg_out[2 + ns:2 + 2 * ns, 0:nchunks * 128]),
                     reads=["stg_out"], dma=True)

        def mixer_conv(p, l, ns, groups, TTp):
            win = w_in[l]
            if ns:
                conv_state_in(l)
            u, slot = ws.get([(KC, 0, GR, win[:, S7:S7 + GR].rearrange("(k p) n -> p k n", p=128))])
            proj_fm(slot, u, KC, xnT, xn_res, groups, BY, M=GR)
            ws.release(u)
            for gi, (c0, n) in enumerate(groups):
                S.op("act", lambda e, o=glrT[0:GR, c0:c0 + n], i=ps[BY[gi]][0:GR, 0:n]: e.copy(out=o, in_=i),
                     reads=[("ps", BY[gi])], writes=["glrT"])
            for c in range(CC):
                u, slot = ws.get([wunit(win, S1 + c * 128)])
                proj_fm(slot, u, KC, xnT, xn_res, groups, BX)
                ws.release(u)
                for gi, (c0, n) in enumerate(groups):
                    S.op("act", lambda e, o=tA[:, c0:c0 + n], i=ps[BX[gi]][:, 0:n]: e.copy(out=o, in_=i),
                         reads=[("ps", BX[gi])], writes=["tA"])
                u, slot = ws.get([wunit(win, S2 + c * 128)])
                proj_fm(slot, u, KC, xnT, xn_res, groups, BY)
                ws.release(u)
                for gi, (c0, n) in enumerate(groups):
                    S.op("dve", lambda e, o=tB[:, 2 + c0:2 + c0 + n], a=tA[:, c0:c0 + n], i=ps[BY[gi]][:, 0:n]:
                         e.tensor_tensor(out=o, in0=a, in1=i, op=ALU.mult),
                         reads=[("ps", BY[gi]), "tA"], writes=["tB"])
                S.op("dve", lambda e: e.tensor_copy(out=tB[:, 0:2], in_=tB[:, 2 + T + NS:2 + T + NX]),
                     reads=["tB"], writes=["tB"])
                w0, w1, w2 = vcol("cw", l, c), vcol("cw", l, 8 + c), vcol("cw", l, 16 + c)
                S.op("act", lambda e, w0=w0: e.activation(out=tC[:, 0:T], in_=tB[:, 0:T], func=AF.Copy, scale=w0),
                     reads=["tB", "vec"], writes=["tC"])
                S.op("dve", lambda e, w1=w1: e.scalar_tensor_tensor(out=tC[:, 0:T], in0=tB[:, 1:1 + T], scalar=w1,
                                                                     in1=tC[:, 0:T], op0=ALU.mult, op1=ALU.add),
                     reads=["tB", "tC", "vec"], writes=["tC"])
                S.op("dve", lambda e, w2=w2: e.scalar_tensor_tensor(out=tC[:, 0:T], in0=tB[:, 2:2 + T], scalar=w2,
                                                                     in1=tC[:, 0:T], op0=ALU.mult, op1=ALU.add),
                     reads=["tB", "tC", "vec"], writes=["tC"])
                if ns:
                    S.op("act", lambda e, w0=w0, c=c: e.activation(out=tC[:, T:T + ns], in_=fst[:, c, 0:ns], func=AF.Copy, scale=w0),
                         reads=["fst", "vec", "tC"], writes=["tC"])
                    S.op("dve", lambda e, w1=w1, c=c: e.scalar_tensor_tensor(out=tC[:, T:T + ns], in0=fst[:, c, ns:2 * ns], scalar=w1,
                                                                              in1=tC[:, T:T + ns], op0=ALU.mult, op1=ALU.add),
                         reads=["fst", "tC", "vec"], writes=["tC"])
                    S.op("dve", lambda e, w2=w2: e.scalar_tensor_tensor(out=tC[:, T:T + ns], in0=tB[:, 2 + T:2 + T + ns], scalar=w2,
                                                                         in1=tC[:, T:T + ns], op0=ALU.mult, op1=ALU.add),
                         reads=["tB", "tC", "vec"], writes=["tC"])
                S.op("dve", lambda e, c=c: e.tensor_copy(out=nst[:, c, 0:2], in_=tB[:, T:T + 2]), reads=["tB"], writes=["nst"])
                if ns:
                    S.op("dve", lambda e, c=c: e.tensor_copy(out=nst[:, c, 2:2 + ns], in_=fst[:, c, ns:2 * ns]),
                         reads=["fst"], writes=["nst"])
                    S.op("dve", lambda e, c=c: e.tensor_copy(out=nst[:, c, 2 + ns:2 + 2 * ns], in_=tB[:, 2 + T:2 + T + ns]),
                         reads=["tB"], writes=["nst"])
                u, slot = ws.get([wunit(win, c * 128)])
                proj_fm(slot, u, KC, xnT, xn_res, groups, BX)
                ws.release(u)
                for gi, (c0, n) in enumerate(groups):
                    S.op("dve", lambda e, o=yT[:, c, c0:c0 + n], a=tC[:, c0:c0 + n], i=ps[BX[gi]][:, 0:n]:
                         e.tensor_tensor(out=o, in0=a, in1=i, op=ALU.mult),
                         reads=[("ps", BX[gi]), "tC"], writes=[("yT", c)])
            state_out(CC, ns, conv_p[p, l], conv_s[l, :, 0, :], conv_s[l, :, 1, :])

        def gla_post(h, l, M, c0, o_ap, obank, sg_ap, par, tbank, ores=None):
            ores = ores or ("ps", obank)
            ss, rs, yb = g_ss[par], g_rs[par], g_yb[par]
            tps = psb[tbank]
            S.op("dve", lambda e: e.memset(ss[0:M, 0:2], 0.0), writes=[("g_ss", par)])
            S.op("act", lambda e: e.activation(out=g_junk[0:M, :], in_=o_ap, func=AF.Square, accum_out=ss[0:M, 0:1]),
                 reads=[ores], writes=["g_junk", ("g_ss", par)])
            S.op("act", lambda e: e.activation(out=rs[0:M, 0:1], in_=ss[0:M, 0:1], func=AF.Ln, scale=1.0 / DV, bias=EPS),
                 reads=[("g_ss", par)], writes=[("g_rs", par)])
            S.op("act", lambda e: e.activation(out=rs[0:M, 0:1], in_=rs[0:M, 0:1], func=AF.Exp, scale=-0.5),
                 reads=[("g_rs", par)], writes=[("g_rs", par)])
            S.op("dve", lambda e: e.scalar_tensor_tensor(out=yb[0:M, :], in0=o_ap, scalar=rs[0:M, 0:1],
                                                          in1=sg_ap, op0=ALU.mult, op1=ALU.mult),
                 reads=[ores, ("g_rs", par), "g_sg", "g_sgs"], writes=[("g_yb", par)])
            for vc in range(2):
                S.op("pe", lambda e, vc=vc: e.transpose(out=tps[:, 256 + vc * 128:256 + vc * 128 + M],
                                                        in_=yb[0:M, vc * 128:(vc + 1) * 128], identity=identb[0:M, 0:M]),
                     reads=[("g_yb", par), "identb"], writes=[("ps", tbank)], signal=(vc == 1))
            for vc in range(2):
                ch = CC + 2 * h + vc
                S.op("dve", lambda e, vc=vc, ch=ch: e.tensor_scalar_mul(yT[:, ch, c0:c0 + M], tps[:, 256 + vc * 128:256 + vc * 128 + M],
                                                                         vcol("gng", l, 2 * h + vc)),
                     reads=[("ps", tbank), "vec"], writes=[("yT", ch)])

        def o_store(tb):
            return 4 + tb // 2, ps[4 + tb // 2][:, (tb % 2) * DV:(tb % 2 + 1) * DV]

        def mixer_gla(p, l, ns):
            for h in range(H):
                gla_head(p, l, ns, h)
            S.op("sp", lambda e: e.dma_start(out=gla_p[p, l].rearrange("h k v -> k h v"), in_=Sst[:]),
                 reads=[("S", hh) for hh in range(H)], dma=True)

        def gla_head(p, l, ns, h):
            win = w_in[l]
            sc = 1.0 / 16.0
            dst = [(0, 0), (0, 128), (1, 0), (1, 128), (1, 256), (1, 384)]
            if True:
                cols = (S3 + h * DK, S4 + h * DK, S5 + h * DV, S5 + h * DV + 128, S6 + h * DV, S6 + h * DV + 128)
                units = [ws.get([wunit(win, c0)]) for c0 in cols]

                def P(i, M, c0, last=False):
                    (u, slot), (bk, co) = units[i], dst[i]
                    for k in range(KC):
                        S.op("pe", lambda e, o=ps[bk][0:M, co:co + 128], lt=xnT[:, k, c0:c0 + M], r=slot[:, k, :],
                             a=(k == 0), b=(k == KC - 1): e.matmul(o, lhsT=lt, rhs=r, start=a, stop=b),
                             reads=[("w", u % NSLOT), "xnT"], writes=[("ps", bk)], signal=(k == KC - 1))
                    if last:
                        ws.release(u)

                def Z(M, c0):
                    S.op("pe", lambda e: e.matmul(ps[2][0:M, 0:128], lhsT=glrT[0:GR + 1, c0:c0 + M],
                                                  rhs=gw[0:GR + 1, l, h * DK:(h + 1) * DK], start=True, stop=True),
                         reads=["glrT", "gw"], writes=[("ps", 2)])
                    S.op("act", lambda e: e.activation(out=g_e[0:M, :], in_=ps[2][0:M, 0:128], func=AF.Exp, scale=-1.0),
                         reads=[("ps", 2)], writes=["g_e"])
                    S.op("act", lambda e: e.activation(out=g_lf[0:M, :], in_=g_e[0:M, :], func=AF.Ln, bias=1.0),
                         reads=["g_e"], writes=["g_lf"])

                def cumsum(n):
                    S.op("pe", lambda e: e.matmul(ps[2][:, 128:256], lhsT=Uc, rhs=g_lf, start=True, stop=True),
                         reads=["g_lf", "consts"], writes=[("ps", 2)], signal=False)
                    S.op("pe", lambda e: e.matmul(ps[2][:, 256:384], lhsT=Lm, rhs=g_lf, start=True, stop=True),
                         reads=["g_lf", "consts"], writes=[("ps", 2)], signal=False)
                    S.op("pe", lambda e: e.matmul(ps[2][:, 384:386], lhsT=g_lf, rhs=cmat, start=True, stop=True),
                         reads=["g_lf", "consts"], writes=[("ps", 2)])

                def gates(n):
                    par = n % 2
                    S.op("act", lambda e: e.activation(out=g_eb, in_=ps[2][:, 128:256], func=AF.Exp, scale=-sc), reads=[("ps", 2)], writes=["g_eb"])
                    S.op("act", lambda e: e.activation(out=g_enb, in_=ps[2][:, 128:256], func=AF.Exp, scale=sc), reads=[("ps", 2)], writes=["g_enb"])
                    S.op("act", lambda e: e.activation(out=g_erev, in_=ps[2][:, 256:384], func=AF.Exp, scale=-sc), reads=[("ps", 2)], writes=["g_erev"])
                    S.op("act", lambda e: e.activation(out=g_dec[par], in_=ps[2][:, 384:386], func=AF.Exp, scale=-sc),
                         reads=[("ps", 2)], writes=[("g_dec", par)])
                    S.op("act", lambda e: e.activation(out=g_ecol[par], in_=ps[2][:, 385:386], func=AF.Exp, scale=-sc, bias=g_Bsn),
                         reads=[("ps", 2), "g_Bsn"], writes=[("g_ecol", par)])
                    S.op("dve", lambda e: e.scalar_tensor_tensor(out=g_Bsn, in0=ps[2][:, 384:385], scalar=-sc, in1=g_Bsn, op0=ALU.mult, op1=ALU.add),
                         reads=[("ps", 2), "g_Bsn"], writes=["g_Bsn"])

                def qkd(n):
                    par = n % 2
                    S.op("dve", lambda e: e.scalar_tensor_tensor(out=g_qd[par], in0=ps[0][:, 0:128], scalar=float(DK) ** -0.5, in1=g_eb,
                                                                  op0=ALU.mult, op1=ALU.mult), reads=[("ps", 0), "g_eb"], writes=[("g_qd", par)])
                    S.op("dve", lambda e: e.tensor_tensor(out=g_kd[par], in0=ps[0][:, 128:256], in1=g_enb, op=ALU.mult),
                         reads=[("ps", 0), "g_enb"], writes=[("g_kd", par)])
                    S.op("dve", lambda e: e.tensor_tensor(out=g_kdec[par], in0=ps[0][:, 128:256], in1=g_erev, op=ALU.mult),
                         reads=[("ps", 0), "g_erev"], writes=[("g_kdec", par)])

                def vg(n):
                    par = n % 2
                    S.op("act", lambda e: e.copy(out=g_vbf[par], in_=ps[1][:, 0:DV]), reads=[("ps", 1)], writes=[("g_vbf", par)])
                    S.op("act", lambda e: e.activation(out=g_sig, in_=ps[1][:, 256:512], func=AF.Exp, scale=-1.0), reads=[("ps", 1)], writes=["g_sig"])
                    S.op("act", lambda e: e.activation(out=g_sig, in_=g_sig, func=AF.Ln, bias=1.0), reads=["g_sig"], writes=["g_sig"])
                    S.op("act", lambda e: e.activation(out=g_sig, in_=g_sig, func=AF.Exp, scale=-1.0), reads=["g_sig"], writes=["g_sig"])
                    S.op("dve", lambda e: e.tensor_tensor(out=g_sgs[:, n, :], in0=ps[1][:, 256:512], in1=g_sig, op=ALU.mult),
                         reads=[("ps", 1), "g_sig"], writes=["g_sgs"])

                def Sbf(n):
                    par = n % 2
                    S.op("dve", lambda e: e.tensor_scalar_mul(g_Sbf, Sst[:, h, :], g_dec[par][:, 1:2]),
                         reads=[("S", h), ("g_dec", par)], writes=["g_Sbf"])

                def T1(tb):
                    par = tb % 2
                    S.op("pe", lambda e: e.transpose(out=psb[3][:, 0:128], in_=g_qd[par], identity=identb[:]), reads=[("g_qd", par), "identb"],
                         writes=[("ps", 3)], signal=False)
                    S.op("pe", lambda e: e.transpose(out=psb[3][:, 128:256], in_=g_kd[par], identity=identb[:]), reads=[("g_kd", par), "identb"],
                         writes=[("ps", 3)])
                    S.op("dve", lambda e: e.tensor_copy(out=g_qkT, in_=psb[3][:, 0:256]), reads=[("ps", 3)], writes=["g_qkT"])
                    S.op("dve", lambda e: e.tensor_scalar_mul(g_qeT[:, tb, :], psb[3][:, 0:128], g_ecol[par][:, 0:1]),
                         reads=[("ps", 3), ("g_ecol", par)], writes=["g_qeT"])

                def T2(tb):
                    S.op("pe", lambda e: e.matmul(ps[0][:, 256:384], lhsT=g_qkT[:, 128:256], rhs=g_qkT[:, 0:128], start=True, stop=True),
                         reads=["g_qkT"], writes=[("ps", 0)])
                    S.op("dve", lambda e: e.tensor_tensor(out=g_ATm, in0=ps[0][:, 256:384], in1=maskT, op=ALU.mult),
                         reads=[("ps", 0), "consts"], writes=["g_ATm"])

                def T3(tb):
                    par = tb % 2
                    ob, o_ap = o_store(tb)
                    S.op("pe", lambda e: e.matmul(o_ap, lhsT=g_ATm, rhs=g_vbf[par], start=False, stop=False, skip_group_check=True),
                         reads=["g_ATm", ("g_vbf", par)], writes=[("ps", ob)], signal=False)
                    S.op("pe", lambda e: e.matmul(o_ap, lhsT=g_qkT[:, 0:128], rhs=g_Sbf, start=False, stop=True, skip_group_check=True),
                         reads=["g_qkT", "g_Sbf"], writes=[("ps", ob)])
                    S.op("pe", lambda e: e.matmul(ps[3][:, 256:512], lhsT=g_kdec[par], rhs=g_vbf[par], start=True, stop=True),
                         reads=[("g_kdec", par), ("g_vbf", par)], writes=[("ps", 3)])
                    S.op("dve", lambda e: e.scalar_tensor_tensor(out=Sst[:, h, :], in0=Sst[:, h, :], scalar=g_dec[par][:, 0:1], in1=ps[3][:, 256:512],
                                                                  op0=ALU.mult, op1=ALU.add),
                         reads=[("ps", 3), ("g_dec", par), ("S", h), "g_Sbf"], writes=[("S", h)])

                S.op("dve", lambda e: e.memset(Sst[:, h, :], 0.0), writes=[("S", h)])
                S.op("dve", lambda e: e.memset(g_Bsn, 0.0), writes=["g_Bsn"])
                one_tile = (NTB == 1)
                def ld(r):
                    S.op("sp", lambda e: e.dma_start(out=g_Sb1[r % NSB], in_=st_gla[l, r, h]), writes=[("g_Sb", r % NSB)], dma=True)

                if ns:
                    M = NS
                    for r in range(min(NSB, M)):
                        ld(r)
                    Z(M, T)
                    for i in range(6):
                        P(i, M, T)
                    S.op("act", lambda e: e.activation(out=g_as[0:M, :], in_=g_lf[0:M, :], func=AF.Exp, scale=-sc), reads=["g_lf"], writes=["g_as"])
                    S.op("dve", lambda e: e.tensor_scalar_mul(g_qs[0:M, :], ps[0][0:M, 0:128], float(DK) ** -0.5),
                         reads=[("ps", 0)], writes=["g_qs"])
                    S.op("dve", lambda e: e.tensor_copy(out=g_ks[0:M, :], in_=ps[0][0:M, 128:256]), reads=[("ps", 0)], writes=["g_ks"])
                    S.op("act", lambda e: e.copy(out=g_vsb[0:M, :], in_=ps[1][0:M, 0:DV]), reads=[("ps", 1)], writes=["g_vs"])
                    S.op("act", lambda e: e.activation(out=g_sig[0:M, :], in_=ps[1][0:M, 256:512], func=AF.Exp, scale=-1.0), reads=[("ps", 1)], writes=["g_sig"])
                    S.op("act", lambda e: e.activation(out=g_sig[0:M, :], in_=g_sig[0:M, :], func=AF.Ln, bias=1.0), reads=["g_sig"], writes=["g_sig"])
                    S.op("act", lambda e: e.activation(out=g_sig[0:M, :], in_=g_sig[0:M, :], func=AF.Exp, scale=-1.0), reads=["g_sig"], writes=["g_sig"])
                    S.op("dve", lambda e: e.tensor_tensor(out=g_sg[0:M, :], in0=ps[1][0:M, 256:512], in1=g_sig[0:M, :], op=ALU.mult),
                         reads=[("ps", 1), "g_sig"], writes=["g_sg"])
                    S.op("dve", lambda e: e.tensor_tensor(out=g_e[0:M, :], in0=g_qs[0:M, :], in1=g_ks[0:M, :], op=ALU.mult),
                         reads=["g_qs", "g_ks", "g_lf"], writes=["g_e"])
                    S.op("dve", lambda e: e.reduce_sum(out=g_c[0:M, 0:1], in_=g_e[0:M, :], axis=mybir.AxisListType.X),
                         reads=["g_e"], writes=["g_c"])
                    S.op("dve", lambda e: e.tensor_scalar_mul(g_cv[0:M, :], ps[1][0:M, 0:DV], g_c[0:M, 0:1]),
                         reads=[("ps", 1), "g_c"], writes=["g_cv"])
                    S.op("dve", lambda e: e.tensor_tensor(out=g_qs[0:M, :], in0=g_qs[0:M, :], in1=g_as[0:M, :], op=ALU.mult),
                         reads=["g_qs", "g_as", "g_e"], writes=["g_qs"])
                    for i, src in enumerate((g_as, g_ks, g_qs)):
                        S.op("pe", lambda e, i=i, src=src: e.transpose(out=ps[2][:, 128 + i * M:128 + (i + 1) * M], in_=src[0:M, :],
                                                                       identity=ident[0:M, 0:M]),
                             reads=["g_as", "g_ks", "g_qs", "consts"], writes=[("ps", 2)], signal=(i == 2))
                    S.op("dve", lambda e: e.tensor_copy(out=g_akq[:, 0:3 * M], in_=ps[2][:, 128:128 + 3 * M]), reads=[("ps", 2)], writes=["g_akq"])
                    S.op("dve", lambda e: e.tensor_tensor(out=g_Qmb, in0=g_akq[:, None, 2 * M:3 * M].broadcast_to([128, M, M]), in1=idb16, op=ALU.mult),
                         reads=["g_akq", "consts"], writes=["g_Qm"])
                    S.op("dve", lambda e: e.tensor_tensor(out=g_Kmb, in0=g_ks[0:M, None, :].broadcast_to([M, M, 128]),
                                                           in1=ident[0:M, 0:M, None].broadcast_to([M, M, 128]), op=ALU.mult),
                         reads=["g_ks", "consts"], writes=["tA", "tB"])
                for b in range(4, 8):
                    S.op("dve", lambda e, b=b: e.memset(ps[b][:, :], 0.0), writes=[("ps", b)])
                Z(128, 0)
                P(0, 128, 0, one_tile)
                cumsum(0); gates(0)
                P(1, 128, 0, one_tile); P(2, 128, 0, one_tile); P(3, 128, 0, one_tile)
                qkd(0)
                P(4, 128, 0, one_tile); P(5, 128, 0, one_tile)
                vg(0); Sbf(0)
                for tb in range(NTB):
                    n = tb + 1
                    has = n < NTB
                    last = (n == NTB - 1)
                    c0 = n * 128
                    if has:
                        Z(128, c0)
                        P(0, 128, c0, last)
                    T1(tb)
                    if has:
                        cumsum(n); gates(n)
                        P(1, 128, c0, last)
                    T2(tb)
                    if has:
                        P(2, 128, c0, last); P(3, 128, c0, last)
                    T3(tb)
                    if has:
                        qkd(n)
                        P(4, 128, c0, last); P(5, 128, c0, last)
                        vg(n); Sbf(n)
                S.op("act", lambda e: e.activation(out=g_Etot, in_=g_Bsn, func=AF.Exp), reads=["g_Bsn"], writes=["g_Etot"])
                S.op("sp", lambda e, h=h: e.dma_start(out=ibS[l][h].ap(), in_=Sst[:, h, :]), reads=[("S", h)], writes=[("ibS", l, h)], dma=True)
                S.op("pool", lambda e, h=h: e.collective_compute("AllGather", ALU.bypass, replica_groups=PAIRS,
                                                                  ins=[ibS[l][h].ap().opt()], outs=[obS[l][h].ap().opt()]),
                     reads=[("ibS", l, h)], writes=[("obS", l, h)], cc=ccsems[l * H + h])
                if ns:
                    M = NS

                    def dS(r):
                        bk = 3 if r % 2 == 0 else 0
                        S.op("pe", lambda e: e.matmul(ps[bk][:, 256:512], lhsT=g_Kmb[:, r, :], rhs=g_vsb[0:M, :], start=True, stop=True),
                             reads=["tA", "tB", "g_vs"], writes=[("ps", bk)])

                    def wb(r):
                        S.op("act", lambda e: e.dma_start(out=gla_s[l, r, h], in_=g_Sb1[r % NSB]), reads=[("g_Sb", r % NSB)], dma=True)
                        if r + NSB < M:
                            ld(r + NSB)

                    dS(0)
                    for r in range(M):
                        bk = 3 if r % 2 == 0 else 0
                        buf = g_Sb1[r % NSB]
                        bres = ("g_Sb", r % NSB)
                        S.op("act", lambda e, buf=buf, r=r: e.copy(out=g_Sbb[r % 2], in_=buf), reads=[bres], writes=[("g_Sbb", r % 2)])
                        S.op("dve", lambda e, buf=buf, bk=bk, r=r: e.scalar_tensor_tensor(
                            out=buf, in0=buf, scalar=g_akq[:, r:r + 1], in1=ps[bk][:, 256:512], op0=ALU.mult, op1=ALU.add),
                            reads=[("ps", bk), "g_akq", bres], writes=[bres])
                        if r + 1 < M:
                            dS(r + 1)
                        S.op("pe", lambda e, r=r: e.matmul(ps[2][0:M, 256:512], lhsT=g_Qmb[:, r, :], rhs=g_Sbb[r % 2],
                                                           start=(r == 0), stop=(r == M - 1)),
                             reads=["g_Qm", ("g_Sbb", r % 2)], writes=[("ps", 2)])
                        if r >= 1:
                            wb(r - 1)
                        if r == max(0, M - NSB):
                            S.op("sp", lambda e: e.dma_start(out=g_SA, in_=obS[l][h].ap()[0:128, :]), reads=[("obS", l, h)],
                                 writes=["g_SA"], dma=True)
                    wb(M - 1)
                    S.op("dve", lambda e: e.tensor_tensor(out=g_cv[0:M, :], in0=g_cv[0:M, :], in1=ps[2][0:M, 256:512], op=ALU.add),
                         reads=[("ps", 2), "g_cv"], writes=["g_cv"])
                    gla_post(h, l, M, T, g_cv[0:M, :], 2, g_sg[0:M, :], 0, 3, ores="g_cv")
                if not ns:
                    S.op("sp", lambda e, h=h: e.dma_start(out=g_SA, in_=obS[l][h].ap()[0:128, :]), reads=[("obS", l, h)], writes=["g_SA"], dma=True)
                S.op("dve", lambda e: e.tensor_scalar_mul(g_SA, g_SA, flag[:, 0:1]), reads=["g_SA", "flag"], writes=["g_SA"])
                S.op("dve", lambda e: e.tensor_copy(out=g_SAb, in_=g_SA), reads=["g_SA"], writes=["g_SAb"])
                for tb in range(NTB):
                    ob, o_ap = o_store(tb)
                    S.op("pe", lambda e, tb=tb, o_ap=o_ap: e.matmul(o_ap, lhsT=g_qeT[:, tb, :], rhs=g_SAb, start=False, stop=True, skip_group_check=True),
                         reads=["g_qeT", "g_SAb"], writes=[("ps", ob)])
                S.op("dve", lambda e: e.memset(g_ss8[:, 0:NTB], 0.0), writes=["g_ss8"])
                for tb in range(NTB):
                    ob, o_ap = o_store(tb)
                    S.op("act", lambda e, tb=tb, o_ap=o_ap: e.activation(out=g_junk, in_=o_ap, func=AF.Square, accum_out=g_ss8[:, tb:tb + 1]),
                         reads=[("ps", ob), "g_ss8"], writes=["g_junk", ("g_ss8", tb)])
                    S.op("act", lambda e, tb=tb: e.activation(out=g_rs8[:, tb:tb + 1], in_=g_ss8[:, tb:tb + 1], func=AF.Ln, scale=1.0 / DV, bias=EPS),
                         reads=[("g_ss8", tb)], writes=[("g_rs8", tb)])
                    S.op("act", lambda e, tb=tb: e.activation(out=g_rs8[:, tb:tb + 1], in_=g_rs8[:, tb:tb + 1], func=AF.Exp, scale=-0.5),
                         reads=[("g_rs8", tb)], writes=[("g_rs8", tb)])

                def evac(tb):
                    tbank = 3 if tb % 2 == 0 else 0
                    for vc in range(2):
                        ch = CC + 2 * h + vc
                        S.op("dve", lambda e, vc=vc, ch=ch: e.tensor_scalar_mul(yT[:, ch, tb * 128:(tb + 1) * 128],
                                                                                 psb[tbank][:, 256 + vc * 128:384 + vc * 128],
                                                                                 vcol("gng", l, 2 * h + vc)),
                             reads=[("ps", tbank), "vec"], writes=[("yT", ch)])

                for tb in range(NTB):
                    ob, o_ap = o_store(tb)
                    par = tb % 2
                    tbank = 3 if par == 0 else 0
                    S.op("dve", lambda e, tb=tb, o_ap=o_ap, par=par: e.scalar_tensor_tensor(
                        out=g_yb[par], in0=o_ap, scalar=g_rs8[:, tb:tb + 1], in1=g_sgs[:, tb, :], op0=ALU.mult, op1=ALU.mult),
                        reads=[("ps", ob), ("g_rs8", tb), "g_sgs"], writes=[("g_yb", par)])
                    for vc in range(2):
                        S.op("pe", lambda e, vc=vc, par=par, tbank=tbank: e.transpose(out=psb[tbank][:, 256 + vc * 128:384 + vc * 128],
                                                                                     in_=g_yb[par][:, vc * 128:(vc + 1) * 128], identity=identb[:]),
                             reads=[("g_yb", par), "identb"], writes=[("ps", tbank)], signal=(vc == 1))
                    if tb >= 1:
                        evac(tb - 1)
                evac(NTB - 1)
                S.op("dve", lambda e, h=h: e.scalar_tensor_tensor(out=Sst[:, h, :], in0=g_SA, scalar=g_Etot[:, 0:1], in1=Sst[:, h, :],
                                                                   op0=ALU.mult, op1=ALU.add),
                     reads=["g_SA", "g_Etot", ("S", h)], writes=[("S", h)])
        def xchg_x(idx):
            S.op("dve", lambda e: e.tensor_copy(out=xhs[:], in_=xT[:, :, T - 2:T]), reads=[("xT", k) for k in range(KC)], writes=["xhs"])
            S.op("sp", lambda e: e.dma_start(out=ibX[idx].ap(), in_=xhs[:].rearrange("p k t -> p (k t)")), reads=["xhs"],
                 writes=[("ibX", idx)], dma=True)
            S.op("pool", lambda e: e.collective_compute("AllGather", ALU.bypass, replica_groups=PAIRS,
                                                         ins=[ibX[idx].ap().opt()], outs=[obX[idx].ap().opt()]),
                 reads=[("ibX", idx)], writes=[("obX", idx)], cc=ccsems[L * H + idx])
            S.op("sp", lambda e: e.dma_start(out=xhr[:].rearrange("p k t -> p (k t)"), in_=obX[idx].ap()[0:128, :]), reads=[("obX", idx)],
                 writes=["xhr"], dma=True)
            S.op("dve", lambda e: e.tensor_scalar_mul(xT[:, :, T + NS:T + NX], xhr[:], flag[:, 0:1]), reads=["xhr", "flag"],
                 writes=[("xT", k) for k in range(KC)])

        def out_proj(l, groups):
            for oc in range(KC):
                u, slot = ws.get([wunit(w_out[l], oc * 128)])
                bk = BX if oc % 2 == 0 else BY
                proj_fm(slot, u, KC, yT, lambda k: [("yT", k)], groups, bk)
                ws.release(u)
                for gi, (c0, n) in enumerate(groups):
                    S.op("dve", lambda e, o=xT[:, oc, c0:c0 + n], i=ps[bk[gi]][:, 0:n]: e.tensor_tensor(out=o, in0=o, in1=i, op=ALU.add),
                         reads=[("ps", bk[gi])], writes=[("xT", oc)])

        def ffn(p, l, ns, groups, TTp):
            wup, wdn = w_up[l], w_down[l]
            for g in range(NG):
                if ns:
                    for r in range(2):
                        S.op("sp", lambda e, r=r, g=g: e.dma_start(out=stg_in[r * NS:(r + 1) * NS, 0:FG * 128],
                                                                    in_=st_ffn[l, :, r, g * FG * 128:(g + 1) * FG * 128]),
                             writes=["stg_in"], dma=True)
                    for j in range(FG):
                        S.op("pe", lambda e, j=j: e.transpose(out=ps[7][:, j * 32:(j + 1) * 32], in_=stg_in[0:32, j * 128:(j + 1) * 128],
                                                              identity=ident[0:32, 0:32]),
                             reads=["stg_in", "consts"], writes=[("ps", 7)], signal=(j == FG - 1))
                    S.op("dve", lambda e: e.tensor_copy(out=fst[:, 0:FG, :], in_=ps[7][:, 0:FG * 32].rearrange("p (j c) -> p j c", c=32)),
                         reads=[("ps", 7)], writes=["fst"])
                for j in range(FG):
                    fc = g * FG + j
                    u, slot = ws.get([wunit(wup, fc * 128)])
                    proj_fm(slot, u, KC, xnT, xn_res, groups, BX)
                    ws.release(u)
                    for gi, (c0, n) in enumerate(groups):
                        S.op("act", lambda e, o=tA[:, 2 + c0:2 + c0 + n], i=ps[BX[gi]][:, 0:n]: e.copy(out=o, in_=i),
                             reads=[("ps", BX[gi])], writes=["tA"])
                    S.op("act", lambda e: e.copy(out=tA[:, 0:2], in_=tA[:, 2 + T + NS:2 + T + NX]), reads=["tA"], writes=["tA"])
                    w0, w1, w2 = vcol("fcw", l, fc), vcol("fcw", l, FC + fc), vcol("fcw", l, 2 * FC + fc)
                    S.op("act", lambda e, w2=w2, fc=fc: e.activation(out=tB[:, 0:TTp], in_=tA[:, 2:2 + TTp], func=AF.Identity,
                                                                     scale=w2, bias=vcol("fcb", l, fc)),
                         reads=["tA", "vec"], writes=["tB"])
                    S.op("dve", lambda e, w1=w1: e.scalar_tensor_tensor(out=tB[:, 0:T], in0=tA[:, 1:1 + T], scalar=w1, in1=tB[:, 0:T],
                                                                         op0=ALU.mult, op1=ALU.add), reads=["tA", "tB", "vec"], writes=["tB"])
                    S.op("dve", lambda e, w0=w0: e.scalar_tensor_tensor(out=tB[:, 0:T], in0=tA[:, 0:T], scalar=w0, in1=tB[:, 0:T],
                                                                         op0=ALU.mult, op1=ALU.add), reads=["tA", "tB", "vec"], writes=["tB"])
                    if ns:
                        S.op("dve", lambda e, w1=w1, j=j: e.scalar_tensor_tensor(out=tB[:, T:T + ns], in0=fst[:, j, ns:2 * ns], scalar=w1,
                                                                                  in1=tB[:, T:T + ns], op0=ALU.mult, op1=ALU.add),
                             reads=["fst", "tB", "vec"], writes=["tB"])
                        S.op("dve", lambda e, w0=w0, j=j: e.scalar_tensor_tensor(out=tB[:, T:T + ns], in0=fst[:, j, 0:ns], scalar=w0,
                                                                                  in1=tB[:, T:T + ns], op0=ALU.mult, op1=ALU.add),
                             reads=["fst", "tB", "vec"], writes=["tB"])
                    S.op("act", lambda e: e.activation(out=tC[:, 0:TTp], in_=tB[:, 0:TTp], func=AF.Silu), reads=["tB"], writes=["tC"])
                    S.op("dve", lambda e, j=j: e.tensor_copy(out=nst[:, j, 0:2], in_=tA[:, T:T + 2]), reads=["tA"], writes=["nst"])
                    if ns:
                        S.op("dve", lambda e, j=j: e.tensor_copy(out=nst[:, j, 2:2 + ns], in_=fst[:, j, ns:2 * ns]), reads=["fst"], writes=["nst"])
                        S.op("dve", lambda e, j=j: e.tensor_copy(out=nst[:, j, 2 + ns:2 + 2 * ns], in_=tA[:, 2 + T:2 + T + ns]),
                             reads=["tA"], writes=["nst"])
                    u, slot = ws.get([wunit(wup, DFF + fc * 128)])
                    proj_fm(slot, u, KC, xnT, xn_res, groups, BY)
                    ws.release(u)
                    for gi, (c0, n) in enumerate(groups):
                        S.op("dve", lambda e, o=yT[:, j, c0:c0 + n], a=tC[:, c0:c0 + n], i=ps[BY[gi]][:, 0:n]:
                             e.tensor_tensor(out=o, in0=a, in1=i, op=ALU.mult),
                             reads=[("ps", BY[gi]), "tC"], writes=[("yT", j)])
                cs = slice(g * FG * 128, (g + 1) * FG * 128)
                state_out(FG, ns, ffn_p[p, l, :, cs], ffn_s[l, :, 0, cs], ffn_s[l, :, 1, cs])
                for oc in range(KC):
                    u, slot = ws.get([(FG, 0, 128, wdn[g * FG * 128:(g + 1) * FG * 128, oc * 128:(oc + 1) * 128]
                                       .rearrange("(k p) n -> p k n", p=128))])
                    bk = BX if oc % 2 == 0 else BY
                    proj_fm(slot, u, FG, yT, lambda k: [("yT", k)], groups, bk)
                    ws.release(u)
                    for gi, (c0, n) in enumerate(groups):
                        S.op("dve", lambda e, o=xT[:, oc, c0:c0 + n], i=ps[bk[gi]][:, 0:n]: e.tensor_tensor(out=o, in0=o, in1=i, op=ALU.add),
                             reads=[("ps", bk[gi])], writes=[("xT", oc)])

        def final_out(p, ns, groups, TTp):
            rms_stats(groups, TTp)
            for dc in range(KC):
                S.op("dve", lambda e, o=xT[:, dc, 0:TTp], g=vcol("fng", 0, dc):
                     e.scalar_tensor_tensor(out=o, in0=o, scalar=g, in1=tC[:, 0:TTp], op0=ALU.mult, op1=ALU.mult),
                     reads=["tC", "vec"], writes=[("xT", dc)])
            tiles = [(tb * 128, 128, y_p[p * T + tb * 128:p * T + (tb + 1) * 128, :]) for tb in range(NTB)]
            if ns:
                tiles.append((T, ns, y_s))
            for (c0, M, dst) in tiles:
                for q in range(4):
                    for k in range(4):
                        dc = 4 * q + k
                        S.op("pe", lambda e, q=q, k=k, dc=dc, c0=c0, M=M: e.transpose(out=ps[q][0:M, k * 128:(k + 1) * 128], in_=xT[:, dc, c0:c0 + M],
                                                                           identity=ident),
                             reads=[("xT", dc), "consts"], writes=[("ps", q)], signal=(k == 3))
                    eng = "act" if q % 2 == 0 else "dve"
                    if eng == "act":
                        S.op("act", lambda e, q=q, M=M: e.copy(out=io[0:M, q * 512:(q + 1) * 512], in_=ps[q][0:M, :]),
                             reads=[("ps", q)], writes=["tA", "tB"])
                    else:
                        S.op("dve", lambda e, q=q, M=M: e.tensor_copy(out=io[0:M, q * 512:(q + 1) * 512], in_=ps[q][0:M, :]),
                             reads=[("ps", q)], writes=["tA", "tB"])
                S.op("sp", lambda e, dst=dst, M=M: e.dma_start(out=dst, in_=io[0:M, :]), reads=["tA", "tB"], dma=True)

        def init():
            S.op("sp", lambda e: e.dma_start(out=cst_t[:], in_=consts_d), writes=["consts"], dma=True)
            S.op("sp", lambda e: e.dma_start(out=tabc[:, 0:cfg.VB * 128].rearrange("p (b c) -> p b c", c=128),
                                             in_=vecs_d.rearrange("(b r) c -> r b c", r=128)), writes=["tA", "tB"], dma=True)
            for l in range(L):
                S.op("pool", lambda e, l=l: e.dma_start(out=gw[0:GR + 1, l, :], in_=gw_d[l]), writes=["gw"], dma=True)
            S.op("sp", lambda e: e.dma_start(out=flag[:], in_=flag_d), writes=["flag"], dma=True)
            S.op("dve", lambda e: e.tensor_copy(out=identb[:], in_=ident), reads=["consts"], writes=["identb"])
            S.op("dve", lambda e: e.memset(onesb[:], 1.0), writes=["onesb"])
            for b in range(cfg.VB):
                S.op("pe", lambda e, b=b: e.transpose(out=ps[b % 4][:, 0:128], in_=tabc[:, b * 128:(b + 1) * 128], identity=ident),
                     reads=["tA", "tB", "consts"], writes=[("ps", b % 4)])
                S.op("act", lambda e, b=b: e.copy(out=vec[:, b * 128:(b + 1) * 128], in_=ps[b % 4][:, 0:128]),
                     reads=[("ps", b % 4)], writes=["vec"])

        def program():
            init()
            for p in range(NPASS):
                ns = NS if p == 0 else 0
                TTp = T + (NX if ns else 0)
                groups = colgroups(ns)
                S.barrier()
                S.op("dve", lambda e: e.memset(glrT[:], 1.0), writes=["glrT"])
                load_x(p, ns)
                for l in range(L):
                    S.barrier()
                    if l == 0:
                        S.op("dve", lambda e: e.memset(yT[:, :, T:TT], 0.0), writes=[("yT", k) for k in range(KC)])
                    rms_norm("nmg", l, groups, TTp)
                    S.barrier()
                    mixer_conv(p, l, ns, groups, TTp)
                    S.barrier()
                    mixer_gla(p, l, ns)
                    out_proj(l, groups)
                    xchg_x(2 * l)
                    S.barrier()
                    rms_norm("nfg", l, groups, TTp)
                    S.barrier()
                    ffn(p, l, ns, groups, TTp)
                    if l + 1 < L:
                        xchg_x(2 * l + 1)
                S.barrier()
                final_out(p, ns, groups, TTp)

        S.dry = True
        program()
        S.dry = False
        ws.i = 0
        ws.start()
        program()

        with nc.Block() as block:
            @block.tensor
            def _(e):
                S.replay("pe", e)

            @block.scalar
            def _(e):
                S.replay("act", e)

            @block.vector
            def _(e):
                S.replay("dve", e)

            @block.gpsimd
            def _(e):
                S.replay("pool", e)

            @block.sync
            def _(e):
                S.replay("sp", e)
                S.final_wait(e)
    return nc


def make_consts():
    c = np.zeros((128, 770), np.float32)
    j = np.arange(128)[:, None]
    i = np.arange(128)[None, :]
    c[:, 0:128] = np.eye(128, dtype=np.float32)
    c[:, 128:256] = (j <= i).astype(np.float32) - (j <= 63).astype(np.float32)
    c[:, 256:384] = (j > i).astype(np.float32)
    c[:, 384:512] = (j <= i).astype(np.float32)
    c[:, 512] = 1.0
    c[:, 513] = (np.arange(128) <= 63).astype(np.float32)
    c[:, 514:770] = np.eye(16, dtype=np.float32).reshape(1, 256)
    return c


def pack_vecs(cfg, inp):
    rows = []
    for l in range(L):
        rows.append(inp["norm_mix_g"][l].reshape(16, 128))
        rows.append(inp["conv_w"][l].reshape(24, 128))
        rows.append(inp["gla_norm_g"][l].reshape(8, 128))
        rows.append(inp["norm_ffn_g"][l].reshape(16, 128))
        rows.append(inp["ffn_conv_w"][l].reshape(3 * cfg.FC, 128))
        rows.append(inp["ffn_conv_b"][l].reshape(cfg.FC, 128))
    rows.append(inp["final_norm_g"].reshape(16, 128))
    v = np.concatenate(rows, axis=0).astype(np.float32)
    out = np.zeros((cfg.VB * 128, 128), np.float32)
    out[:v.shape[0]] = v
    return out


_NC_CACHE = {}


def run(cfg, inp, ncores):
    key = (cfg.T, cfg.NPASS, cfg.DFF, cfg.NG)
    if key not in _NC_CACHE:
        _NC_CACHE[key] = build_nc(cfg)
    nc = _NC_CACHE[key]
    T = cfg.T
    consts = make_consts()
    vecs = pack_vecs(cfg, inp)
    gwp = np.ascontiguousarray(np.concatenate([inp["gate_w2"], inp["gate_b"][:, None, :]], axis=1), dtype=np.float32)
    shared = {"w_in": inp["w_in"], "w_out": inp["w_out"], "w_up": inp["w_up"], "w_down": inp["w_down"],
              "gw": gwp, "vecs": vecs, "consts": consts}
    in_maps = []
    for c in range(ncores):
        m = dict(shared)
        sl = slice(c * NS, (c + 1) * NS)
        b, half = c // 2, c % 2
        m["x_p"] = np.ascontiguousarray(inp["x_prompt"][b, half * T:(half + 1) * T])
        m["x_h"] = (np.ascontiguousarray(inp["x_prompt"][b, T - NH:T]) if half else np.zeros((NH, D), np.float32))
        fl = np.zeros((128, 2), np.float32)
        fl[:, 0] = float(half)
        fl[:, 1] = 1.0 - float(half)
        m["flag"] = fl
        m["x_s"] = np.ascontiguousarray(inp["x_sample"][sl, 0, :])
        m["st_conv"] = np.ascontiguousarray(inp["state_conv"][:, sl])
        m["st_gla"] = np.ascontiguousarray(inp["state_gla"][:, sl])
        m["st_ffn"] = np.ascontiguousarray(inp["state_ffn_conv"][:, sl])
        in_maps.append(m)
    res = run_bass_kernel_spmd(nc, in_maps, core_ids=list(range(ncores)))
    return res.results


def assemble(cfg, R, ncores):
    nb = ncores // 2
    y_prompt = np.stack([np.concatenate([R[2 * b]["y_p"], R[2 * b + 1]["y_p"]], 0) for b in range(nb)], 0)
    y_sample = np.concatenate([R[c]["y_s"] for c in range(ncores)], 0)[:, None, :]
    conv_p = np.stack([R[2 * b + 1]["conv_p"][0] for b in range(nb)], 1)
    gla_p = np.stack([R[2 * b + 1]["gla_p"][0] for b in range(nb)], 1)
    ffn_p = np.stack([R[2 * b + 1]["ffn_p"][0] for b in range(nb)], 1)
    conv_s = np.concatenate([R[c]["conv_s"] for c in range(ncores)], 1)
    gla_s = np.concatenate([R[c]["gla_s"] for c in range(ncores)], 1)
    ffn_s = np.concatenate([R[c]["ffn_s"] for c in range(ncores)], 1)
    return tuple(np.ascontiguousarray(a, dtype=np.float32) for a in
                 (y_prompt, y_sample, conv_p, gla_p, ffn_p, conv_s, gla_s, ffn_s))


def kernel(**inputs):
    inp = {k: np.asarray(v) for k, v in inputs.items()}
    cfg = Cfg()
    R = run(cfg, inp, 8)
    return assemble(cfg, R, 8)
```

```python
import numpy as np
from contextlib import ExitStack
import concourse.bass as bass
import concourse.mybir as mybir
from concourse.bass_utils import run_bass_kernel_spmd

F32 = mybir.dt.float32
BF16 = mybir.dt.bfloat16
ALU = mybir.AluOpType
AF = mybir.ActivationFunctionType

ENGS = ("pe", "act", "dve", "pool", "sp")

D = 2048
KC = 16
L = 2
CD = 1024
CC = 8
H = 4
DK = 128
DV = 256
GR = 16
S1, S2, S3 = 1024, 2048, 3072
S4, S5, S6, S7 = 3584, 4096, 5120, 6144
INC = 6160
EPS = 1e-6
NS = 16
NH = 2
NX = NS + NH
NSLOT = 6


class Cfg:
    def __init__(self, T=1024, NPASS=1, DFF=5632, NG=4):
        self.T, self.NPASS, self.DFF, self.NG = T, NPASS, DFF, NG
        self.FC = DFF // 128
        self.FG = self.FC // NG
        self.SEQ = T * NPASS
        self.TT = T + NX
        self.NTB = T // 128
        o = 0
        self.off = {}
        for l in range(L):
            for name, n in (("nmg", 16), ("cw", 24), ("gng", 8), ("nfg", 16),
                            ("fcw", 3 * self.FC), ("fcb", self.FC)):
                self.off[(name, l)] = o
                o += n
        self.off[("fng", 0)] = o
        o += 16
        self.VR = o
        self.VB = (o + 127) // 128


class Sched:
    def __init__(self, sems, dma_sems):
        self.sem = dict(zip(ENGS, sems))
        self.cnt = {e: 0 for e in ENGS}
        self.ops = {e: [] for e in ENGS}
        self.waited = {e: {} for e in ENGS}
        self.res = {}
        self.dma_sems = {k: list(v) for k, v in dma_sems.items()}
        self.dma_val = {k: [0] * len(v) for k, v in self.dma_sems.items()}
        self.dma_rr = {k: 0 for k in self.dma_sems}
        self.dry = False

    def _need(self, eng, h, waits):
        if h is None:
            return
        sem, val, heng = h
        if heng == "pe" and eng == "pe":
            return
        k = id(sem)
        if self.waited[eng].get(k, 0) >= val:
            return
        prev = waits.get(k)
        if prev is None or prev[1] < val:
            waits[k] = (sem, val)

    def op(self, eng, fn, reads=(), writes=(), signal=True, dma=False, cc=None):
        if self.dry:
            return None
        waits = {}
        for r in reads:
            ent = self.res.get(r)
            if ent is not None:
                self._need(eng, ent[0], waits)
        for w in writes:
            ent = self.res.get(w)
            if ent is not None:
                self._need(eng, ent[0], waits)
                for rh in ent[1]:
                    self._need(eng, rh, waits)
        if dma:
            i = self.dma_rr[eng]
            self.dma_rr[eng] = (i + 1) % len(self.dma_sems[eng])
            dsem = self.dma_sems[eng][i]
            if self.dma_val[eng][i] > 0:
                self._need(eng, (dsem, self.dma_val[eng][i], "dma"), waits)
            self.dma_val[eng][i] += 16
            handle = (dsem, self.dma_val[eng][i], "dma")
            inc = (dsem, 16)
        elif cc is not None:
            handle = (cc, 1, "cc")
            inc = (cc, None)
        elif signal:
            self.cnt[eng] += 1
            handle = (self.sem[eng], self.cnt[eng], eng)
            inc = (self.sem[eng], 1)
        else:
            handle = (self.sem[eng], self.cnt[eng] + 1, eng)
            inc = None
        wl = list(waits.values())
        for sem, val in wl:
            self.waited[eng][id(sem)] = val
        self.ops[eng].append((wl, fn, inc))
        for r in reads:
            self.res.setdefault(r, [None, []])[1].append(handle)
        for w in writes:
            self.res[w] = [handle, []]
        return handle

    def barrier(self, engs=("pe", "act", "dve", "sp")):
        if self.dry:
            return
        for e in engs:
            wl = []
            for o in ENGS:
                if o == e or self.cnt[o] == 0:
                    continue
                if o == "pool":
                    continue
                if self.waited[e].get(id(self.sem[o]), 0) < self.cnt[o]:
                    wl.append((self.sem[o], self.cnt[o]))
                    self.waited[e][id(self.sem[o])] = self.cnt[o]
            for q in ("sp", "act"):
                for i, dsem in enumerate(self.dma_sems[q]):
                    v = self.dma_val[q][i]
                    if v > 0 and self.waited[e].get(id(dsem), 0) < v:
                        wl.append((dsem, v))
                        self.waited[e][id(dsem)] = v
            if wl:
                self.ops[e].append((wl, None, None))

    def replay(self, eng, e):
        for wl, fn, inc in self.ops[eng]:
            for sem, val in wl:
                e.wait_ge(sem, val)
            if fn is None:
                continue
            ins = fn(e)
            if inc is not None:
                if inc[1] is None:
                    ins.then_inc(inc[0])
                else:
                    ins.then_inc(inc[0], inc[1])

    def final_wait(self, e):
        for q in self.dma_sems:
            for i, dsem in enumerate(self.dma_sems[q]):
                if self.dma_val[q][i] > 0:
                    e.wait_ge(dsem, self.dma_val[q][i])
        for en in ENGS:
            if self.cnt[en] > 0:
                e.wait_ge(self.sem[en], self.cnt[en])


def build_nc(cfg):
    T, TT, NPASS, DFF, FC, FG, NG, NTB = cfg.T, cfg.TT, cfg.NPASS, cfg.DFF, cfg.FC, cfg.FG, cfg.NG, cfg.NTB
    SEQ = cfg.SEQ
    nc = bass.Bass("TRN2", target_bir_lowering=False)

    def din(name, shape):
        return nc.dram_tensor(name, list(shape), F32, kind="ExternalInput").ap()

    def dout(name, shape):
        return nc.dram_tensor(name, list(shape), F32, kind="ExternalOutput").ap()

    x_p = din("x_p", [SEQ, D]); x_s = din("x_s", [NS, D]); x_h = din("x_h", [NH, D])
    flag_d = din("flag", [128, 2])
    st_conv = din("st_conv", [L, NS, 2, CD]); st_gla = din("st_gla", [L, NS, H, DK, DV])
    st_ffn = din("st_ffn", [L, NS, 2, DFF])
    w_in = din("w_in", [L, D, INC]); w_out = din("w_out", [L, D, D])
    w_up = din("w_up", [L, D, 2 * DFF]); w_down = din("w_down", [L, DFF, D])
    gw_d = din("gw", [L, GR + 1, H * DK])
    vecs_d = din("vecs", [cfg.VB * 128, 128])
    consts_d = din("consts", [128, 770])
    y_p = dout("y_p", [SEQ, D]); y_s = dout("y_s", [NS, D])
    conv_p = dout("conv_p", [NPASS, L, 2, CD]); gla_p = dout("gla_p", [NPASS, L, H, DK, DV])
    ffn_p = dout("ffn_p", [NPASS, L, 2, DFF])
    conv_s = dout("conv_s", [L, NS, 2, CD]); gla_s = dout("gla_s", [L, NS, H, DK, DV])
    ffn_s = dout("ffn_s", [L, NS, 2, DFF])

    es = ExitStack()
    with es:
        def sb(name, shape, dt=F32):
            return es.enter_context(nc.sbuf_tensor(name, list(shape), dt))

        xT = sb("xT", [128, KC, TT]); xnT = sb("xnT", [128, KC, TT], BF16)
        yT = sb("yT", [128, KC, TT], BF16)
        wsl = [sb("wsl%d" % i, [128, KC, 128], BF16) for i in range(NSLOT)]
        EW = TT + 2
        tabc_n = max(3 * EW, 2048 + EW)
        tabc = sb("tabc", [128, tabc_n])
        tA = tabc[:, 0:EW]; tB = tabc[:, EW:2 * EW]; tC = tabc[:, tabc_n - EW:tabc_n]
        io = tabc[:, 0:2048]
        nxb = (KC * TT // 2) // 2048
        if nxb >= 1:
            yflat = yT[:].rearrange("p k t -> p (k t)").bitcast(F32)
            xin = [(yflat[:, i * 2048:(i + 1) * 2048], [("xin", i)]) for i in range(nxb)]
        else:
            xin = [(io, ["tA", "tB"])]
        sqv = tB.bitcast(BF16)
        SCR = 7016
        scr = sb("scr", [128, SCR])
        cst_t = sb("consts_sb", [128, 770])
        ident = cst_t[:, 0:128]; Uc = cst_t[:, 128:256]; Lm = cst_t[:, 256:384]
        maskT = cst_t[:, 384:512]; cmat = cst_t[:, 512:514]
        idb16 = cst_t[:, 514:770].rearrange("p (r t) -> p r t", r=16)
        identb = sb("identb", [128, 128], BF16); onesb = sb("onesb", [128, 128], BF16)
        vec = sb("vec_sb", [128, cfg.VB * 128])
        gw = sb("gw_sb", [32, L, H * DK], BF16)
        glrT = sb("glrT", [32, TT], BF16)
        Sst = sb("Sst", [128, H, DV])
        ps = [es.enter_context(nc.psum_tensor("ps%d" % i, [128, 512], F32)) for i in range(8)]
        psb = [p[:].bitcast(BF16) for p in ps]
        sems = [es.enter_context(nc.semaphore("s_%s" % e)) for e in ENGS]
        dsems = {"sp": [es.enter_context(nc.semaphore("dsp_%d" % i)) for i in range(16)],
                 "pool": [es.enter_context(nc.semaphore("dpl_%d" % i)) for i in range(8)],
                 "act": [es.enter_context(nc.semaphore("dac_%d" % i)) for i in range(8)]}
        S = Sched(sems, dsems)
        ibS = [[nc.dram_tensor("ibS_%d_%d" % (l, h), [128, DV], F32, kind="Internal") for h in range(H)] for l in range(L)]
        obS = [[nc.dram_tensor("obS_%d_%d" % (l, h), [256, DV], F32, kind="Internal") for h in range(H)] for l in range(L)]
        ibX = [nc.dram_tensor("ibX_%d" % i, [128, 2 * KC], F32, kind="Internal") for i in range(2 * L)]
        obX = [nc.dram_tensor("obX_%d" % i, [256, 2 * KC], F32, kind="Internal") for i in range(2 * L)]
        ccsems = [es.enter_context(nc.semaphore("cc_%d" % i)) for i in range(L * H + 2 * L)]
        PAIRS = [[0, 1], [2, 3], [4, 5], [6, 7]]
        flag = sb("flag_sb", [128, 2]); xhs = sb("xhs", [128, KC, 2]); xhr = sb("xhr", [128, KC, 2])

        def sv(off, n, dt=F32, parts=128):
            v = scr[0:parts, off:off + n]
            return v.bitcast(dt) if dt != F32 else v
        g_e = sv(0, 128); g_lf = sv(128, 128); g_eb = sv(256, 128); g_enb = sv(384, 128)
        g_erev = sv(512, 128); g_dec = sv(640, 2); g_ss = sv(642, 2); g_rs = sv(644, 2)
        g_qd = sv(648, 64, BF16); g_kd = sv(712, 64, BF16); g_kdec = sv(776, 64, BF16)
        g_vbf = sv(840, 128, BF16); g_Sbf = sv(968, 128, BF16); g_qkT = sv(1096, 128, BF16)
        g_ATm = sv(1224, 64, BF16); g_sg = sv(1288, 256); g_yb = sv(1544, 128, BF16)
        g_Qm = sv(1672, 256).rearrange("p (r t) -> p r t", r=16)
        g_akq = sv(1928, 48); g_qs = sv(1976, 128); g_ks = sv(2104, 128); g_as = sv(2232, 128)
        g_vs = sv(2360, 256)
        g_Sb = [sv(2616 + i * 512, 512).rearrange("p (r v) -> p r v", r=2) for i in range(2)]
        g_Km = tabc[0:16, 0:2048].rearrange("p (r k) -> p r k", r=16)
        g_qeT = sv(3640, 64 * 8, BF16).rearrange("p (b t) -> p b t", t=128)
        g_sgs = sv(4152, 128 * 8, BF16).rearrange("p (b v) -> p b v", v=DV)
        g_SA = sv(5176, 256); g_SAb = sv(5432, 128, BF16); g_junk = sv(5560, 128, BF16)
        g_Bsn = sv(5688, 1); g_Etot = sv(5692, 1)
        g_ecol = [sv(5690, 1), sv(5694, 1)]
        g_dec = [g_dec, sv(5696, 2)]
        g_qd = [g_qd, sv(5700, 64, BF16)]; g_kd = [g_kd, sv(5764, 64, BF16)]; g_kdec = [g_kdec, sv(5828, 64, BF16)]
        g_vbf = [g_vbf, sv(5892, 128, BF16)]
        g_yb = [g_yb, sv(6020, 128, BF16)]
        g_ss = [g_ss, sv(6148, 2)]; g_rs = [g_rs, sv(6152, 2)]
        g_sig = sv(6156, 256)
        g_ss8 = sv(6720, 8); g_rs8 = sv(6728, 8)
        g_c = sv(6736, 2); g_cv = sv(6752, 256)
        g_Sbb = [sv(6412 + i * 128, 128, BF16) for i in range(2)]
        g_vsb = sv(2360, 128, BF16)
        g_Qmb = sv(1672, 128, BF16).rearrange("p (r t) -> p r t", r=16)
        g_Kmb = tabc[0:16, 0:1024].bitcast(BF16).rearrange("p (r k) -> p r k", r=16)
        g_Sb1 = [sv(2616 + i * 256, 256) for i in range(4)] + [tabc[:, 1024 + i * 256:1280 + i * 256] for i in range(min(8, (tabc_n - 1024) // 256))]
        NSB = len(g_Sb1)
        stg_in = sv(0, 1408); stg_out = sv(1408, 1408)
        fst = sv(2816, 11 * 32).rearrange("p (j c) -> p j c", c=32)
        nst = sv(3168, 11 * 34).rearrange("p (j c) -> p j c", c=34)

        class WS:
            def __init__(self):
                self.req = []
                self.i = 0
                self.loaded = 0
            def get(self, parts):
                if S.dry:
                    self.req.append(parts)
                    self.i += 1
                    return self.i - 1, wsl[(self.i - 1) % NSLOT]
                u = self.i
                self.i += 1
                return u, wsl[u % NSLOT]
            def load(self, u):
                if u >= len(self.req):
                    return
                slot = wsl[u % NSLOT]
                for (kn, c0, ncol, src) in self.req[u]:
                    S.op("pool", lambda e, o=slot[:, 0:kn, c0:c0 + ncol], s=src: e.dma_start(out=o, in_=s),
                         writes=[("w", u % NSLOT)], dma=True)
                self.loaded = u + 1
            def release(self, u):
                if S.dry:
                    return
                if u + NSLOT == self.loaded:
                    self.load(u + NSLOT)
            def start(self):
                for u in range(min(NSLOT, len(self.req))):
                    self.load(u)
        ws = WS()

        def wunit(src2d, c0, ncols=128, kn=KC):
            return (kn, 0, ncols, src2d[:, c0:c0 + ncols].rearrange("(k p) n -> p k n", p=128))

        def colgroups(ns):
            g = []
            c = 0
            while c < T:
                n = min(512, T - c)
                g.append((c, n))
                c += n
            if ns:
                g.append((T, NX))
            return g

        BX, BY = (0, 1, 2), (3, 4, 5)

        def proj_fm(slot, u, nk, rhs_t, rhs_res, groups, banks, M=128):
            for k in range(nk):
                for gi, (c0, n) in enumerate(groups):
                    S.op("pe", lambda e, o=ps[banks[gi]][0:M, 0:n], l=slot[:, k, 0:M], r=rhs_t[:, k, c0:c0 + n],
                         a=(k == 0), b=(k == nk - 1): e.matmul(o, lhsT=l, rhs=r, start=a, stop=b),
                         reads=[("w", u % NSLOT)] + rhs_res(k), writes=[("ps", banks[gi])], signal=(k == nk - 1))

        def vcol(name, l, i):
            o = cfg.off[(name, l)] + i
            return vec[:, o:o + 1]

        def load_x(p, ns):
            for tb in range(NTB):
                xb, xres = xin[tb % len(xin)]
                S.op("sp", lambda e, r0=p * T + tb * 128, xb=xb: e.dma_start(out=xb, in_=x_p[r0:r0 + 128, :]),
                     writes=xres, dma=True)
                for q in range(4):
                    bk = 4 + ((4 * tb + q) % 4)
                    for k in range(4):
                        dc = 4 * q + k
                        S.op("pe", lambda e, o=ps[bk][:, k * 128:(k + 1) * 128], i=xb[:, dc * 128:(dc + 1) * 128]:
                             e.transpose(out=o, in_=i, identity=ident), reads=xres + ["consts"],
                             writes=[("ps", bk)], signal=(k == 3))
                    eng = "act" if q % 2 == 0 else "dve"
                    o = xT[:, 4 * q:4 * q + 4, tb * 128:(tb + 1) * 128]
                    i = ps[bk][:, :].rearrange("p (k t) -> p k t", k=4)
                    if eng == "act":
                        S.op("act", lambda e, o=o, i=i: e.copy(out=o, in_=i), reads=[("ps", bk)],
                             writes=[("xT", 4 * q + k) for k in range(4)])
                    else:
                        S.op("dve", lambda e, o=o, i=i: e.tensor_copy(out=o, in_=i), reads=[("ps", bk)],
                             writes=[("xT", 4 * q + k) for k in range(4)])
            if ns:
                S.op("sp", lambda e: e.dma_start(out=io[0:NS, :], in_=x_s), writes=["tA", "tB"], dma=True)
                S.op("sp", lambda e: e.dma_start(out=io[NS:NX, :], in_=x_h), writes=["tA", "tB"], dma=True)
                for dc in range(KC):
                    S.op("pe", lambda e, o=ps[6][:, dc * NX:(dc + 1) * NX], i=io[0:NX, dc * 128:(dc + 1) * 128]:
                         e.transpose(out=o, in_=i, identity=ident[0:NX, 0:NX]), reads=["tA", "tB", "consts"],
                         writes=[("ps", 6)], signal=(dc == KC - 1))
                S.op("dve", lambda e: e.tensor_copy(out=xT[:, :, T:T + NX],
                                                     in_=ps[6][:, 0:KC * NX].rearrange("p (k t) -> p k t", k=KC)),
                     reads=[("ps", 6)], writes=[("xT", k) for k in range(KC)])

        def rms_stats(groups, TTp):
            for dc in range(KC):
                sq = sqv[:, (dc % 2) * EW:(dc % 2) * EW + TTp]
                S.op("act", lambda e, o=sq, i=xT[:, dc, 0:TTp]: e.activation(out=o, in_=i, func=AF.Square),
                     reads=[("xT", dc)], writes=[("sq", dc % 2)])
                for gi, (c0, n) in enumerate(groups):
                    S.op("pe", lambda e, o=ps[BX[gi]][:, 0:n], r=sq[:, c0:c0 + n], a=(dc == 0), b=(dc == KC - 1):
                         e.matmul(o, lhsT=onesb[:], rhs=r, start=a, stop=b),
                         reads=[("sq", dc % 2), "onesb"], writes=[("ps", BX[gi])],
                         signal=(gi == len(groups) - 1))
            for gi, (c0, n) in enumerate(groups):
                S.op("act", lambda e, o=tC[:, c0:c0 + n], i=ps[BX[gi]][:, 0:n]:
                     e.activation(out=o, in_=i, func=AF.Ln, scale=1.0 / D, bias=EPS),
                     reads=[("ps", BX[gi])], writes=["tC"])
            S.op("act", lambda e: e.activation(out=tC[:, 0:TTp], in_=tC[:, 0:TTp], func=AF.Exp, scale=-0.5),
                 reads=["tC"], writes=["tC"])

        def rms_norm(gname, l, groups, TTp):
            rms_stats(groups, TTp)
            for dc in range(KC):
                S.op("dve", lambda e, o=xnT[:, dc, 0:TTp], i=xT[:, dc, 0:TTp], g=vcol(gname, l, dc):
                     e.scalar_tensor_tensor(out=o, in0=i, scalar=g, in1=tC[:, 0:TTp], op0=ALU.mult, op1=ALU.mult),
                     reads=[("xT", dc), "tC", "vec"], writes=["xnT"])

        def xn_res(k):
            return ["xnT"]

        def conv_state_in(l):
            for r in range(2):
                S.op("sp", lambda e, r=r: e.dma_start(out=stg_in[r * NS:(r + 1) * NS, 0:CD], in_=st_conv[l, :, r, :]),
                     writes=["stg_in"], dma=True)
            for c in range(CC):
                S.op("pe", lambda e, c=c: e.transpose(out=ps[7][:, c * 32:(c + 1) * 32],
                                                      in_=stg_in[0:32, c * 128:(c + 1) * 128], identity=ident[0:32, 0:32]),
                     reads=["stg_in", "consts"], writes=[("ps", 7)], signal=(c == CC - 1))
            S.op("dve", lambda e: e.tensor_copy(out=fst[:, 0:CC, :], in_=ps[7][:, 0:CC * 32].rearrange("p (j c) -> p j c", c=32)),
                 reads=[("ps", 7)], writes=["fst"])

        def state_out(nchunks, ns, dst_p, dst_s0, dst_s1):
            W = 2 + 2 * ns
            nb = (nchunks + 3) // 4
            for j in range(nchunks):
                bk = [5, 6, 7][j // 4]
                S.op("pe", lambda e, j=j, bk=bk: e.transpose(out=ps[bk][0:W, (j % 4) * 128:(j % 4 + 1) * 128],
                                                               in_=nst[:, j, 0:W], identity=ident),
                     reads=["nst", "consts"], writes=[("ps", bk)], signal=(j % 4 == 3 or j == nchunks - 1))
            for b in range(nb):
                bk = [5, 6, 7][b]
                n = min(4, nchunks - 4 * b) * 128
                S.op("act", lambda e, b=b, bk=bk, n=n: e.copy(out=stg_out[0:W, b * 512:b * 512 + n], in_=ps[bk][0:W, 0:n]),
                     reads=[("ps", bk)], writes=["stg_out"])
            S.op("sp", lambda e: e.dma_start(out=dst_p, in_=stg_out[0:2, 0:nchunks * 128]), reads=["stg_out"], dma=True)
            if ns:
                S.op("sp", lambda e: e.dma_start(out=dst_s0, in_=stg_out[2:2 + ns, 0:nchunks * 128]), reads=["stg_out"], dma=True)
                S.op("sp", lambda e: e.dma_start(out=dst_s1, in_=stg_out[2 + ns:2 + 2 * ns, 0:nchunks * 128]),
                     reads=["stg_out"], dma=True)

        def mixer_conv(p, l, ns, groups, TTp):
            win = w_in[l]
            if ns:
                conv_state_in(l)
            u, slot = ws.get([(KC, 0, GR, win[:, S7:S7 + GR].rearrange("(k p) n -> p k n", p=128))])
            proj_fm(slot, u, KC, xnT, xn_res, groups, BY, M=GR)
            ws.release(u)
            for gi, (c0, n) in enumerate(groups):
                S.op("act", lambda e, o=glrT[0:GR, c0:c0 + n], i=ps[BY[gi]][0:GR, 0:n]: e.copy(out=o, in_=i),
                     reads=[("ps", BY[gi])], writes=["glrT"])
            for c in range(CC):
                B1, B2 = (BX, BY) if c % 2 == 0 else (BY, BX)
                u, slot = ws.get([wunit(win, S1 + c * 128)])
                proj_fm(slot, u, KC, xnT, xn_res, groups, B1)
                ws.release(u)
                for gi, (c0, n) in enumerate(groups):
                    S.op("act", lambda e, o=tA[:, c0:c0 + n], i=ps[B1[gi]][:, 0:n]: e.copy(out=o, in_=i),
                         reads=[("ps", B1[gi])], writes=["tA"])
                u, slot = ws.get([wunit(win, S2 + c * 128)])
                proj_fm(slot, u, KC, xnT, xn_res, groups, B2)
                ws.release(u)
                for gi, (c0, n) in enumerate(groups):
                    S.op("dve", lambda e, o=tB[:, 2 + c0:2 + c0 + n], a=tA[:, c0:c0 + n], i=ps[B2[gi]][:, 0:n]:
                         e.tensor_tensor(out=o, in0=a, in1=i, op=ALU.mult),
                         reads=[("ps", B2[gi]), "tA"], writes=["tB"])
                S.op("dve", lambda e: e.tensor_copy(out=tB[:, 0:2], in_=tB[:, 2 + T + NS:2 + T + NX]),
                     reads=["tB"], writes=["tB"])
                w0, w1, w2 = vcol("cw", l, c), vcol("cw", l, 8 + c), vcol("cw", l, 16 + c)
                S.op("act", lambda e, w0=w0: e.activation(out=tC[:, 0:T], in_=tB[:, 0:T], func=AF.Copy, scale=w0),
                     reads=["tB", "vec"], writes=["tC"])
                S.op("dve", lambda e, w1=w1: e.scalar_tensor_tensor(out=tC[:, 0:T], in0=tB[:, 1:1 + T], scalar=w1,
                                                                     in1=tC[:, 0:T], op0=ALU.mult, op1=ALU.add),
                     reads=["tB", "tC", "vec"], writes=["tC"])
                S.op("dve", lambda e, w2=w2: e.scalar_tensor_tensor(out=tC[:, 0:T], in0=tB[:, 2:2 + T], scalar=w2,
                                                                     in1=tC[:, 0:T], op0=ALU.mult, op1=ALU.add),
                     reads=["tB", "tC", "vec"], writes=["tC"])
                if ns:
                    S.op("act", lambda e, w0=w0, c=c: e.activation(out=tC[:, T:T + ns], in_=fst[:, c, 0:ns], func=AF.Copy, scale=w0),
                         reads=["fst", "vec", "tC"], writes=["tC"])
                    S.op("dve", lambda e, w1=w1, c=c: e.scalar_tensor_tensor(out=tC[:, T:T + ns], in0=fst[:, c, ns:2 * ns], scalar=w1,
                                                                              in1=tC[:, T:T + ns], op0=ALU.mult, op1=ALU.add),
                         reads=["fst", "tC", "vec"], writes=["tC"])
                    S.op("dve", lambda e, w2=w2: e.scalar_tensor_tensor(out=tC[:, T:T + ns], in0=tB[:, 2 + T:2 + T + ns], scalar=w2,
                                                                         in1=tC[:, T:T + ns], op0=ALU.mult, op1=ALU.add),
                         reads=["tB", "tC", "vec"], writes=["tC"])
                S.op("dve", lambda e, c=c: e.tensor_copy(out=nst[:, c, 0:2], in_=tB[:, T:T + 2]), reads=["tB"], writes=["nst"])
                if ns:
                    S.op("dve", lambda e, c=c: e.tensor_copy(out=nst[:, c, 2:2 + ns], in_=fst[:, c, ns:2 * ns]),
                         reads=["fst"], writes=["nst"])
                    S.op("dve", lambda e, c=c: e.tensor_copy(out=nst[:, c, 2 + ns:2 + 2 * ns], in_=tB[:, 2 + T:2 + T + ns]),
                         reads=["tB"], writes=["nst"])
                u, slot = ws.get([wunit(win, c * 128)])
                proj_fm(slot, u, KC, xnT, xn_res, groups, B1)
                ws.release(u)
                for gi, (c0, n) in enumerate(groups):
                    S.op("dve", lambda e, o=yT[:, c, c0:c0 + n], a=tC[:, c0:c0 + n], i=ps[B1[gi]][:, 0:n]:
                         e.tensor_tensor(out=o, in0=a, in1=i, op=ALU.mult),
                         reads=[("ps", B1[gi]), "tC"], writes=[("yT", c)])
            state_out(CC, ns, conv_p[p, l], conv_s[l, :, 0, :], conv_s[l, :, 1, :])

        def gla_post(h, l, M, c0, o_ap, obank, sg_ap, par, tbank, ores=None):
            ores = ores or ("ps", obank)
            ss, rs, yb = g_ss[par], g_rs[par], g_yb[par]
            tps = psb[tbank]
            S.op("dve", lambda e: e.memset(ss[0:M, 0:2], 0.0), writes=[("g_ss", par)])
            S.op("act", lambda e: e.activation(out=g_junk[0:M, :], in_=o_ap, func=AF.Square, accum_out=ss[0:M, 0:1]),
                 reads=[ores], writes=["g_junk", ("g_ss", par)])
            S.op("act", lambda e: e.activation(out=rs[0:M, 0:1], in_=ss[0:M, 0:1], func=AF.Ln, scale=1.0 / DV, bias=EPS),
                 reads=[("g_ss", par)], writes=[("g_rs", par)])
            S.op("act", lambda e: e.activation(out=rs[0:M, 0:1], in_=rs[0:M, 0:1], func=AF.Exp, scale=-0.5),
                 reads=[("g_rs", par)], writes=[("g_rs", par)])
            S.op("dve", lambda e: e.scalar_tensor_tensor(out=yb[0:M, :], in0=o_ap, scalar=rs[0:M, 0:1],
                                                          in1=sg_ap, op0=ALU.mult, op1=ALU.mult),
                 reads=[ores, ("g_rs", par), "g_sg", "g_sgs"], writes=[("g_yb", par)])
            for vc in range(2):
                S.op("pe", lambda e, vc=vc: e.transpose(out=tps[:, 256 + vc * 128:256 + vc * 128 + M],
                                                        in_=yb[0:M, vc * 128:(vc + 1) * 128], identity=identb[0:M, 0:M]),
                     reads=[("g_yb", par), "identb"], writes=[("ps", tbank)], signal=(vc == 1))
            for vc in range(2):
                ch = CC + 2 * h + vc
                S.op("dve", lambda e, vc=vc, ch=ch: e.tensor_scalar_mul(yT[:, ch, c0:c0 + M], tps[:, 256 + vc * 128:256 + vc * 128 + M],
                                                                         vcol("gng", l, 2 * h + vc)),
                     reads=[("ps", tbank), "vec"], writes=[("yT", ch)])

        def o_store(tb):
            return 4 + tb // 2, ps[4 + tb // 2][:, (tb % 2) * DV:(tb % 2 + 1) * DV]

        def mixer_gla(p, l, ns):
            for h in range(H):
                gla_head(p, l, ns, h)
            S.op("sp", lambda e: e.dma_start(out=gla_p[p, l].rearrange("h k v -> k h v"), in_=Sst[:]),
                 reads=[("S", hh) for hh in range(H)], dma=True)

        def gla_head(p, l, ns, h):
            win = w_in[l]
            sc = 1.0 / 16.0
            dst = [(0, 0), (0, 128), (1, 0), (1, 128), (1, 256), (1, 384)]
            if True:
                cols = (S3 + h * DK, S4 + h * DK, S5 + h * DV, S5 + h * DV + 128, S6 + h * DV, S6 + h * DV + 128)
                units = [ws.get([wunit(win, c0)]) for c0 in cols]

                def P(i, M, c0, last=False):
                    (u, slot), (bk, co) = units[i], dst[i]
                    for k in range(KC):
                        S.op("pe", lambda e, o=ps[bk][0:M, co:co + 128], lt=xnT[:, k, c0:c0 + M], r=slot[:, k, :],
                             a=(k == 0), b=(k == KC - 1): e.matmul(o, lhsT=lt, rhs=r, start=a, stop=b),
                             reads=[("w", u % NSLOT), "xnT"], writes=[("ps", bk)], signal=(k == KC - 1))
                    if last:
                        ws.release(u)

                def Z(M, c0):
                    S.op("pe", lambda e: e.matmul(ps[2][0:M, 0:128], lhsT=glrT[0:GR + 1, c0:c0 + M],
                                                  rhs=gw[0:GR + 1, l, h * DK:(h + 1) * DK], start=True, stop=True),
                         reads=["glrT", "gw"], writes=[("ps", 2)])
                    S.op("act", lambda e: e.activation(out=g_e[0:M, :], in_=ps[2][0:M, 0:128], func=AF.Exp, scale=-1.0),
                         reads=[("ps", 2)], writes=["g_e"])
                    S.op("act", lambda e: e.activation(out=g_lf[0:M, :], in_=g_e[0:M, :], func=AF.Ln, bias=1.0),
                         reads=["g_e"], writes=["g_lf"])

                def cumsum(n):
                    S.op("pe", lambda e: e.matmul(ps[2][:, 128:256], lhsT=Uc, rhs=g_lf, start=True, stop=True),
                         reads=["g_lf", "consts"], writes=[("ps", 2)], signal=False)
                    S.op("pe", lambda e: e.matmul(ps[2][:, 256:384], lhsT=Lm, rhs=g_lf, start=True, stop=True),
                         reads=["g_lf", "consts"], writes=[("ps", 2)], signal=False)
                    S.op("pe", lambda e: e.matmul(ps[2][:, 384:386], lhsT=g_lf, rhs=cmat, start=True, stop=True),
                         reads=["g_lf", "consts"], writes=[("ps", 2)])

                def gates(n):
                    par = n % 2
                    S.op("act", lambda e: e.activation(out=g_eb, in_=ps[2][:, 128:256], func=AF.Exp, scale=-sc), reads=[("ps", 2)], writes=["g_eb"])
                    S.op("act", lambda e: e.activation(out=g_enb, in_=ps[2][:, 128:256], func=AF.Exp, scale=sc), reads=[("ps", 2)], writes=["g_enb"])
                    S.op("act", lambda e: e.activation(out=g_erev, in_=ps[2][:, 256:384], func=AF.Exp, scale=-sc), reads=[("ps", 2)], writes=["g_erev"])
                    S.op("act", lambda e: e.activation(out=g_dec[par], in_=ps[2][:, 384:386], func=AF.Exp, scale=-sc),
                         reads=[("ps", 2)], writes=[("g_dec", par)])
                    S.op("act", lambda e: e.activation(out=g_ecol[par], in_=ps[2][:, 385:386], func=AF.Exp, scale=-sc, bias=g_Bsn),
                         reads=[("ps", 2), "g_Bsn"], writes=[("g_ecol", par)])
                    S.op("dve", lambda e: e.scalar_tensor_tensor(out=g_Bsn, in0=ps[2][:, 384:385], scalar=-sc, in1=g_Bsn, op0=ALU.mult, op1=ALU.add),
                         reads=[("ps", 2), "g_Bsn"], writes=["g_Bsn"])

                def qkd(n):
                    par = n % 2
                    S.op("dve", lambda e: e.scalar_tensor_tensor(out=g_qd[par], in0=ps[0][:, 0:128], scalar=float(DK) ** -0.5, in1=g_eb,
                                                                  op0=ALU.mult, op1=ALU.mult), reads=[("ps", 0), "g_eb"], writes=[("g_qd", par)])
                    S.op("dve", lambda e: e.tensor_tensor(out=g_kd[par], in0=ps[0][:, 128:256], in1=g_enb, op=ALU.mult),
                         reads=[("ps", 0), "g_enb"], writes=[("g_kd", par)])
                    S.op("dve", lambda e: e.tensor_tensor(out=g_kdec[par], in0=ps[0][:, 128:256], in1=g_erev, op=ALU.mult),
                         reads=[("ps", 0), "g_erev"], writes=[("g_kdec", par)])

                def vg(n):
                    par = n % 2
                    S.op("act", lambda e: e.copy(out=g_vbf[par], in_=ps[1][:, 0:DV]), reads=[("ps", 1)], writes=[("g_vbf", par)])
                    S.op("act", lambda e: e.activation(out=g_sig, in_=ps[1][:, 256:512], func=AF.Exp, scale=-1.0), reads=[("ps", 1)], writes=["g_sig"])
                    S.op("act", lambda e: e.activation(out=g_sig, in_=g_sig, func=AF.Ln, bias=1.0), reads=["g_sig"], writes=["g_sig"])
                    S.op("act", lambda e: e.activation(out=g_sig, in_=g_sig, func=AF.Exp, scale=-1.0), reads=["g_sig"], writes=["g_sig"])
                    S.op("dve", lambda e: e.tensor_tensor(out=g_sgs[:, n, :], in0=ps[1][:, 256:512], in1=g_sig, op=ALU.mult),
                         reads=[("ps", 1), "g_sig"], writes=["g_sgs"])

                def Sbf(n):
                    par = n % 2
                    S.op("dve", lambda e: e.tensor_scalar_mul(g_Sbf, Sst[:, h, :], g_dec[par][:, 1:2]),
                         reads=[("S", h), ("g_dec", par)], writes=["g_Sbf"])

                def T1(tb):
                    par = tb % 2
                    S.op("pe", lambda e: e.transpose(out=psb[3][:, 0:128], in_=g_qd[par], identity=identb[:]), reads=[("g_qd", par), "identb"],
                         writes=[("ps", 3)], signal=False)
                    S.op("pe", lambda e: e.transpose(out=psb[3][:, 128:256], in_=g_kd[par], identity=identb[:]), reads=[("g_kd", par), "identb"],
                         writes=[("ps", 3)])
                    S.op("dve", lambda e: e.tensor_copy(out=g_qkT, in_=psb[3][:, 0:256]), reads=[("ps", 3)], writes=["g_qkT"])
                    S.op("dve", lambda e: e.tensor_scalar_mul(g_qeT[:, tb, :], psb[3][:, 0:128], g_ecol[par][:, 0:1]),
                         reads=[("ps", 3), ("g_ecol", par)], writes=["g_qeT"])

                def T2(tb):
                    S.op("pe", lambda e: e.matmul(ps[0][:, 256:384], lhsT=g_qkT[:, 128:256], rhs=g_qkT[:, 0:128], start=True, stop=True),
                         reads=["g_qkT"], writes=[("ps", 0)])
                    S.op("dve", lambda e: e.tensor_tensor(out=g_ATm, in0=ps[0][:, 256:384], in1=maskT, op=ALU.mult),
                         reads=[("ps", 0), "consts"], writes=["g_ATm"])

                def T3(tb):
                    par = tb % 2
                    ob, o_ap = o_store(tb)
                    S.op("pe", lambda e: e.matmul(o_ap, lhsT=g_ATm, rhs=g_vbf[par], start=False, stop=False, skip_group_check=True),
                         reads=["g_ATm", ("g_vbf", par)], writes=[("ps", ob)], signal=False)
                    S.op("pe", lambda e: e.matmul(o_ap, lhsT=g_qkT[:, 0:128], rhs=g_Sbf, start=False, stop=True, skip_group_check=True),
                         reads=["g_qkT", "g_Sbf"], writes=[("ps", ob)])
                    S.op("pe", lambda e: e.matmul(ps[3][:, 256:512], lhsT=g_kdec[par], rhs=g_vbf[par], start=True, stop=True),
                         reads=[("g_kdec", par), ("g_vbf", par)], writes=[("ps", 3)])
                    S.op("dve", lambda e: e.scalar_tensor_tensor(out=Sst[:, h, :], in0=Sst[:, h, :], scalar=g_dec[par][:, 0:1], in1=ps[3][:, 256:512],
                                                                  op0=ALU.mult, op1=ALU.add),
                         reads=[("ps", 3), ("g_dec", par), ("S", h), "g_Sbf"], writes=[("S", h)])

                S.op("dve", lambda e: e.memset(Sst[:, h, :], 0.0), writes=[("S", h)])
                S.op("dve", lambda e: e.memset(g_Bsn, 0.0), writes=["g_Bsn"])
                one_tile = (NTB == 1)
                def ld(r):
                    S.op("sp", lambda e: e.dma_start(out=g_Sb1[r % NSB], in_=st_gla[l, r, h]), writes=[("g_Sb", r % NSB)], dma=True)

                if ns:
                    M = NS
                    for r in range(min(NSB, M)):
                        ld(r)
                    Z(M, T)
                    for i in range(6):
                        P(i, M, T)
                    S.op("act", lambda e: e.activation(out=g_as[0:M, :], in_=g_lf[0:M, :], func=AF.Exp, scale=-sc), reads=["g_lf"], writes=["g_as"])
                    S.op("dve", lambda e: e.tensor_scalar_mul(g_qs[0:M, :], ps[0][0:M, 0:128], float(DK) ** -0.5),
                         reads=[("ps", 0)], writes=["g_qs"])
                    S.op("dve", lambda e: e.tensor_copy(out=g_ks[0:M, :], in_=ps[0][0:M, 128:256]), reads=[("ps", 0)], writes=["g_ks"])
                    S.op("act", lambda e: e.copy(out=g_vsb[0:M, :], in_=ps[1][0:M, 0:DV]), reads=[("ps", 1)], writes=["g_vs"])
                    S.op("act", lambda e: e.activation(out=g_sig[0:M, :], in_=ps[1][0:M, 256:512], func=AF.Exp, scale=-1.0), reads=[("ps", 1)], writes=["g_sig"])
                    S.op("act", lambda e: e.activation(out=g_sig[0:M, :], in_=g_sig[0:M, :], func=AF.Ln, bias=1.0), reads=["g_sig"], writes=["g_sig"])
                    S.op("act", lambda e: e.activation(out=g_sig[0:M, :], in_=g_sig[0:M, :], func=AF.Exp, scale=-1.0), reads=["g_sig"], writes=["g_sig"])
                    S.op("dve", lambda e: e.tensor_tensor(out=g_sg[0:M, :], in0=ps[1][0:M, 256:512], in1=g_sig[0:M, :], op=ALU.mult),
                         reads=[("ps", 1), "g_sig"], writes=["g_sg"])
                    S.op("dve", lambda e: e.tensor_tensor(out=g_e[0:M, :], in0=g_qs[0:M, :], in1=g_ks[0:M, :], op=ALU.mult),
                         reads=["g_qs", "g_ks", "g_lf"], writes=["g_e"])
                    S.op("dve", lambda e: e.reduce_sum(out=g_c[0:M, 0:1], in_=g_e[0:M, :], axis=mybir.AxisListType.X),
                         reads=["g_e"], writes=["g_c"])
                    S.op("dve", lambda e: e.tensor_scalar_mul(g_cv[0:M, :], ps[1][0:M, 0:DV], g_c[0:M, 0:1]),
                         reads=[("ps", 1), "g_c"], writes=["g_cv"])
                    S.op("dve", lambda e: e.tensor_tensor(out=g_qs[0:M, :], in0=g_qs[0:M, :], in1=g_as[0:M, :], op=ALU.mult),
                         reads=["g_qs", "g_as", "g_e"], writes=["g_qs"])
                    for i, src in enumerate((g_as, g_ks, g_qs)):
                        S.op("pe", lambda e, i=i, src=src: e.transpose(out=ps[2][:, 128 + i * M:128 + (i + 1) * M], in_=src[0:M, :],
                                                                       identity=ident[0:M, 0:M]),
                             reads=["g_as", "g_ks", "g_qs", "consts"], writes=[("ps", 2)], signal=(i == 2))
                    S.op("dve", lambda e: e.tensor_copy(out=g_akq[:, 0:3 * M], in_=ps[2][:, 128:128 + 3 * M]), reads=[("ps", 2)], writes=["g_akq"])
                    S.op("dve", lambda e: e.tensor_tensor(out=g_Qmb, in0=g_akq[:, None, 2 * M:3 * M].broadcast_to([128, M, M]), in1=idb16, op=ALU.mult),
                         reads=["g_akq", "consts"], writes=["g_Qm"])
                    S.op("dve", lambda e: e.tensor_tensor(out=g_Kmb, in0=g_ks[0:M, None, :].broadcast_to([M, M, 128]),
                                                           in1=ident[0:M, 0:M, None].broadcast_to([M, M, 128]), op=ALU.mult),
                         reads=["g_ks", "consts"], writes=["tA", "tB"])
                for b in range(4, 8):
                    S.op("dve", lambda e, b=b: e.memset(ps[b][:, :], 0.0), writes=[("ps", b)])
                Z(128, 0)
                P(0, 128, 0, one_tile)
                cumsum(0); gates(0)
                P(1, 128, 0, one_tile); P(2, 128, 0, one_tile); P(3, 128, 0, one_tile)
                qkd(0)
                P(4, 128, 0, one_tile); P(5, 128, 0, one_tile)
                vg(0); Sbf(0)
                for tb in range(NTB):
                    n = tb + 1
                    has = n < NTB
                    last = (n == NTB - 1)
                    c0 = n * 128
                    if has:
                        Z(128, c0)
                        P(0, 128, c0, last)
                    T1(tb)
                    if has:
                        cumsum(n); gates(n)
                        P(1, 128, c0, last)
                    T2(tb)
                    if has:
                        P(2, 128, c0, last); P(3, 128, c0, last)
                    T3(tb)
                    if has:
                        qkd(n)
                        P(4, 128, c0, last); P(5, 128, c0, last)
                        vg(n); Sbf(n)
                S.op("act", lambda e: e.activation(out=g_Etot, in_=g_Bsn, func=AF.Exp), reads=["g_Bsn"], writes=["g_Etot"])
                S.op("sp", lambda e, h=h: e.dma_start(out=ibS[l][h].ap(), in_=Sst[:, h, :]), reads=[("S", h)], writes=[("ibS", l, h)], dma=True)
                S.op("pool", lambda e, h=h: e.collective_compute("AllGather", ALU.bypass, replica_groups=PAIRS,
                                                                  ins=[ibS[l][h].ap().opt()], outs=[obS[l][h].ap().opt()]),
                     reads=[("ibS", l, h)], writes=[("obS", l, h)], cc=ccsems[l * H + h])
                if ns:
                    M = NS

                    def dS(r):
                        bk = 3 if r % 2 == 0 else 0
                        S.op("pe", lambda e: e.matmul(ps[bk][:, 256:512], lhsT=g_Kmb[:, r, :], rhs=g_vsb[0:M, :], start=True, stop=True),
                             reads=["tA", "tB", "g_vs"], writes=[("ps", bk)])

                    def wb(r):
                        S.op("act", lambda e: e.dma_start(out=gla_s[l, r, h], in_=g_Sb1[r % NSB]), reads=[("g_Sb", r % NSB)], dma=True)
                        if r + NSB < M:
                            ld(r + NSB)

                    dS(0)
                    for r in range(M):
                        bk = 3 if r % 2 == 0 else 0
                        buf = g_Sb1[r % NSB]
                        bres = ("g_Sb", r % NSB)
                        S.op("act", lambda e, buf=buf, r=r: e.copy(out=g_Sbb[r % 2], in_=buf), reads=[bres], writes=[("g_Sbb", r % 2)])
                        S.op("dve", lambda e, buf=buf, bk=bk, r=r: e.scalar_tensor_tensor(
                            out=buf, in0=buf, scalar=g_akq[:, r:r + 1], in1=ps[bk][:, 256:512], op0=ALU.mult, op1=ALU.add),
                            reads=[("ps", bk), "g_akq", bres], writes=[bres])
                        if r + 1 < M:
                            dS(r + 1)
                        S.op("pe", lambda e, r=r: e.matmul(ps[2][0:M, 256:512], lhsT=g_Qmb[:, r, :], rhs=g_Sbb[r % 2],
                                                           start=(r == 0), stop=(r == M - 1)),
                             reads=["g_Qm", ("g_Sbb", r % 2)], writes=[("ps", 2)])
                        if r >= 1:
                            wb(r - 1)
                        if r == max(0, M - NSB):
                            S.op("sp", lambda e: e.dma_start(out=g_SA, in_=obS[l][h].ap()[0:128, :]), reads=[("obS", l, h)],
                                 writes=["g_SA"], dma=True)
                    wb(M - 1)
                    S.op("dve", lambda e: e.tensor_tensor(out=g_cv[0:M, :], in0=g_cv[0:M, :], in1=ps[2][0:M, 256:512], op=ALU.add),
                         reads=[("ps", 2), "g_cv"], writes=["g_cv"])
                    gla_post(h, l, M, T, g_cv[0:M, :], 2, g_sg[0:M, :], 0, 3, ores="g_cv")
                if not ns:
                    S.op("sp", lambda e, h=h: e.dma_start(out=g_SA, in_=obS[l][h].ap()[0:128, :]), reads=[("obS", l, h)], writes=["g_SA"], dma=True)
                S.op("dve", lambda e: e.tensor_scalar_mul(g_SA, g_SA, flag[:, 0:1]), reads=["g_SA", "flag"], writes=["g_SA"])
                S.op("dve", lambda e: e.tensor_copy(out=g_SAb, in_=g_SA), reads=["g_SA"], writes=["g_SAb"])
                for tb in range(NTB):
                    ob, o_ap = o_store(tb)
                    S.op("pe", lambda e, tb=tb, o_ap=o_ap: e.matmul(o_ap, lhsT=g_qeT[:, tb, :], rhs=g_SAb, start=False, stop=True, skip_group_check=True),
                         reads=["g_qeT", "g_SAb"], writes=[("ps", ob)])
                S.op("dve", lambda e: e.memset(g_ss8[:, 0:NTB], 0.0), writes=["g_ss8"])
                for tb in range(NTB):
                    ob, o_ap = o_store(tb)
                    S.op("act", lambda e, tb=tb, o_ap=o_ap: e.activation(out=g_junk, in_=o_ap, func=AF.Square, accum_out=g_ss8[:, tb:tb + 1]),
                         reads=[("ps", ob), "g_ss8"], writes=["g_junk", ("g_ss8", tb)])
                    S.op("act", lambda e, tb=tb: e.activation(out=g_rs8[:, tb:tb + 1], in_=g_ss8[:, tb:tb + 1], func=AF.Ln, scale=1.0 / DV, bias=EPS),
                         reads=[("g_ss8", tb)], writes=[("g_rs8", tb)])
                    S.op("act", lambda e, tb=tb: e.activation(out=g_rs8[:, tb:tb + 1], in_=g_rs8[:, tb:tb + 1], func=AF.Exp, scale=-0.5),
                         reads=[("g_rs8", tb)], writes=[("g_rs8", tb)])

                def evac(tb):
                    tbank = 3 if tb % 2 == 0 else 0
                    for vc in range(2):
                        ch = CC + 2 * h + vc
                        S.op("dve", lambda e, vc=vc, ch=ch: e.tensor_scalar_mul(yT[:, ch, tb * 128:(tb + 1) * 128],
                                                                                 psb[tbank][:, 256 + vc * 128:384 + vc * 128],
                                                                                 vcol("gng", l, 2 * h + vc)),
                             reads=[("ps", tbank), "vec"], writes=[("yT", ch)])

                for tb in range(NTB):
                    ob, o_ap = o_store(tb)
                    par = tb % 2
                    tbank = 3 if par == 0 else 0
                    S.op("dve", lambda e, tb=tb, o_ap=o_ap, par=par: e.scalar_tensor_tensor(
                        out=g_yb[par], in0=o_ap, scalar=g_rs8[:, tb:tb + 1], in1=g_sgs[:, tb, :], op0=ALU.mult, op1=ALU.mult),
                        reads=[("ps", ob), ("g_rs8", tb), "g_sgs"], writes=[("g_yb", par)])
                    for vc in range(2):
                        S.op("pe", lambda e, vc=vc, par=par, tbank=tbank: e.transpose(out=psb[tbank][:, 256 + vc * 128:384 + vc * 128],
                                                                                     in_=g_yb[par][:, vc * 128:(vc + 1) * 128], identity=identb[:]),
                             reads=[("g_yb", par), "identb"], writes=[("ps", tbank)], signal=(vc == 1))
                    if tb >= 1:
                        evac(tb - 1)
                evac(NTB - 1)
                S.op("dve", lambda e, h=h: e.scalar_tensor_tensor(out=Sst[:, h, :], in0=g_SA, scalar=g_Etot[:, 0:1], in1=Sst[:, h, :],
                                                                   op0=ALU.mult, op1=ALU.add),
                     reads=["g_SA", "g_Etot", ("S", h)], writes=[("S", h)])
        def xchg_x(idx):
            S.op("dve", lambda e: e.tensor_copy(out=xhs[:], in_=xT[:, :, T - 2:T]), reads=[("xT", k) for k in range(KC)], writes=["xhs"])
            S.op("sp", lambda e: e.dma_start(out=ibX[idx].ap(), in_=xhs[:].rearrange("p k t -> p (k t)")), reads=["xhs"],
                 writes=[("ibX", idx)], dma=True)
            S.op("pool", lambda e: e.collective_compute("AllGather", ALU.bypass, replica_groups=PAIRS,
                                                         ins=[ibX[idx].ap().opt()], outs=[obX[idx].ap().opt()]),
                 reads=[("ibX", idx)], writes=[("obX", idx)], cc=ccsems[L * H + idx])
            S.op("sp", lambda e: e.dma_start(out=xhr[:].rearrange("p k t -> p (k t)"), in_=obX[idx].ap()[0:128, :]), reads=[("obX", idx)],
                 writes=["xhr"], dma=True)
            S.op("dve", lambda e: e.tensor_scalar_mul(xT[:, :, T + NS:T + NX], xhr[:], flag[:, 0:1]), reads=["xhr", "flag"],
                 writes=[("xT", k) for k in range(KC)])

        def out_proj(l, groups):
            for oc in range(KC):
                u, slot = ws.get([wunit(w_out[l], oc * 128)])
                bk = BX if oc % 2 == 0 else BY
                proj_fm(slot, u, KC, yT, lambda k: [("yT", k)], groups, bk)
                ws.release(u)
                for gi, (c0, n) in enumerate(groups):
                    S.op("dve", lambda e, o=xT[:, oc, c0:c0 + n], i=ps[bk[gi]][:, 0:n]: e.tensor_tensor(out=o, in0=o, in1=i, op=ALU.add),
                         reads=[("ps", bk[gi])], writes=[("xT", oc)])

        def ffn(p, l, ns, groups, TTp):
            wup, wdn = w_up[l], w_down[l]
            for g in range(NG):
                if ns:
                    for r in range(2):
                        S.op("sp", lambda e, r=r, g=g: e.dma_start(out=stg_in[r * NS:(r + 1) * NS, 0:FG * 128],
                                                                    in_=st_ffn[l, :, r, g * FG * 128:(g + 1) * FG * 128]),
                             writes=["stg_in"], dma=True)
                    for j in range(FG):
                        S.op("pe", lambda e, j=j: e.transpose(out=ps[7][:, j * 32:(j + 1) * 32], in_=stg_in[0:32, j * 128:(j + 1) * 128],
                                                              identity=ident[0:32, 0:32]),
                             reads=["stg_in", "consts"], writes=[("ps", 7)], signal=(j == FG - 1))
                    S.op("dve", lambda e: e.tensor_copy(out=fst[:, 0:FG, :], in_=ps[7][:, 0:FG * 32].rearrange("p (j c) -> p j c", c=32)),
                         reads=[("ps", 7)], writes=["fst"])
                for j in range(FG):
                    fc = g * FG + j
                    u, slot = ws.get([wunit(wup, fc * 128)])
                    proj_fm(slot, u, KC, xnT, xn_res, groups, BX)
                    ws.release(u)
                    for gi, (c0, n) in enumerate(groups):
                        S.op("act", lambda e, o=tA[:, 2 + c0:2 + c0 + n], i=ps[BX[gi]][:, 0:n]: e.copy(out=o, in_=i),
                             reads=[("ps", BX[gi])], writes=["tA"])
                    S.op("act", lambda e: e.copy(out=tA[:, 0:2], in_=tA[:, 2 + T + NS:2 + T + NX]), reads=["tA"], writes=["tA"])
                    w0, w1, w2 = vcol("fcw", l, fc), vcol("fcw", l, FC + fc), vcol("fcw", l, 2 * FC + fc)
                    S.op("act", lambda e, w2=w2, fc=fc: e.activation(out=tB[:, 0:TTp], in_=tA[:, 2:2 + TTp], func=AF.Identity,
                                                                     scale=w2, bias=vcol("fcb", l, fc)),
                         reads=["tA", "vec"], writes=["tB"])
                    S.op("dve", lambda e, w1=w1: e.scalar_tensor_tensor(out=tB[:, 0:T], in0=tA[:, 1:1 + T], scalar=w1, in1=tB[:, 0:T],
                                                                         op0=ALU.mult, op1=ALU.add), reads=["tA", "tB", "vec"], writes=["tB"])
                    S.op("dve", lambda e, w0=w0: e.scalar_tensor_tensor(out=tB[:, 0:T], in0=tA[:, 0:T], scalar=w0, in1=tB[:, 0:T],
                                                                         op0=ALU.mult, op1=ALU.add), reads=["tA", "tB", "vec"], writes=["tB"])
                    if ns:
                        S.op("dve", lambda e, w1=w1, j=j: e.scalar_tensor_tensor(out=tB[:, T:T + ns], in0=fst[:, j, ns:2 * ns], scalar=w1,
                                                                                  in1=tB[:, T:T + ns], op0=ALU.mult, op1=ALU.add),
                             reads=["fst", "tB", "vec"], writes=["tB"])
                        S.op("dve", lambda e, w0=w0, j=j: e.scalar_tensor_tensor(out=tB[:, T:T + ns], in0=fst[:, j, 0:ns], scalar=w0,
                                                                                  in1=tB[:, T:T + ns], op0=ALU.mult, op1=ALU.add),
                             reads=["fst", "tB", "vec"], writes=["tB"])
                    S.op("act", lambda e: e.activation(out=tC[:, 0:TTp], in_=tB[:, 0:TTp], func=AF.Silu), reads=["tB"], writes=["tC"])
                    S.op("dve", lambda e, j=j: e.tensor_copy(out=nst[:, j, 0:2], in_=tA[:, T:T + 2]), reads=["tA"], writes=["nst"])
                    if ns:
                        S.op("dve", lambda e, j=j: e.tensor_copy(out=nst[:, j, 2:2 + ns], in_=fst[:, j, ns:2 * ns]), reads=["fst"], writes=["nst"])
                        S.op("dve", lambda e, j=j: e.tensor_copy(out=nst[:, j, 2 + ns:2 + 2 * ns], in_=tA[:, 2 + T:2 + T + ns]),
                             reads=["tA"], writes=["nst"])
                    u, slot = ws.get([wunit(wup, DFF + fc * 128)])
                    proj_fm(slot, u, KC, xnT, xn_res, groups, BY)
                    ws.release(u)
                    for gi, (c0, n) in enumerate(groups):
                        S.op("dve", lambda e, o=yT[:, j, c0:c0 + n], a=tC[:, c0:c0 + n], i=ps[BY[gi]][:, 0:n]:
                             e.tensor_tensor(out=o, in0=a, in1=i, op=ALU.mult),
                             reads=[("ps", BY[gi]), "tC"], writes=[("yT", j)])
                cs = slice(g * FG * 128, (g + 1) * FG * 128)
                state_out(FG, ns, ffn_p[p, l, :, cs], ffn_s[l, :, 0, cs], ffn_s[l, :, 1, cs])
                for oc in range(KC):
                    u, slot = ws.get([(FG, 0, 128, wdn[g * FG * 128:(g + 1) * FG * 128, oc * 128:(oc + 1) * 128]
                                       .rearrange("(k p) n -> p k n", p=128))])
                    bk = BX if oc % 2 == 0 else BY
                    proj_fm(slot, u, FG, yT, lambda k: [("yT", k)], groups, bk)
                    ws.release(u)
                    for gi, (c0, n) in enumerate(groups):
                        S.op("dve", lambda e, o=xT[:, oc, c0:c0 + n], i=ps[bk[gi]][:, 0:n]: e.tensor_tensor(out=o, in0=o, in1=i, op=ALU.add),
                             reads=[("ps", bk[gi])], writes=[("xT", oc)])

        def final_out(p, ns, groups, TTp):
            rms_stats(groups, TTp)
            for dc in range(KC):
                S.op("dve", lambda e, o=xT[:, dc, 0:TTp], g=vcol("fng", 0, dc):
                     e.scalar_tensor_tensor(out=o, in0=o, scalar=g, in1=tC[:, 0:TTp], op0=ALU.mult, op1=ALU.mult),
                     reads=["tC", "vec"], writes=[("xT", dc)])
            tiles = [(tb * 128, 128, y_p[p * T + tb * 128:p * T + (tb + 1) * 128, :]) for tb in range(NTB)]
            if ns:
                tiles.append((T, ns, y_s))
            for ti, (c0, M, dst) in enumerate(tiles):
                ob, ores = xin[ti % len(xin)]
                for q in range(4):
                    for k in range(4):
                        dc = 4 * q + k
                        S.op("pe", lambda e, q=q, k=k, dc=dc, c0=c0, M=M: e.transpose(out=ps[q][0:M, k * 128:(k + 1) * 128], in_=xT[:, dc, c0:c0 + M],
                                                                           identity=ident),
                             reads=[("xT", dc), "consts"], writes=[("ps", q)], signal=(k == 3))
                    eng = "act" if q % 2 == 0 else "dve"
                    if eng == "act":
                        S.op("act", lambda e, q=q, M=M, ob=ob: e.copy(out=ob[0:M, q * 512:(q + 1) * 512], in_=ps[q][0:M, :]),
                             reads=[("ps", q)], writes=ores)
                    else:
                        S.op("dve", lambda e, q=q, M=M, ob=ob: e.tensor_copy(out=ob[0:M, q * 512:(q + 1) * 512], in_=ps[q][0:M, :]),
                             reads=[("ps", q)], writes=ores)
                S.op("sp", lambda e, dst=dst, M=M, ob=ob: e.dma_start(out=dst, in_=ob[0:M, :]), reads=ores, dma=True)

        def init():
            S.op("sp", lambda e: e.dma_start(out=cst_t[:], in_=consts_d), writes=["consts"], dma=True)
            S.op("sp", lambda e: e.dma_start(out=tabc[:, 0:cfg.VB * 128].rearrange("p (b c) -> p b c", c=128),
                                             in_=vecs_d.rearrange("(b r) c -> r b c", r=128)), writes=["tA", "tB"], dma=True)
            for l in range(L):
                S.op("pool", lambda e, l=l: e.dma_start(out=gw[0:GR + 1, l, :], in_=gw_d[l]), writes=["gw"], dma=True)
            S.op("sp", lambda e: e.dma_start(out=flag[:], in_=flag_d), writes=["flag"], dma=True)
            S.op("dve", lambda e: e.tensor_copy(out=identb[:], in_=ident), reads=["consts"], writes=["identb"])
            S.op("dve", lambda e: e.memset(onesb[:], 1.0), writes=["onesb"])
            for b in range(cfg.VB):
                S.op("pe", lambda e, b=b: e.transpose(out=ps[b % 4][:, 0:128], in_=tabc[:, b * 128:(b + 1) * 128], identity=ident),
                     reads=["tA", "tB", "consts"], writes=[("ps", b % 4)])
                S.op("act", lambda e, b=b: e.copy(out=vec[:, b * 128:(b + 1) * 128], in_=ps[b % 4][:, 0:128]),
                     reads=[("ps", b % 4)], writes=["vec"])

        def program():
            init()
            for p in range(NPASS):
                ns = NS if p == 0 else 0
                TTp = T + (NX if ns else 0)
                groups = colgroups(ns)
                S.barrier()
                S.op("dve", lambda e: e.memset(glrT[:], 1.0), writes=["glrT"])
                load_x(p, ns)
                for l in range(L):
                    S.barrier()
                    if l == 0:
                        S.op("dve", lambda e: e.memset(yT[:, :, T:TT], 0.0), writes=[("yT", k) for k in range(KC)])
                    rms_norm("nmg", l, groups, TTp)
                    S.barrier()
                    mixer_conv(p, l, ns, groups, TTp)
                    S.barrier()
                    mixer_gla(p, l, ns)
                    out_proj(l, groups)
                    xchg_x(2 * l)
                    S.barrier()
                    rms_norm("nfg", l, groups, TTp)
                    S.barrier()
                    ffn(p, l, ns, groups, TTp)
                    if l + 1 < L:
                        xchg_x(2 * l + 1)
                S.barrier()
                final_out(p, ns, groups, TTp)

        S.dry = True
        program()
        S.dry = False
        ws.i = 0
        ws.start()
        program()

        with nc.Block() as block:
            @block.tensor
            def _(e):
                S.replay("pe", e)

            @block.scalar
            def _(e):
                S.replay("act", e)

            @block.vector
            def _(e):
                S.replay("dve", e)

            @block.gpsimd
            def _(e):
                S.replay("pool", e)

            @block.sync
            def _(e):
                S.replay("sp", e)
                S.final_wait(e)
    return nc


def make_consts():
    c = np.zeros((128, 770), np.float32)
    j = np.arange(128)[:, None]
    i = np.arange(128)[None, :]
    c[:, 0:128] = np.eye(128, dtype=np.float32)
    c[:, 128:256] = (j <= i).astype(np.float32) - (j <= 63).astype(np.float32)
    c[:, 256:384] = (j > i).astype(np.float32)
    c[:, 384:512] = (j <= i).astype(np.float32)
    c[:, 512] = 1.0
    c[:, 513] = (np.arange(128) <= 63).astype(np.float32)
    c[:, 514:770] = np.eye(16, dtype=np.float32).reshape(1, 256)
    return c


def pack_vecs(cfg, inp):
    rows = []
    for l in range(L):
        rows.append(inp["norm_mix_g"][l].reshape(16, 128))
        rows.append(inp["conv_w"][l].reshape(24, 128))
        rows.append(inp["gla_norm_g"][l].reshape(8, 128))
        rows.append(inp["norm_ffn_g"][l].reshape(16, 128))
        rows.append(inp["ffn_conv_w"][l].reshape(3 * cfg.FC, 128))
        rows.append(inp["ffn_conv_b"][l].reshape(cfg.FC, 128))
    rows.append(inp["final_norm_g"].reshape(16, 128))
    v = np.concatenate(rows, axis=0).astype(np.float32)
    out = np.zeros((cfg.VB * 128, 128), np.float32)
    out[:v.shape[0]] = v
    return out


_NC_CACHE = {}


def run(cfg, inp, ncores):
    key = (cfg.T, cfg.NPASS, cfg.DFF, cfg.NG)
    if key not in _NC_CACHE:
        _NC_CACHE[key] = build_nc(cfg)
    nc = _NC_CACHE[key]
    T = cfg.T
    consts = make_consts()
    vecs = pack_vecs(cfg, inp)
    gwp = np.ascontiguousarray(np.concatenate([inp["gate_w2"], inp["gate_b"][:, None, :]], axis=1), dtype=np.float32)
    shared = {"w_in": inp["w_in"], "w_out": inp["w_out"], "w_up": inp["w_up"], "w_down": inp["w_down"],
              "gw": gwp, "vecs": vecs, "consts": consts}
    in_maps = []
    for c in range(ncores):
        m = dict(shared)
        sl = slice(c * NS, (c + 1) * NS)
        b, half = c // 2, c % 2
        m["x_p"] = np.ascontiguousarray(inp["x_prompt"][b, half * T:(half + 1) * T])
        m["x_h"] = (np.ascontiguousarray(inp["x_prompt"][b, T - NH:T]) if half else np.zeros((NH, D), np.float32))
        fl = np.zeros((128, 2), np.float32)
        fl[:, 0] = float(half)
        fl[:, 1] = 1.0 - float(half)
        m["flag"] = fl
        m["x_s"] = np.ascontiguousarray(inp["x_sample"][sl, 0, :])
        m["st_conv"] = np.ascontiguousarray(inp["state_conv"][:, sl])
        m["st_gla"] = np.ascontiguousarray(inp["state_gla"][:, sl])
        m["st_ffn"] = np.ascontiguousarray(inp["state_ffn_conv"][:, sl])
        in_maps.append(m)
    res = run_bass_kernel_spmd(nc, in_maps, core_ids=list(range(ncores)))
    return res.results


def assemble(cfg, R, ncores):
    nb = ncores // 2
    y_prompt = np.stack([np.concatenate([R[2 * b]["y_p"], R[2 * b + 1]["y_p"]], 0) for b in range(nb)], 0)
    y_sample = np.concatenate([R[c]["y_s"] for c in range(ncores)], 0)[:, None, :]
    conv_p = np.stack([R[2 * b + 1]["conv_p"][0] for b in range(nb)], 1)
    gla_p = np.stack([R[2 * b + 1]["gla_p"][0] for b in range(nb)], 1)
    ffn_p = np.stack([R[2 * b + 1]["ffn_p"][0] for b in range(nb)], 1)
    conv_s = np.concatenate([R[c]["conv_s"] for c in range(ncores)], 1)
    gla_s = np.concatenate([R[c]["gla_s"] for c in range(ncores)], 1)
    ffn_s = np.concatenate([R[c]["ffn_s"] for c in range(ncores)], 1)
    return tuple(np.ascontiguousarray(a, dtype=np.float32) for a in
                 (y_prompt, y_sample, conv_p, gla_p, ffn_p, conv_s, gla_s, ffn_s))


def kernel(**inputs):
    inp = {k: np.asarray(v) for k, v in inputs.items()}
    cfg = Cfg()
    R = run(cfg, inp, 8)
    return assemble(cfg, R, 8)
```

```python
import numpy as np
from contextlib import ExitStack
import concourse.bass as bass
import concourse.mybir as mybir
from concourse.bass_utils import run_bass_kernel_spmd

F32 = mybir.dt.float32
BF16 = mybir.dt.bfloat16
ALU = mybir.AluOpType
AF = mybir.ActivationFunctionType

ENGS = ("pe", "act", "dve", "pool", "sp")

D = 2048
KC = 16
L = 2
CD = 1024
CC = 8
H = 4
DK = 128
DV = 256
GR = 16
S1, S2, S3 = 1024, 2048, 3072
S4, S5, S6, S7 = 3584, 4096, 5120, 6144
INC = 6160
EPS = 1e-6
NS = 16
NH = 2
NX = NS + NH
NSLOT = 6


class Cfg:
    def __init__(self, T=1024, NPASS=1, DFF=5632, NG=4):
        self.T, self.NPASS, self.DFF, self.NG = T, NPASS, DFF, NG
        self.FC = DFF // 128
        self.FG = self.FC // NG
        self.SEQ = T * NPASS
        self.TT = T + NX
        self.NTB = T // 128
        o = 0
        self.off = {}
        for l in range(L):
            for name, n in (("nmg", 16), ("cw", 24), ("gng", 8), ("nfg", 16),
                            ("fcw", 3 * self.FC), ("fcb", self.FC)):
                self.off[(name, l)] = o
                o += n
        self.off[("fng", 0)] = o
        o += 16
        self.VR = o
        self.VB = (o + 127) // 128


class Sched:
    def __init__(self, sems, dma_sems):
        self.sem = dict(zip(ENGS, sems))
        self.cnt = {e: 0 for e in ENGS}
        self.ops = {e: [] for e in ENGS}
        self.waited = {e: {} for e in ENGS}
        self.res = {}
        self.dma_sems = {k: list(v) for k, v in dma_sems.items()}
        self.dma_val = {k: [0] * len(v) for k, v in self.dma_sems.items()}
        self.dma_rr = {k: 0 for k in self.dma_sems}
        self.dry = False

    def _need(self, eng, h, waits):
        if h is None:
            return
        sem, val, heng = h
        if heng == "pe" and eng == "pe":
            return
        k = id(sem)
        if self.waited[eng].get(k, 0) >= val:
            return
        prev = waits.get(k)
        if prev is None or prev[1] < val:
            waits[k] = (sem, val)

    def op(self, eng, fn, reads=(), writes=(), signal=True, dma=False, cc=None):
        if self.dry:
            return None
        waits = {}
        for r in reads:
            ent = self.res.get(r)
            if ent is not None:
                self._need(eng, ent[0], waits)
        for w in writes:
            ent = self.res.get(w)
            if ent is not None:
                self._need(eng, ent[0], waits)
                for rh in ent[1]:
                    self._need(eng, rh, waits)
        if dma:
            i = self.dma_rr[eng]
            self.dma_rr[eng] = (i + 1) % len(self.dma_sems[eng])
            dsem = self.dma_sems[eng][i]
            if self.dma_val[eng][i] > 0:
                self._need(eng, (dsem, self.dma_val[eng][i], "dma"), waits)
            self.dma_val[eng][i] += 16
            handle = (dsem, self.dma_val[eng][i], "dma")
            inc = (dsem, 16)
        elif cc is not None:
            handle = (cc, 1, "cc")
            inc = (cc, None)
        elif signal:
            self.cnt[eng] += 1
            handle = (self.sem[eng], self.cnt[eng], eng)
            inc = (self.sem[eng], 1)
        else:
            handle = (self.sem[eng], self.cnt[eng] + 1, eng)
            inc = None
        wl = list(waits.values())
        for sem, val in wl:
            self.waited[eng][id(sem)] = val
        self.ops[eng].append((wl, fn, inc))
        for r in reads:
            self.res.setdefault(r, [None, []])[1].append(handle)
        for w in writes:
            self.res[w] = [handle, []]
        return handle

    def barrier(self, engs=("pe", "act", "dve", "sp")):
        if self.dry:
            return
        for e in engs:
            wl = []
            for o in ENGS:
                if o == e or self.cnt[o] == 0:
                    continue
                if o == "pool":
                    continue
                if self.waited[e].get(id(self.sem[o]), 0) < self.cnt[o]:
                    wl.append((self.sem[o], self.cnt[o]))
                    self.waited[e][id(self.sem[o])] = self.cnt[o]
            for q in ("sp", "act"):
                for i, dsem in enumerate(self.dma_sems[q]):
                    v = self.dma_val[q][i]
                    if v > 0 and self.waited[e].get(id(dsem), 0) < v:
                        wl.append((dsem, v))
                        self.waited[e][id(dsem)] = v
            if wl:
                self.ops[e].append((wl, None, None))

    def replay(self, eng, e):
        for wl, fn, inc in self.ops[eng]:
            for sem, val in wl:
                e.wait_ge(sem, val)
            if fn is None:
                continue
            ins = fn(e)
            if inc is not None:
                if inc[1] is None:
                    ins.then_inc(inc[0])
                else:
                    ins.then_inc(inc[0], inc[1])

    def final_wait(self, e):
        for q in self.dma_sems:
            for i, dsem in enumerate(self.dma_sems[q]):
                if self.dma_val[q][i] > 0:
                    e.wait_ge(dsem, self.dma_val[q][i])
        for en in ENGS:
            if self.cnt[en] > 0:
                e.wait_ge(self.sem[en], self.cnt[en])


def build_nc(cfg):
    T, TT, NPASS, DFF, FC, FG, NG, NTB = cfg.T, cfg.TT, cfg.NPASS, cfg.DFF, cfg.FC, cfg.FG, cfg.NG, cfg.NTB
    SEQ = cfg.SEQ
    nc = bass.Bass("TRN2", target_bir_lowering=False)

    def din(name, shape):
        return nc.dram_tensor(name, list(shape), F32, kind="ExternalInput").ap()

    def dout(name, shape):
        return nc.dram_tensor(name, list(shape), F32, kind="ExternalOutput").ap()

    x_p = din("x_p", [SEQ, D]); x_s = din("x_s", [NS, D]); x_h = din("x_h", [NH, D])
    flag_d = din("flag", [128, 2])
    st_conv = din("st_conv", [L, NS, 2, CD]); st_gla = din("st_gla", [L, NS, H, DK, DV])
    st_ffn = din("st_ffn", [L, NS, 2, DFF])
    w_in = din("w_in", [L, D, INC]); w_out = din("w_out", [L, D, D])
    w_up = din("w_up", [L, D, 2 * DFF]); w_down = din("w_down", [L, DFF, D])
    gw_d = din("gw", [L, GR + 1, H * DK])
    vecs_d = din("vecs", [cfg.VB * 128, 128])
    consts_d = din("consts", [128, 770])
    y_p = dout("y_p", [SEQ, D]); y_s = dout("y_s", [NS, D])
    conv_p = dout("conv_p", [NPASS, L, 2, CD]); gla_p = dout("gla_p", [NPASS, L, H, DK, DV])
    ffn_p = dout("ffn_p", [NPASS, L, 2, DFF])
    conv_s = dout("conv_s", [L, NS, 2, CD]); gla_s = dout("gla_s", [L, NS, H, DK, DV])
    ffn_s = dout("ffn_s", [L, NS, 2, DFF])

    es = ExitStack()
    with es:
        def sb(name, shape, dt=F32):
            return es.enter_context(nc.sbuf_tensor(name, list(shape), dt))

        xT = sb("xT", [128, KC, TT]); xnT = sb("xnT", [128, KC, TT], BF16)
        yT = sb("yT", [128, KC, TT], BF16)
        wsl = [sb("wsl%d" % i, [128, KC, 128], BF16) for i in range(NSLOT)]
        EW = TT + 2
        tabc_n = max(3 * EW, 2048 + EW)
        tabc = sb("tabc", [128, tabc_n])
        tA = tabc[:, 0:EW]; tB = tabc[:, EW:2 * EW]; tC = tabc[:, tabc_n - EW:tabc_n]
        io = tabc[:, 0:2048]
        nxb = (KC * TT // 2) // 2048
        if nxb >= 1:
            yflat = yT[:].rearrange("p k t -> p (k t)").bitcast(F32)
            xin = [(yflat[:, i * 2048:(i + 1) * 2048], [("xin", i)]) for i in range(nxb)]
        else:
            xin = [(io, ["tA", "tB"])]
        sqv = tB.bitcast(BF16)
        SCR = 7016
        scr = sb("scr", [128, SCR])
        cst_t = sb("consts_sb", [128, 770])
        ident = cst_t[:, 0:128]; Uc = cst_t[:, 128:256]; Lm = cst_t[:, 256:384]
        maskT = cst_t[:, 384:512]; cmat = cst_t[:, 512:514]
        idb16 = cst_t[:, 514:770].rearrange("p (r t) -> p r t", r=16)
        identb = sb("identb", [128, 128], BF16); onesb = sb("onesb", [128, 128], BF16)
        vec = sb("vec_sb", [128, cfg.VB * 128])
        gw = sb("gw_sb", [32, L, H * DK], BF16)
        glrT = sb("glrT", [32, TT], BF16)
        Sst = sb("Sst", [128, H, DV])
        ps = [es.enter_context(nc.psum_tensor("ps%d" % i, [128, 512], F32)) for i in range(8)]
        psb = [p[:].bitcast(BF16) for p in ps]
        sems = [es.enter_context(nc.semaphore("s_%s" % e)) for e in ENGS]
        dsems = {"sp": [es.enter_context(nc.semaphore("dsp_%d" % i)) for i in range(16)],
                 "pool": [es.enter_context(nc.semaphore("dpl_%d" % i)) for i in range(8)],
                 "act": [es.enter_context(nc.semaphore("dac_%d" % i)) for i in range(8)]}
        S = Sched(sems, dsems)
        ibS = [[nc.dram_tensor("ibS_%d_%d" % (l, h), [128, DV], F32, kind="Internal") for h in range(H)] for l in range(L)]
        obS = [[nc.dram_tensor("obS_%d_%d" % (l, h), [256, DV], F32, kind="Internal") for h in range(H)] for l in range(L)]
        ibX = [nc.dram_tensor("ibX_%d" % i, [128, 2 * KC], F32, kind="Internal") for i in range(2 * L)]
        obX = [nc.dram_tensor("obX_%d" % i, [256, 2 * KC], F32, kind="Internal") for i in range(2 * L)]
        ccsems = [es.enter_context(nc.semaphore("cc_%d" % i)) for i in range(L * H + 2 * L)]
        PAIRS = [[0, 1], [2, 3], [4, 5], [6, 7]]
        flag = sb("flag_sb", [128, 2]); xhs = sb("xhs", [128, KC, 2]); xhr = sb("xhr", [128, KC, 2])

        def sv(off, n, dt=F32, parts=128):
            v = scr[0:parts, off:off + n]
            return v.bitcast(dt) if dt != F32 else v
        g_e = sv(0, 128); g_lf = sv(128, 128); g_eb = sv(256, 128); g_enb = sv(384, 128)
        g_erev = sv(512, 128); g_dec = sv(640, 2); g_ss = sv(642, 2); g_rs = sv(644, 2)
        g_qd = sv(648, 64, BF16); g_kd = sv(712, 64, BF16); g_kdec = sv(776, 64, BF16)
        g_vbf = sv(840, 128, BF16); g_Sbf = sv(968, 128, BF16); g_qkT = sv(1096, 128, BF16)
        g_ATm = sv(1224, 64, BF16); g_sg = sv(1288, 256); g_yb = sv(1544, 128, BF16)
        g_Qm = sv(1672, 256).rearrange("p (r t) -> p r t", r=16)
        g_akq = sv(1928, 48); g_qs = sv(1976, 128); g_ks = sv(2104, 128); g_as = sv(2232, 128)
        g_vs = sv(2360, 256)
        g_Sb = [sv(2616 + i * 512, 512).rearrange("p (r v) -> p r v", r=2) for i in range(2)]
        g_Km = tabc[0:16, 0:2048].rearrange("p (r k) -> p r k", r=16)
        g_qeT = sv(3640, 64 * 8, BF16).rearrange("p (b t) -> p b t", t=128)
        g_sgs = sv(4152, 128 * 8, BF16).rearrange("p (b v) -> p b v", v=DV)
        g_SA = sv(5176, 256); g_SAb = sv(5432, 128, BF16); g_junk = sv(5560, 128, BF16)
        g_Bsn = sv(5688, 1); g_Etot = sv(5692, 1)
        g_ecol = [sv(5690, 1), sv(5694, 1)]
        g_dec = [g_dec, sv(5696, 2)]
        g_qd = [g_qd, sv(5700, 64, BF16)]; g_kd = [g_kd, sv(5764, 64, BF16)]; g_kdec = [g_kdec, sv(5828, 64, BF16)]
        g_vbf = [g_vbf, sv(5892, 128, BF16)]
        g_yb = [g_yb, sv(6020, 128, BF16)]
        g_ss = [g_ss, sv(6148, 2)]; g_rs = [g_rs, sv(6152, 2)]
        g_sig = sv(6156, 256)
        g_ss8 = sv(6720, 8); g_rs8 = sv(6728, 8)
        g_c = sv(6736, 2); g_cv = sv(6752, 256)
        g_Sbb = [sv(6412 + i * 128, 128, BF16) for i in range(2)]
        g_vsb = sv(2360, 128, BF16)
        g_Qmb = sv(1672, 128, BF16).rearrange("p (r t) -> p r t", r=16)
        g_Kmb = tabc[0:16, 0:1024].bitcast(BF16).rearrange("p (r k) -> p r k", r=16)
        g_Sb1 = [sv(2616 + i * 256, 256) for i in range(4)] + [tabc[:, 1024 + i * 256:1280 + i * 256] for i in range(min(8, (tabc_n - 1024) // 256))]
        NSB = len(g_Sb1)
        stg_in = sv(0, 1408); stg_out = sv(1408, 1408)
        fst = sv(2816, 11 * 32).rearrange("p (j c) -> p j c", c=32)
        nst = sv(3168, 11 * 34).rearrange("p (j c) -> p j c", c=34)

        class WS:
            def __init__(self):
                self.req = []
                self.i = 0
                self.loaded = 0
            def get(self, parts):
                if S.dry:
                    self.req.append(parts)
                    self.i += 1
                    return self.i - 1, wsl[(self.i - 1) % NSLOT]
                u = self.i
                self.i += 1
                return u, wsl[u % NSLOT]
            def load(self, u):
                if u >= len(self.req):
                    return
                slot = wsl[u % NSLOT]
                for (kn, c0, ncol, src) in self.req[u]:
                    S.op("pool", lambda e, o=slot[:, 0:kn, c0:c0 + ncol], s=src: e.dma_start(out=o, in_=s),
                         writes=[("w", u % NSLOT)], dma=True)
                self.loaded = u + 1
            def release(self, u):
                if S.dry:
                    return
                if u + NSLOT == self.loaded:
                    self.load(u + NSLOT)
            def start(self):
                for u in range(min(NSLOT, len(self.req))):
                    self.load(u)
        ws = WS()

        def wunit(src2d, c0, ncols=128, kn=KC):
            return (kn, 0, ncols, src2d[:, c0:c0 + ncols].rearrange("(k p) n -> p k n", p=128))

        def colgroups(ns):
            g = []
            c = 0
            while c < T:
                n = min(512, T - c)
                g.append((c, n))
                c += n
            if ns:
                g.append((T, NX))
            return g

        BX, BY = (0, 1, 2), (3, 4, 5)

        def proj_fm(slot, u, nk, rhs_t, rhs_res, groups, banks, M=128):
            for k in range(nk):
                for gi, (c0, n) in enumerate(groups):
                    S.op("pe", lambda e, o=ps[banks[gi]][0:M, 0:n], l=slot[:, k, 0:M], r=rhs_t[:, k, c0:c0 + n],
                         a=(k == 0), b=(k == nk - 1): e.matmul(o, lhsT=l, rhs=r, start=a, stop=b),
                         reads=[("w", u % NSLOT)] + rhs_res(k), writes=[("ps", banks[gi])], signal=(k == nk - 1))

        def vcol(name, l, i):
            o = cfg.off[(name, l)] + i
            return vec[:, o:o + 1]

        def load_x(p, ns):
            for tb in range(NTB):
                xb, xres = xin[tb % len(xin)]
                S.op("sp", lambda e, r0=p * T + tb * 128, xb=xb: e.dma_start(out=xb, in_=x_p[r0:r0 + 128, :]),
                     writes=xres, dma=True)
                for q in range(4):
                    bk = 4 + ((4 * tb + q) % 4)
                    for k in range(4):
                        dc = 4 * q + k
                        S.op("pe", lambda e, o=ps[bk][:, k * 128:(k + 1) * 128], i=xb[:, dc * 128:(dc + 1) * 128]:
                             e.transpose(out=o, in_=i, identity=ident), reads=xres + ["consts"],
                             writes=[("ps", bk)], signal=(k == 3))
                    eng = "act" if q % 2 == 0 else "dve"
                    o = xT[:, 4 * q:4 * q + 4, tb * 128:(tb + 1) * 128]
                    i = ps[bk][:, :].rearrange("p (k t) -> p k t", k=4)
                    if eng == "act":
                        S.op("act", lambda e, o=o, i=i: e.copy(out=o, in_=i), reads=[("ps", bk)],
                             writes=[("xT", 4 * q + k) for k in range(4)])
                    else:
                        S.op("dve", lambda e, o=o, i=i: e.tensor_copy(out=o, in_=i), reads=[("ps", bk)],
                             writes=[("xT", 4 * q + k) for k in range(4)])
            if ns:
                S.op("sp", lambda e: e.dma_start(out=io[0:NS, :], in_=x_s), writes=["tA", "tB"], dma=True)
                S.op("sp", lambda e: e.dma_start(out=io[NS:NX, :], in_=x_h), writes=["tA", "tB"], dma=True)
                for dc in range(KC):
                    S.op("pe", lambda e, o=ps[6][:, dc * NX:(dc + 1) * NX], i=io[0:NX, dc * 128:(dc + 1) * 128]:
                         e.transpose(out=o, in_=i, identity=ident[0:NX, 0:NX]), reads=["tA", "tB", "consts"],
                         writes=[("ps", 6)], signal=(dc == KC - 1))
                S.op("dve", lambda e: e.tensor_copy(out=xT[:, :, T:T + NX],
                                                     in_=ps[6][:, 0:KC * NX].rearrange("p (k t) -> p k t", k=KC)),
                     reads=[("ps", 6)], writes=[("xT", k) for k in range(KC)])

        def rms_stats(groups, TTp):
            for dc in range(KC):
                sq = sqv[:, (dc % 2) * EW:(dc % 2) * EW + TTp]
                S.op("act", lambda e, o=sq, i=xT[:, dc, 0:TTp]: e.activation(out=o, in_=i, func=AF.Square),
                     reads=[("xT", dc)], writes=[("sq", dc % 2)])
                for gi, (c0, n) in enumerate(groups):
                    S.op("pe", lambda e, o=ps[BX[gi]][:, 0:n], r=sq[:, c0:c0 + n], a=(dc == 0), b=(dc == KC - 1):
                         e.matmul(o, lhsT=onesb[:], rhs=r, start=a, stop=b),
                         reads=[("sq", dc % 2), "onesb"], writes=[("ps", BX[gi])],
                         signal=(gi == len(groups) - 1))
            for gi, (c0, n) in enumerate(groups):
                S.op("act", lambda e, o=tC[:, c0:c0 + n], i=ps[BX[gi]][:, 0:n]:
                     e.activation(out=o, in_=i, func=AF.Ln, scale=1.0 / D, bias=EPS),
                     reads=[("ps", BX[gi])], writes=["tC"])
            S.op("act", lambda e: e.activation(out=tC[:, 0:TTp], in_=tC[:, 0:TTp], func=AF.Exp, scale=-0.5),
                 reads=["tC"], writes=["tC"])

        def rms_norm(gname, l, groups, TTp):
            rms_stats(groups, TTp)
            for dc in range(KC):
                S.op("dve", lambda e, o=xnT[:, dc, 0:TTp], i=xT[:, dc, 0:TTp], g=vcol(gname, l, dc):
                     e.scalar_tensor_tensor(out=o, in0=i, scalar=g, in1=tC[:, 0:TTp], op0=ALU.mult, op1=ALU.mult),
                     reads=[("xT", dc), "tC", "vec"], writes=["xnT"])

        def xn_res(k):
            return ["xnT"]

        def conv_state_in(l):
            for r in range(2):
                S.op("sp", lambda e, r=r: e.dma_start(out=stg_in[r * NS:(r + 1) * NS, 0:CD], in_=st_conv[l, :, r, :]),
                     writes=["stg_in"], dma=True)
            for c in range(CC):
                S.op("pe", lambda e, c=c: e.transpose(out=ps[7][:, c * 32:(c + 1) * 32],
                                                      in_=stg_in[0:32, c * 128:(c + 1) * 128], identity=ident[0:32, 0:32]),
                     reads=["stg_in", "consts"], writes=[("ps", 7)], signal=(c == CC - 1))
            S.op("dve", lambda e: e.tensor_copy(out=fst[:, 0:CC, :], in_=ps[7][:, 0:CC * 32].rearrange("p (j c) -> p j c", c=32)),
                 reads=[("ps", 7)], writes=["fst"])

        def state_out(nchunks, ns, dst_p, dst_s0, dst_s1):
            W = 2 + 2 * ns
            nb = (nchunks + 3) // 4
            for j in range(nchunks):
                bk = [5, 6, 7][j // 4]
                S.op("pe", lambda e, j=j, bk=bk: e.transpose(out=ps[bk][0:W, (j % 4) * 128:(j % 4 + 1) * 128],
                                                               in_=nst[:, j, 0:W], identity=ident),
                     reads=["nst", "consts"], writes=[("ps", bk)], signal=(j % 4 == 3 or j == nchunks - 1))
            for b in range(nb):
                bk = [5, 6, 7][b]
                n = min(4, nchunks - 4 * b) * 128
                S.op("act", lambda e, b=b, bk=bk, n=n: e.copy(out=stg_out[0:W, b * 512:b * 512 + n], in_=ps[bk][0:W, 0:n]),
                     reads=[("ps", bk)], writes=["stg_out"])
            S.op("sp", lambda e: e.dma_start(out=dst_p, in_=stg_out[0:2, 0:nchunks * 128]), reads=["stg_out"], dma=True)
            if ns:
                S.op("sp", lambda e: e.dma_start(out=dst_s0, in_=stg_out[2:2 + ns, 0:nchunks * 128]), reads=["stg_out"], dma=True)
                S.op("sp", lambda e: e.dma_start(out=dst_s1, in_=stg_out[2 + ns:2 + 2 * ns, 0:nchunks * 128]),
                     reads=["stg_out"], dma=True)

        def mixer_conv(p, l, ns, groups, TTp):
            win = w_in[l]
            if ns:
                conv_state_in(l)
            u, slot = ws.get([(KC, 0, GR, win[:, S7:S7 + GR].rearrange("(k p) n -> p k n", p=128))])
            proj_fm(slot, u, KC, xnT, xn_res, groups, BY, M=GR)
            ws.release(u)
            for gi, (c0, n) in enumerate(groups):
                S.op("act", lambda e, o=glrT[0:GR, c0:c0 + n], i=ps[BY[gi]][0:GR, 0:n]: e.copy(out=o, in_=i),
                     reads=[("ps", BY[gi])], writes=["glrT"])
            for c in range(CC):
                B1, B2 = (BX, BY) if c % 2 == 0 else (BY, BX)
                u, slot = ws.get([wunit(win, S1 + c * 128)])
                proj_fm(slot, u, KC, xnT, xn_res, groups, B1)
                ws.release(u)
                for gi, (c0, n) in enumerate(groups):
                    S.op("act", lambda e, o=tA[:, c0:c0 + n], i=ps[B1[gi]][:, 0:n]: e.copy(out=o, in_=i),
                         reads=[("ps", B1[gi])], writes=["tA"])
                u, slot = ws.get([wunit(win, S2 + c * 128)])
                proj_fm(slot, u, KC, xnT, xn_res, groups, B2)
                ws.release(u)
                for gi, (c0, n) in enumerate(groups):
                    S.op("dve", lambda e, o=tB[:, 2 + c0:2 + c0 + n], a=tA[:, c0:c0 + n], i=ps[B2[gi]][:, 0:n]:
                         e.tensor_tensor(out=o, in0=a, in1=i, op=ALU.mult),
                         reads=[("ps", B2[gi]), "tA"], writes=["tB"])
                S.op("dve", lambda e: e.tensor_copy(out=tB[:, 0:2], in_=tB[:, 2 + T + NS:2 + T + NX]),
                     reads=["tB"], writes=["tB"])
                w0, w1, w2 = vcol("cw", l, c), vcol("cw", l, 8 + c), vcol("cw", l, 16 + c)
                S.op("act", lambda e, w0=w0: e.activation(out=tC[:, 0:T], in_=tB[:, 0:T], func=AF.Copy, scale=w0),
                     reads=["tB", "vec"], writes=["tC"])
                S.op("dve", lambda e, w1=w1: e.scalar_tensor_tensor(out=tC[:, 0:T], in0=tB[:, 1:1 + T], scalar=w1,
                                                                     in1=tC[:, 0:T], op0=ALU.mult, op1=ALU.add),
                     reads=["tB", "tC", "vec"], writes=["tC"])
                S.op("dve", lambda e, w2=w2: e.scalar_tensor_tensor(out=tC[:, 0:T], in0=tB[:, 2:2 + T], scalar=w2,
                                                                     in1=tC[:, 0:T], op0=ALU.mult, op1=ALU.add),
                     reads=["tB", "tC", "vec"], writes=["tC"])
                if ns:
                    S.op("act", lambda e, w0=w0, c=c: e.activation(out=tC[:, T:T + ns], in_=fst[:, c, 0:ns], func=AF.Copy, scale=w0),
                         reads=["fst", "vec", "tC"], writes=["tC"])
                    S.op("dve", lambda e, w1=w1, c=c: e.scalar_tensor_tensor(out=tC[:, T:T + ns], in0=fst[:, c, ns:2 * ns], scalar=w1,
                                                                              in1=tC[:, T:T + ns], op0=ALU.mult, op1=ALU.add),
                         reads=["fst", "tC", "vec"], writes=["tC"])
                    S.op("dve", lambda e, w2=w2: e.scalar_tensor_tensor(out=tC[:, T:T + ns], in0=tB[:, 2 + T:2 + T + ns], scalar=w2,
                                                                         in1=tC[:, T:T + ns], op0=ALU.mult, op1=ALU.add),
                         reads=["tB", "tC", "vec"], writes=["tC"])
                S.op("dve", lambda e, c=c: e.tensor_copy(out=nst[:, c, 0:2], in_=tB[:, T:T + 2]), reads=["tB"], writes=["nst"])
                if ns:
                    S.op("dve", lambda e, c=c: e.tensor_copy(out=nst[:, c, 2:2 + ns], in_=fst[:, c, ns:2 * ns]),
                         reads=["fst"], writes=["nst"])
                    S.op("dve", lambda e, c=c: e.tensor_copy(out=nst[:, c, 2 + ns:2 + 2 * ns], in_=tB[:, 2 + T:2 + T + ns]),
                         reads=["tB"], writes=["nst"])
                u, slot = ws.get([wunit(win, c * 128)])
                proj_fm(slot, u, KC, xnT, xn_res, groups, B1)
                ws.release(u)
                for gi, (c0, n) in enumerate(groups):
                    S.op("dve", lambda e, o=yT[:, c, c0:c0 + n], a=tC[:, c0:c0 + n], i=ps[B1[gi]][:, 0:n]:
                         e.tensor_tensor(out=o, in0=a, in1=i, op=ALU.mult),
                         reads=[("ps", B1[gi]), "tC"], writes=[("yT", c)])
            state_out(CC, ns, conv_p[p, l], conv_s[l, :, 0, :], conv_s[l, :, 1, :])

        def gla_post(h, l, M, c0, o_ap, obank, sg_ap, par, tbank, ores=None):
            ores = ores or ("ps", obank)
            ss, rs, yb = g_ss[par], g_rs[par], g_yb[par]
            tps = psb[tbank]
            S.op("dve", lambda e: e.memset(ss[0:M, 0:2], 0.0), writes=[("g_ss", par)])
            S.op("act", lambda e: e.activation(out=g_junk[0:M, :], in_=o_ap, func=AF.Square, accum_out=ss[0:M, 0:1]),
                 reads=[ores], writes=["g_junk", ("g_ss", par)])
            S.op("act", lambda e: e.activation(out=rs[0:M, 0:1], in_=ss[0:M, 0:1], func=AF.Ln, scale=1.0 / DV, bias=EPS),
                 reads=[("g_ss", par)], writes=[("g_rs", par)])
            S.op("act", lambda e: e.activation(out=rs[0:M, 0:1], in_=rs[0:M, 0:1], func=AF.Exp, scale=-0.5),
                 reads=[("g_rs", par)], writes=[("g_rs", par)])
            S.op("dve", lambda e: e.scalar_tensor_tensor(out=yb[0:M, :], in0=o_ap, scalar=rs[0:M, 0:1],
                                                          in1=sg_ap, op0=ALU.mult, op1=ALU.mult),
                 reads=[ores, ("g_rs", par), "g_sg", "g_sgs"], writes=[("g_yb", par)])
            for vc in range(2):
                S.op("pe", lambda e, vc=vc: e.transpose(out=tps[:, 256 + vc * 128:256 + vc * 128 + M],
                                                        in_=yb[0:M, vc * 128:(vc + 1) * 128], identity=identb[0:M, 0:M]),
                     reads=[("g_yb", par), "identb"], writes=[("ps", tbank)], signal=(vc == 1))
            for vc in range(2):
                ch = CC + 2 * h + vc
                S.op("dve", lambda e, vc=vc, ch=ch: e.tensor_scalar_mul(yT[:, ch, c0:c0 + M], tps[:, 256 + vc * 128:256 + vc * 128 + M],
                                                                         vcol("gng", l, 2 * h + vc)),
                     reads=[("ps", tbank), "vec"], writes=[("yT", ch)])

        def o_store(tb):
            return 4 + tb // 2, ps[4 + tb // 2][:, (tb % 2) * DV:(tb % 2 + 1) * DV]

        def mixer_gla(p, l, ns):
            for h in range(H):
                gla_head(p, l, ns, h)
            S.op("sp", lambda e: e.dma_start(out=gla_p[p, l].rearrange("h k v -> k h v"), in_=Sst[:]),
                 reads=[("S", hh) for hh in range(H)], dma=True)

        def gla_head(p, l, ns, h):
            win = w_in[l]
            sc = 1.0 / 16.0
            dst = [(0, 0), (0, 128), (1, 0), (1, 128), (1, 256), (1, 384)]
            if True:
                cols = (S3 + h * DK, S4 + h * DK, S5 + h * DV, S5 + h * DV + 128, S6 + h * DV, S6 + h * DV + 128)
                units = [ws.get([wunit(win, c0)]) for c0 in cols]

                def P(i, M, c0, last=False):
                    (u, slot), (bk, co) = units[i], dst[i]
                    for k in range(KC):
                        S.op("pe", lambda e, o=ps[bk][0:M, co:co + 128], lt=xnT[:, k, c0:c0 + M], r=slot[:, k, :],
                             a=(k == 0), b=(k == KC - 1): e.matmul(o, lhsT=lt, rhs=r, start=a, stop=b),
                             reads=[("w", u % NSLOT), "xnT"], writes=[("ps", bk)], signal=(k == KC - 1))
                    if last:
                        ws.release(u)

                def Z(M, c0):
                    S.op("pe", lambda e: e.matmul(ps[2][0:M, 0:128], lhsT=glrT[0:GR + 1, c0:c0 + M],
                                                  rhs=gw[0:GR + 1, l, h * DK:(h + 1) * DK], start=True, stop=True),
                         reads=["glrT", "gw"], writes=[("ps", 2)])
                    S.op("act", lambda e: e.activation(out=g_e[0:M, :], in_=ps[2][0:M, 0:128], func=AF.Exp, scale=-1.0),
                         reads=[("ps", 2)], writes=["g_e"])
                    S.op("act", lambda e: e.activation(out=g_lf[0:M, :], in_=g_e[0:M, :], func=AF.Ln, bias=1.0),
                         reads=["g_e"], writes=["g_lf"])

                def cumsum(n):
                    S.op("pe", lambda e: e.matmul(ps[2][:, 128:256], lhsT=Uc, rhs=g_lf, start=True, stop=True),
                         reads=["g_lf", "consts"], writes=[("ps", 2)], signal=False)
                    S.op("pe", lambda e: e.matmul(ps[2][:, 256:384], lhsT=Lm, rhs=g_lf, start=True, stop=True),
                         reads=["g_lf", "consts"], writes=[("ps", 2)], signal=False)
                    S.op("pe", lambda e: e.matmul(ps[2][:, 384:386], lhsT=g_lf, rhs=cmat, start=True, stop=True),
                         reads=["g_lf", "consts"], writes=[("ps", 2)])

                def gates(n):
                    par = n % 2
                    S.op("act", lambda e: e.activation(out=g_eb, in_=ps[2][:, 128:256], func=AF.Exp, scale=-sc), reads=[("ps", 2)], writes=["g_eb"])
                    S.op("act", lambda e: e.activation(out=g_enb, in_=ps[2][:, 128:256], func=AF.Exp, scale=sc), reads=[("ps", 2)], writes=["g_enb"])
                    S.op("act", lambda e: e.activation(out=g_erev, in_=ps[2][:, 256:384], func=AF.Exp, scale=-sc), reads=[("ps", 2)], writes=["g_erev"])
                    S.op("act", lambda e: e.activation(out=g_dec[par], in_=ps[2][:, 384:386], func=AF.Exp, scale=-sc),
                         reads=[("ps", 2)], writes=[("g_dec", par)])
                    S.op("act", lambda e: e.activation(out=g_ecol[par], in_=ps[2][:, 385:386], func=AF.Exp, scale=-sc, bias=g_Bsn),
                         reads=[("ps", 2), "g_Bsn"], writes=[("g_ecol", par)])
                    S.op("dve", lambda e: e.scalar_tensor_tensor(out=g_Bsn, in0=ps[2][:, 384:385], scalar=-sc, in1=g_Bsn, op0=ALU.mult, op1=ALU.add),
                         reads=[("ps", 2), "g_Bsn"], writes=["g_Bsn"])

                def qkd(n):
                    par = n % 2
                    S.op("dve", lambda e: e.scalar_tensor_tensor(out=g_qd[par], in0=ps[0][:, 0:128], scalar=float(DK) ** -0.5, in1=g_eb,
                                                                  op0=ALU.mult, op1=ALU.mult), reads=[("ps", 0), "g_eb"], writes=[("g_qd", par)])
                    S.op("dve", lambda e: e.tensor_tensor(out=g_kd[par], in0=ps[0][:, 128:256], in1=g_enb, op=ALU.mult),
                         reads=[("ps", 0), "g_enb"], writes=[("g_kd", par)])
                    S.op("dve", lambda e: e.tensor_tensor(out=g_kdec[par], in0=ps[0][:, 128:256], in1=g_erev, op=ALU.mult),
                         reads=[("ps", 0), "g_erev"], writes=[("g_kdec", par)])

                def vg(n):
                    par = n % 2
                    S.op("act", lambda e: e.copy(out=g_vbf[par], in_=ps[1][:, 0:DV]), reads=[("ps", 1)], writes=[("g_vbf", par)])
                    S.op("act", lambda e: e.activation(out=g_sig, in_=ps[1][:, 256:512], func=AF.Exp, scale=-1.0), reads=[("ps", 1)], writes=["g_sig"])
                    S.op("act", lambda e: e.activation(out=g_sig, in_=g_sig, func=AF.Ln, bias=1.0), reads=["g_sig"], writes=["g_sig"])
                    S.op("act", lambda e: e.activation(out=g_sig, in_=g_sig, func=AF.Exp, scale=-1.0), reads=["g_sig"], writes=["g_sig"])
                    S.op("dve", lambda e: e.tensor_tensor(out=g_sgs[:, n, :], in0=ps[1][:, 256:512], in1=g_sig, op=ALU.mult),
                         reads=[("ps", 1), "g_sig"], writes=["g_sgs"])

                def Sbf(n):
                    par = n % 2
                    S.op("dve", lambda e: e.tensor_scalar_mul(g_Sbf, Sst[:, h, :], g_dec[par][:, 1:2]),
                         reads=[("S", h), ("g_dec", par)], writes=["g_Sbf"])

                def T1(tb):
                    par = tb % 2
                    S.op("pe", lambda e: e.transpose(out=psb[3][:, 0:128], in_=g_qd[par], identity=identb[:]), reads=[("g_qd", par), "identb"],
                         writes=[("ps", 3)], signal=False)
                    S.op("pe", lambda e: e.transpose(out=psb[3][:, 128:256], in_=g_kd[par], identity=identb[:]), reads=[("g_kd", par), "identb"],
                         writes=[("ps", 3)])
                    S.op("dve", lambda e: e.tensor_copy(out=g_qkT, in_=psb[3][:, 0:256]), reads=[("ps", 3)], writes=["g_qkT"])
                    S.op("dve", lambda e: e.tensor_scalar_mul(g_qeT[:, tb, :], psb[3][:, 0:128], g_ecol[par][:, 0:1]),
                         reads=[("ps", 3), ("g_ecol", par)], writes=["g_qeT"])

                def T2(tb):
                    S.op("pe", lambda e: e.matmul(ps[0][:, 256:384], lhsT=g_qkT[:, 128:256], rhs=g_qkT[:, 0:128], start=True, stop=True),
                         reads=["g_qkT"], writes=[("ps", 0)])
                    S.op("dve", lambda e: e.tensor_tensor(out=g_ATm, in0=ps[0][:, 256:384], in1=maskT, op=ALU.mult),
                         reads=[("ps", 0), "consts"], writes=["g_ATm"])

                def T3(tb):
                    par = tb % 2
                    ob, o_ap = o_store(tb)
                    S.op("pe", lambda e: e.matmul(o_ap, lhsT=g_ATm, rhs=g_vbf[par], start=False, stop=False, skip_group_check=True),
                         reads=["g_ATm", ("g_vbf", par)], writes=[("ps", ob)], signal=False)
                    S.op("pe", lambda e: e.matmul(o_ap, lhsT=g_qkT[:, 0:128], rhs=g_Sbf, start=False, stop=True, skip_group_check=True),
                         reads=["g_qkT", "g_Sbf"], writes=[("ps", ob)])
                    S.op("pe", lambda e: e.matmul(ps[3][:, 256:512], lhsT=g_kdec[par], rhs=g_vbf[par], start=True, stop=True),
                         reads=[("g_kdec", par), ("g_vbf", par)], writes=[("ps", 3)])
                    S.op("dve", lambda e: e.scalar_tensor_tensor(out=Sst[:, h, :], in0=Sst[:, h, :], scalar=g_dec[par][:, 0:1], in1=ps[3][:, 256:512],
                                                                  op0=ALU.mult, op1=ALU.add),
                         reads=[("ps", 3), ("g_dec", par), ("S", h), "g_Sbf"], writes=[("S", h)])

                S.op("dve", lambda e: e.memset(Sst[:, h, :], 0.0), writes=[("S", h)])
                S.op("dve", lambda e: e.memset(g_Bsn, 0.0), writes=["g_Bsn"])
                one_tile = (NTB == 1)
                def ld(r):
                    S.op("sp", lambda e: e.dma_start(out=g_Sb1[r % NSB], in_=st_gla[l, r, h]), writes=[("g_Sb", r % NSB)], dma=True)

                if ns:
                    M = NS
                    for r in range(min(NSB, M)):
                        ld(r)
                    Z(M, T)
                    for i in range(6):
                        P(i, M, T)
                    S.op("act", lambda e: e.activation(out=g_as[0:M, :], in_=g_lf[0:M, :], func=AF.Exp, scale=-sc), reads=["g_lf"], writes=["g_as"])
                    S.op("dve", lambda e: e.tensor_scalar_mul(g_qs[0:M, :], ps[0][0:M, 0:128], float(DK) ** -0.5),
                         reads=[("ps", 0)], writes=["g_qs"])
                    S.op("dve", lambda e: e.tensor_copy(out=g_ks[0:M, :], in_=ps[0][0:M, 128:256]), reads=[("ps", 0)], writes=["g_ks"])
                    S.op("act", lambda e: e.copy(out=g_vsb[0:M, :], in_=ps[1][0:M, 0:DV]), reads=[("ps", 1)], writes=["g_vs"])
                    S.op("act", lambda e: e.activation(out=g_sig[0:M, :], in_=ps[1][0:M, 256:512], func=AF.Exp, scale=-1.0), reads=[("ps", 1)], writes=["g_sig"])
                    S.op("act", lambda e: e.activation(out=g_sig[0:M, :], in_=g_sig[0:M, :], func=AF.Ln, bias=1.0), reads=["g_sig"], writes=["g_sig"])
                    S.op("act", lambda e: e.activation(out=g_sig[0:M, :], in_=g_sig[0:M, :], func=AF.Exp, scale=-1.0), reads=["g_sig"], writes=["g_sig"])
                    S.op("dve", lambda e: e.tensor_tensor(out=g_sg[0:M, :], in0=ps[1][0:M, 256:512], in1=g_sig[0:M, :], op=ALU.mult),
                         reads=[("ps", 1), "g_sig"], writes=["g_sg"])
                    S.op("dve", lambda e: e.tensor_tensor(out=g_e[0:M, :], in0=g_qs[0:M, :], in1=g_ks[0:M, :], op=ALU.mult),
                         reads=["g_qs", "g_ks", "g_lf"], writes=["g_e"])
                    S.op("dve", lambda e: e.reduce_sum(out=g_c[0:M, 0:1], in_=g_e[0:M, :], axis=mybir.AxisListType.X),
                         reads=["g_e"], writes=["g_c"])
                    S.op("dve", lambda e: e.tensor_scalar_mul(g_cv[0:M, :], ps[1][0:M, 0:DV], g_c[0:M, 0:1]),
                         reads=[("ps", 1), "g_c"], writes=["g_cv"])
                    S.op("dve", lambda e: e.tensor_tensor(out=g_qs[0:M, :], in0=g_qs[0:M, :], in1=g_as[0:M, :], op=ALU.mult),
                         reads=["g_qs", "g_as", "g_e"], writes=["g_qs"])
                    for i, src in enumerate((g_as, g_ks, g_qs)):
                        S.op("pe", lambda e, i=i, src=src: e.transpose(out=ps[2][:, 128 + i * M:128 + (i + 1) * M], in_=src[0:M, :],
                                                                       identity=ident[0:M, 0:M]),
                             reads=["g_as", "g_ks", "g_qs", "consts"], writes=[("ps", 2)], signal=(i == 2))
                    S.op("dve", lambda e: e.tensor_copy(out=g_akq[:, 0:3 * M], in_=ps[2][:, 128:128 + 3 * M]), reads=[("ps", 2)], writes=["g_akq"])
                    S.op("dve", lambda e: e.tensor_tensor(out=g_Qmb, in0=g_akq[:, None, 2 * M:3 * M].broadcast_to([128, M, M]), in1=idb16, op=ALU.mult),
                         reads=["g_akq", "consts"], writes=["g_Qm"])
                    S.op("dve", lambda e: e.tensor_tensor(out=g_Kmb, in0=g_ks[0:M, None, :].broadcast_to([M, M, 128]),
                                                           in1=ident[0:M, 0:M, None].broadcast_to([M, M, 128]), op=ALU.mult),
                         reads=["g_ks", "consts"], writes=["tA", "tB"])
                for b in range(4, 8):
                    S.op("dve", lambda e, b=b: e.memset(ps[b][:, :], 0.0), writes=[("ps", b)])
                Z(128, 0)
                P(0, 128, 0, one_tile)
                cumsum(0); gates(0)
                P(1, 128, 0, one_tile); P(2, 128, 0, one_tile); P(3, 128, 0, one_tile)
                qkd(0)
                P(4, 128, 0, one_tile); P(5, 128, 0, one_tile)
                vg(0); Sbf(0)
                for tb in range(NTB):
                    n = tb + 1
                    has = n < NTB
                    last = (n == NTB - 1)
                    c0 = n * 128
                    if has:
                        Z(128, c0)
                        P(0, 128, c0, last)
                    T1(tb)
                    if has:
                        P(1, 128, c0, last)
                        cumsum(n); gates(n)
                    T2(tb)
                    if has:
                        P(2, 128, c0, last); P(3, 128, c0, last)
                    T3(tb)
                    if has:
                        qkd(n)
                        P(4, 128, c0, last); P(5, 128, c0, last)
                        vg(n); Sbf(n)
                S.op("act", lambda e: e.activation(out=g_Etot, in_=g_Bsn, func=AF.Exp), reads=["g_Bsn"], writes=["g_Etot"])
                S.op("sp", lambda e, h=h: e.dma_start(out=ibS[l][h].ap(), in_=Sst[:, h, :]), reads=[("S", h)], writes=[("ibS", l, h)], dma=True)
                S.op("pool", lambda e, h=h: e.collective_compute("AllGather", ALU.bypass, replica_groups=PAIRS,
                                                                  ins=[ibS[l][h].ap().opt()], outs=[obS[l][h].ap().opt()]),
                     reads=[("ibS", l, h)], writes=[("obS", l, h)], cc=ccsems[l * H + h])
                if ns:
                    M = NS

                    def dS(r):
                        bk = 3 if r % 2 == 0 else 0
                        S.op("pe", lambda e: e.matmul(ps[bk][:, 256:512], lhsT=g_Kmb[:, r, :], rhs=g_vsb[0:M, :], start=True, stop=True),
                             reads=["tA", "tB", "g_vs"], writes=[("ps", bk)])

                    def wb(r):
                        S.op("act", lambda e: e.dma_start(out=gla_s[l, r, h], in_=g_Sb1[r % NSB]), reads=[("g_Sb", r % NSB)], dma=True)
                        if r + NSB < M:
                            ld(r + NSB)

                    dS(0)
                    for r in range(M):
                        bk = 3 if r % 2 == 0 else 0
                        buf = g_Sb1[r % NSB]
                        bres = ("g_Sb", r % NSB)
                        S.op("act", lambda e, buf=buf, r=r: e.copy(out=g_Sbb[r % 2], in_=buf), reads=[bres], writes=[("g_Sbb", r % 2)])
                        S.op("dve", lambda e, buf=buf, bk=bk, r=r: e.scalar_tensor_tensor(
                            out=buf, in0=buf, scalar=g_akq[:, r:r + 1], in1=ps[bk][:, 256:512], op0=ALU.mult, op1=ALU.add),
                            reads=[("ps", bk), "g_akq", bres], writes=[bres])
                        if r + 1 < M:
                            dS(r + 1)
                        S.op("pe", lambda e, r=r: e.matmul(ps[2][0:M, 256:512], lhsT=g_Qmb[:, r, :], rhs=g_Sbb[r % 2],
                                                           start=(r == 0), stop=(r == M - 1)),
                             reads=["g_Qm", ("g_Sbb", r % 2)], writes=[("ps", 2)])
                        if r >= 1:
                            wb(r - 1)
                        if r == max(0, M - NSB):
                            S.op("sp", lambda e: e.dma_start(out=g_SA, in_=obS[l][h].ap()[0:128, :]), reads=[("obS", l, h)],
                                 writes=["g_SA"], dma=True)
                    wb(M - 1)
                    S.op("dve", lambda e: e.tensor_tensor(out=g_cv[0:M, :], in0=g_cv[0:M, :], in1=ps[2][0:M, 256:512], op=ALU.add),
                         reads=[("ps", 2), "g_cv"], writes=["g_cv"])
                    gla_post(h, l, M, T, g_cv[0:M, :], 2, g_sg[0:M, :], 0, 3, ores="g_cv")
                if not ns:
                    S.op("sp", lambda e, h=h: e.dma_start(out=g_SA, in_=obS[l][h].ap()[0:128, :]), reads=[("obS", l, h)], writes=["g_SA"], dma=True)
                S.op("dve", lambda e: e.tensor_scalar_mul(g_SA, g_SA, flag[:, 0:1]), reads=["g_SA", "flag"], writes=["g_SA"])
                S.op("dve", lambda e: e.tensor_copy(out=g_SAb, in_=g_SA), reads=["g_SA"], writes=["g_SAb"])
                for tb in range(NTB):
                    ob, o_ap = o_store(tb)
                    S.op("pe", lambda e, tb=tb, o_ap=o_ap: e.matmul(o_ap, lhsT=g_qeT[:, tb, :], rhs=g_SAb, start=False, stop=True, skip_group_check=True),
                         reads=["g_qeT", "g_SAb"], writes=[("ps", ob)])
                S.op("dve", lambda e: e.memset(g_ss8[:, 0:NTB], 0.0), writes=["g_ss8"])
                for tb in range(NTB):
                    ob, o_ap = o_store(tb)
                    S.op("act", lambda e, tb=tb, o_ap=o_ap: e.activation(out=g_junk, in_=o_ap, func=AF.Square, accum_out=g_ss8[:, tb:tb + 1]),
                         reads=[("ps", ob), "g_ss8"], writes=["g_junk", ("g_ss8", tb)])
                    S.op("act", lambda e, tb=tb: e.activation(out=g_rs8[:, tb:tb + 1], in_=g_ss8[:, tb:tb + 1], func=AF.Ln, scale=1.0 / DV, bias=EPS),
                         reads=[("g_ss8", tb)], writes=[("g_rs8", tb)])
                    S.op("act", lambda e, tb=tb: e.activation(out=g_rs8[:, tb:tb + 1], in_=g_rs8[:, tb:tb + 1], func=AF.Exp, scale=-0.5),
                         reads=[("g_rs8", tb)], writes=[("g_rs8", tb)])

                def evac(tb):
                    tbank = 3 if tb % 2 == 0 else 0
                    for vc in range(2):
                        ch = CC + 2 * h + vc
                        S.op("dve", lambda e, vc=vc, ch=ch: e.tensor_scalar_mul(yT[:, ch, tb * 128:(tb + 1) * 128],
                                                                                 psb[tbank][:, 256 + vc * 128:384 + vc * 128],
                                                                                 vcol("gng", l, 2 * h + vc)),
                             reads=[("ps", tbank), "vec"], writes=[("yT", ch)])

                for tb in range(NTB):
                    ob, o_ap = o_store(tb)
                    par = tb % 2
                    tbank = 3 if par == 0 else 0
                    S.op("dve", lambda e, tb=tb, o_ap=o_ap, par=par: e.scalar_tensor_tensor(
                        out=g_yb[par], in0=o_ap, scalar=g_rs8[:, tb:tb + 1], in1=g_sgs[:, tb, :], op0=ALU.mult, op1=ALU.mult),
                        reads=[("ps", ob), ("g_rs8", tb), "g_sgs"], writes=[("g_yb", par)])
                    for vc in range(2):
                        S.op("pe", lambda e, vc=vc, par=par, tbank=tbank: e.transpose(out=psb[tbank][:, 256 + vc * 128:384 + vc * 128],
                                                                                     in_=g_yb[par][:, vc * 128:(vc + 1) * 128], identity=identb[:]),
                             reads=[("g_yb", par), "identb"], writes=[("ps", tbank)], signal=(vc == 1))
                    if tb >= 1:
                        evac(tb - 1)
                evac(NTB - 1)
                S.op("dve", lambda e, h=h: e.scalar_tensor_tensor(out=Sst[:, h, :], in0=g_SA, scalar=g_Etot[:, 0:1], in1=Sst[:, h, :],
                                                                   op0=ALU.mult, op1=ALU.add),
                     reads=["g_SA", "g_Etot", ("S", h)], writes=[("S", h)])
        def xchg_x(idx):
            S.op("dve", lambda e: e.tensor_copy(out=xhs[:], in_=xT[:, :, T - 2:T]), reads=[("xT", k) for k in range(KC)], writes=["xhs"])
            S.op("sp", lambda e: e.dma_start(out=ibX[idx].ap(), in_=xhs[:].rearrange("p k t -> p (k t)")), reads=["xhs"],
                 writes=[("ibX", idx)], dma=True)
            S.op("pool", lambda e: e.collective_compute("AllGather", ALU.bypass, replica_groups=PAIRS,
                                                         ins=[ibX[idx].ap().opt()], outs=[obX[idx].ap().opt()]),
                 reads=[("ibX", idx)], writes=[("obX", idx)], cc=ccsems[L * H + idx])
            S.op("sp", lambda e: e.dma_start(out=xhr[:].rearrange("p k t -> p (k t)"), in_=obX[idx].ap()[0:128, :]), reads=[("obX", idx)],
                 writes=["xhr"], dma=True)
            S.op("dve", lambda e: e.tensor_scalar_mul(xT[:, :, T + NS:T + NX], xhr[:], flag[:, 0:1]), reads=["xhr", "flag"],
                 writes=[("xT", k) for k in range(KC)])

        def out_proj(l, groups):
            for oc in range(KC):
                u, slot = ws.get([wunit(w_out[l], oc * 128)])
                bk = BX if oc % 2 == 0 else BY
                proj_fm(slot, u, KC, yT, lambda k: [("yT", k)], groups, bk)
                ws.release(u)
                for gi, (c0, n) in enumerate(groups):
                    S.op("dve", lambda e, o=xT[:, oc, c0:c0 + n], i=ps[bk[gi]][:, 0:n]: e.tensor_tensor(out=o, in0=o, in1=i, op=ALU.add),
                         reads=[("ps", bk[gi])], writes=[("xT", oc)])

        def ffn(p, l, ns, groups, TTp):
            wup, wdn = w_up[l], w_down[l]
            for g in range(NG):
                if ns:
                    for r in range(2):
                        S.op("sp", lambda e, r=r, g=g: e.dma_start(out=stg_in[r * NS:(r + 1) * NS, 0:FG * 128],
                                                                    in_=st_ffn[l, :, r, g * FG * 128:(g + 1) * FG * 128]),
                             writes=["stg_in"], dma=True)
                    for j in range(FG):
                        S.op("pe", lambda e, j=j: e.transpose(out=ps[7][:, j * 32:(j + 1) * 32], in_=stg_in[0:32, j * 128:(j + 1) * 128],
                                                              identity=ident[0:32, 0:32]),
                             reads=["stg_in", "consts"], writes=[("ps", 7)], signal=(j == FG - 1))
                    S.op("dve", lambda e: e.tensor_copy(out=fst[:, 0:FG, :], in_=ps[7][:, 0:FG * 32].rearrange("p (j c) -> p j c", c=32)),
                         reads=[("ps", 7)], writes=["fst"])
                for j in range(FG):
                    fc = g * FG + j
                    u, slot = ws.get([wunit(wup, fc * 128)])
                    proj_fm(slot, u, KC, xnT, xn_res, groups, BX)
                    ws.release(u)
                    for gi, (c0, n) in enumerate(groups):
                        S.op("act", lambda e, o=tA[:, 2 + c0:2 + c0 + n], i=ps[BX[gi]][:, 0:n]: e.copy(out=o, in_=i),
                             reads=[("ps", BX[gi])], writes=["tA"])
                    S.op("act", lambda e: e.copy(out=tA[:, 0:2], in_=tA[:, 2 + T + NS:2 + T + NX]), reads=["tA"], writes=["tA"])
                    w0, w1, w2 = vcol("fcw", l, fc), vcol("fcw", l, FC + fc), vcol("fcw", l, 2 * FC + fc)
                    S.op("act", lambda e, w2=w2, fc=fc: e.activation(out=tB[:, 0:TTp], in_=tA[:, 2:2 + TTp], func=AF.Identity,
                                                                     scale=w2, bias=vcol("fcb", l, fc)),
                         reads=["tA", "vec"], writes=["tB"])
                    S.op("dve", lambda e, w1=w1: e.scalar_tensor_tensor(out=tB[:, 0:T], in0=tA[:, 1:1 + T], scalar=w1, in1=tB[:, 0:T],
                                                                         op0=ALU.mult, op1=ALU.add), reads=["tA", "tB", "vec"], writes=["tB"])
                    S.op("dve", lambda e, w0=w0: e.scalar_tensor_tensor(out=tB[:, 0:T], in0=tA[:, 0:T], scalar=w0, in1=tB[:, 0:T],
                                                                         op0=ALU.mult, op1=ALU.add), reads=["tA", "tB", "vec"], writes=["tB"])
                    if ns:
                        S.op("dve", lambda e, w1=w1, j=j: e.scalar_tensor_tensor(out=tB[:, T:T + ns], in0=fst[:, j, ns:2 * ns], scalar=w1,
                                                                                  in1=tB[:, T:T + ns], op0=ALU.mult, op1=ALU.add),
                             reads=["fst", "tB", "vec"], writes=["tB"])
                        S.op("dve", lambda e, w0=w0, j=j: e.scalar_tensor_tensor(out=tB[:, T:T + ns], in0=fst[:, j, 0:ns], scalar=w0,
                                                                                  in1=tB[:, T:T + ns], op0=ALU.mult, op1=ALU.add),
                             reads=["fst", "tB", "vec"], writes=["tB"])
                    S.op("act", lambda e: e.activation(out=tC[:, 0:TTp], in_=tB[:, 0:TTp], func=AF.Silu), reads=["tB"], writes=["tC"])
                    S.op("dve", lambda e, j=j: e.tensor_copy(out=nst[:, j, 0:2], in_=tA[:, T:T + 2]), reads=["tA"], writes=["nst"])
                    if ns:
                        S.op("dve", lambda e, j=j: e.tensor_copy(out=nst[:, j, 2:2 + ns], in_=fst[:, j, ns:2 * ns]), reads=["fst"], writes=["nst"])
                        S.op("dve", lambda e, j=j: e.tensor_copy(out=nst[:, j, 2 + ns:2 + 2 * ns], in_=tA[:, 2 + T:2 + T + ns]),
                             reads=["tA"], writes=["nst"])
                    u, slot = ws.get([wunit(wup, DFF + fc * 128)])
                    proj_fm(slot, u, KC, xnT, xn_res, groups, BY)
                    ws.release(u)
                    for gi, (c0, n) in enumerate(groups):
                        S.op("dve", lambda e, o=yT[:, j, c0:c0 + n], a=tC[:, c0:c0 + n], i=ps[BY[gi]][:, 0:n]:
                             e.tensor_tensor(out=o, in0=a, in1=i, op=ALU.mult),
                             reads=[("ps", BY[gi]), "tC"], writes=[("yT", j)])
                cs = slice(g * FG * 128, (g + 1) * FG * 128)
                state_out(FG, ns, ffn_p[p, l, :, cs], ffn_s[l, :, 0, cs], ffn_s[l, :, 1, cs])
                for oc in range(KC):
                    u, slot = ws.get([(FG, 0, 128, wdn[g * FG * 128:(g + 1) * FG * 128, oc * 128:(oc + 1) * 128]
                                       .rearrange("(k p) n -> p k n", p=128))])
                    bk = BX if oc % 2 == 0 else BY
                    proj_fm(slot, u, FG, yT, lambda k: [("yT", k)], groups, bk)
                    ws.release(u)
                    for gi, (c0, n) in enumerate(groups):
                        S.op("dve", lambda e, o=xT[:, oc, c0:c0 + n], i=ps[bk[gi]][:, 0:n]: e.tensor_tensor(out=o, in0=o, in1=i, op=ALU.add),
                             reads=[("ps", bk[gi])], writes=[("xT", oc)])

        def final_out(p, ns, groups, TTp):
            rms_stats(groups, TTp)
            for dc in range(KC):
                S.op("dve", lambda e, o=xT[:, dc, 0:TTp], g=vcol("fng", 0, dc):
                     e.scalar_tensor_tensor(out=o, in0=o, scalar=g, in1=tC[:, 0:TTp], op0=ALU.mult, op1=ALU.mult),
                     reads=["tC", "vec"], writes=[("xT", dc)])
            tiles = [(tb * 128, 128, y_p[p * T + tb * 128:p * T + (tb + 1) * 128, :]) for tb in range(NTB)]
            if ns:
                tiles.append((T, ns, y_s))
            for ti, (c0, M, dst) in enumerate(tiles):
                ob, ores = xin[ti % len(xin)]
                for q in range(4):
                    for k in range(4):
                        dc = 4 * q + k
                        S.op("pe", lambda e, q=q, k=k, dc=dc, c0=c0, M=M: e.transpose(out=ps[q][0:M, k * 128:(k + 1) * 128], in_=xT[:, dc, c0:c0 + M],
                                                                           identity=ident),
                             reads=[("xT", dc), "consts"], writes=[("ps", q)], signal=(k == 3))
                    eng = "act" if q % 2 == 0 else "dve"
                    if eng == "act":
                        S.op("act", lambda e, q=q, M=M, ob=ob: e.copy(out=ob[0:M, q * 512:(q + 1) * 512], in_=ps[q][0:M, :]),
                             reads=[("ps", q)], writes=ores)
                    else:
                        S.op("dve", lambda e, q=q, M=M, ob=ob: e.tensor_copy(out=ob[0:M, q * 512:(q + 1) * 512], in_=ps[q][0:M, :]),
                             reads=[("ps", q)], writes=ores)
                S.op("sp", lambda e, dst=dst, M=M, ob=ob: e.dma_start(out=dst, in_=ob[0:M, :]), reads=ores, dma=True)

        def init():
            S.op("sp", lambda e: e.dma_start(out=cst_t[:], in_=consts_d), writes=["consts"], dma=True)
            S.op("sp", lambda e: e.dma_start(out=tabc[:, 0:cfg.VB * 128].rearrange("p (b c) -> p b c", c=128),
                                             in_=vecs_d.rearrange("(b r) c -> r b c", r=128)), writes=["tA", "tB"], dma=True)
            for l in range(L):
                S.op("pool", lambda e, l=l: e.dma_start(out=gw[0:GR + 1, l, :], in_=gw_d[l]), writes=["gw"], dma=True)
            S.op("sp", lambda e: e.dma_start(out=flag[:], in_=flag_d), writes=["flag"], dma=True)
            S.op("dve", lambda e: e.tensor_copy(out=identb[:], in_=ident), reads=["consts"], writes=["identb"])
            S.op("dve", lambda e: e.memset(onesb[:], 1.0), writes=["onesb"])
            for b in range(cfg.VB):
                S.op("pe", lambda e, b=b: e.transpose(out=ps[b % 4][:, 0:128], in_=tabc[:, b * 128:(b + 1) * 128], identity=ident),
                     reads=["tA", "tB", "consts"], writes=[("ps", b % 4)])
                S.op("act", lambda e, b=b: e.copy(out=vec[:, b * 128:(b + 1) * 128], in_=ps[b % 4][:, 0:128]),
                     reads=[("ps", b % 4)], writes=["vec"])

        def program():
            init()
            for p in range(NPASS):
                ns = NS if p == 0 else 0
                TTp = T + (NX if ns else 0)
                groups = colgroups(ns)
                S.barrier()
                S.op("dve", lambda e: e.memset(glrT[:], 1.0), writes=["glrT"])
                load_x(p, ns)
                for l in range(L):
                    S.barrier()
                    if l == 0:
                        S.op("dve", lambda e: e.memset(yT[:, :, T:TT], 0.0), writes=[("yT", k) for k in range(KC)])
                    rms_norm("nmg", l, groups, TTp)
                    S.barrier()
                    mixer_conv(p, l, ns, groups, TTp)
                    S.barrier()
                    mixer_gla(p, l, ns)
                    out_proj(l, groups)
                    xchg_x(2 * l)
                    S.barrier()
                    rms_norm("nfg", l, groups, TTp)
                    S.barrier()
                    ffn(p, l, ns, groups, TTp)
                    if l + 1 < L:
                        xchg_x(2 * l + 1)
                S.barrier()
                final_out(p, ns, groups, TTp)

        S.dry = True
        program()
        S.dry = False
        ws.i = 0
        ws.start()
        program()

        with nc.Block() as block:
            @block.tensor
            def _(e):
                S.replay("pe", e)

            @block.scalar
            def _(e):
                S.replay("act", e)

            @block.vector
            def _(e):
                S.replay("dve", e)

            @block.gpsimd
            def _(e):
                S.replay("pool", e)

            @block.sync
            def _(e):
                S.replay("sp", e)
                S.final_wait(e)
    return nc


def make_consts():
    c = np.zeros((128, 770), np.float32)
    j = np.arange(128)[:, None]
    i = np.arange(128)[None, :]
    c[:, 0:128] = np.eye(128, dtype=np.float32)
    c[:, 128:256] = (j <= i).astype(np.float32) - (j <= 63).astype(np.float32)
    c[:, 256:384] = (j > i).astype(np.float32)
    c[:, 384:512] = (j <= i).astype(np.float32)
    c[:, 512] = 1.0
    c[:, 513] = (np.arange(128) <= 63).astype(np.float32)
    c[:, 514:770] = np.eye(16, dtype=np.float32).reshape(1, 256)
    return c


def pack_vecs(cfg, inp):
    rows = []
    for l in range(L):
        rows.append(inp["norm_mix_g"][l].reshape(16, 128))
        rows.append(inp["conv_w"][l].reshape(24, 128))
        rows.append(inp["gla_norm_g"][l].reshape(8, 128))
        rows.append(inp["norm_ffn_g"][l].reshape(16, 128))
        rows.append(inp["ffn_conv_w"][l].reshape(3 * cfg.FC, 128))
        rows.append(inp["ffn_conv_b"][l].reshape(cfg.FC, 128))
    rows.append(inp["final_norm_g"].reshape(16, 128))
    v = np.concatenate(rows, axis=0).astype(np.float32)
    out = np.zeros((cfg.VB * 128, 128), np.float32)
    out[:v.shape[0]] = v
    return out


_NC_CACHE = {}


def run(cfg, inp, ncores):
    key = (cfg.T, cfg.NPASS, cfg.DFF, cfg.NG)
    if key not in _NC_CACHE:
        _NC_CACHE[key] = build_nc(cfg)
    nc = _NC_CACHE[key]
    T = cfg.T
    consts = make_consts()
    vecs = pack_vecs(cfg, inp)
    gwp = np.ascontiguousarray(np.concatenate([inp["gate_w2"], inp["gate_b"][:, None, :]], axis=1), dtype=np.float32)
    shared = {"w_in": inp["w_in"], "w_out": inp["w_out"], "w_up": inp["w_up"], "w_down": inp["w_down"],
              "gw": gwp, "vecs": vecs, "consts": consts}
    in_maps = []
    for c in range(ncores):
        m = dict(shared)
        sl = slice(c * NS, (c + 1) * NS)
        b, half = c // 2, c % 2
        m["x_p"] = np.ascontiguousarray(inp["x_prompt"][b, half * T:(half + 1) * T])
        m["x_h"] = (np.ascontiguousarray(inp["x_prompt"][b, T - NH:T]) if half else np.zeros((NH, D), np.float32))
        fl = np.zeros((128, 2), np.float32)
        fl[:, 0] = float(half)
        fl[:, 1] = 1.0 - float(half)
        m["flag"] = fl
        m["x_s"] = np.ascontiguousarray(inp["x_sample"][sl, 0, :])
        m["st_conv"] = np.ascontiguousarray(inp["state_conv"][:, sl])
        m["st_gla"] = np.ascontiguousarray(inp["state_gla"][:, sl])
        m["st_ffn"] = np.ascontiguousarray(inp["state_ffn_conv"][:, sl])
        in_maps.append(m)
    res = run_bass_kernel_spmd(nc, in_maps, core_ids=list(range(ncores)))
    return res.results


def assemble(cfg, R, ncores):
    nb = ncores // 2
    y_prompt = np.stack([np.concatenate([R[2 * b]["y_p"], R[2 * b + 1]["y_p"]], 0) for b in range(nb)], 0)
    y_sample = np.concatenate([R[c]["y_s"] for c in range(ncores)], 0)[:, None, :]
    conv_p = np.stack([R[2 * b + 1]["conv_p"][0] for b in range(nb)], 1)
    gla_p = np.stack([R[2 * b + 1]["gla_p"][0] for b in range(nb)], 1)
    ffn_p = np.stack([R[2 * b + 1]["ffn_p"][0] for b in range(nb)], 1)
    conv_s = np.concatenate([R[c]["conv_s"] for c in range(ncores)], 1)
    gla_s = np.concatenate([R[c]["gla_s"] for c in range(ncores)], 1)
    ffn_s = np.concatenate([R[c]["ffn_s"] for c in range(ncores)], 1)
    return tuple(np.ascontiguousarray(a, dtype=np.float32) for a in
                 (y_prompt, y_sample, conv_p, gla_p, ffn_p, conv_s, gla_s, ffn_s))


def kernel(**inputs):
    inp = {k: np.asarray(v) for k, v in inputs.items()}
    cfg = Cfg()
    R = run(cfg, inp, 8)
    return assemble(cfg, R, 8)
```
